# Optimizing a Trainium2 kernel written in Bass

```python
import math
import jax, jax.numpy as jnp
from jax import lax
import numpy as np

D_MODEL = 1024
BATCH = 8
SEQ = 4096
DEPTH = 4

FOX_HEADS = 8
FOX_HEAD_DIM = 64
FOX_WIDTH = FOX_HEADS * FOX_HEAD_DIM
Q_BLOCK = 128
RET_HEADS = 8
RET_QK_DIM = 64
RET_V_DIM = 128
RET_QK_WIDTH = RET_HEADS * RET_QK_DIM
RET_V_WIDTH = RET_HEADS * RET_V_DIM
RET_CHUNK = 128
ROPE_BASE = 10000.0
N_BRANCH = 2
IN_COLS = 3 * FOX_WIDTH + FOX_HEADS + 2 * RET_QK_WIDTH + 2 * RET_V_WIDTH + N_BRANCH * D_MODEL
D_FF = 2816
N_EXPERTS = 8
TOP_K = 2
D_FF_EXPERT = 3584
N_DENSE = (DEPTH + 1) // 2
N_MOE = DEPTH // 2
DN_ALPHA = (2 * DEPTH) ** 0.25
DN_BETA = (8 * DEPTH) ** -0.25
LN_EPS = 1e-5
RMS_EPS = 1e-6

kernel_name = "fox_retnet_gated_hybrid_deepnorm_moe"

f32 = jnp.float32


def layer_norm(x, g, b):
    xf = x.astype(f32)
    mu = jnp.mean(xf, axis=-1, keepdims=True)
    var = jnp.mean(jnp.square(xf - mu), axis=-1, keepdims=True)
    y = (xf - mu) * lax.rsqrt(var + LN_EPS) * g.astype(f32) + b.astype(f32)
    return y.astype(x.dtype)


def rotary(x, positions):
    half = x.shape[-1] // 2
    inv_freq = ROPE_BASE ** (-jnp.arange(half, dtype=f32) / half)
    ang = positions.astype(f32)[..., None] * inv_freq
    cos = jnp.cos(ang)[:, :, None, :]
    sin = jnp.sin(ang)[:, :, None, :]
    x1 = x[..., :half].astype(f32)
    x2 = x[..., half:].astype(f32)
    return jnp.concatenate([x1 * cos - x2 * sin, x1 * sin + x2 * cos], axis=-1).astype(x.dtype)


def forgetting_attention(q, k, v, log_f):
    B, S, H, dh = q.shape
    n_blk = S // Q_BLOCK
    c = jnp.cumsum(log_f, axis=1).transpose(0, 2, 1)
    kh = k.transpose(0, 2, 1, 3)
    vh = v.transpose(0, 2, 1, 3)
    qb = q.reshape(B, n_blk, Q_BLOCK, H, dh).transpose(1, 0, 3, 2, 4)
    cb = c.reshape(B, H, n_blk, Q_BLOCK).transpose(2, 0, 1, 3)
    key_pos = jnp.arange(S)
    scale = dh ** -0.5

    def one_block(args):
        i, q_i, c_i = args
        s = jnp.einsum('bhqd,bhkd->bhqk', q_i, kh, preferred_element_type=f32) * scale
        s = s + c_i[..., :, None] - c[..., None, :]
        q_pos = i * Q_BLOCK + jnp.arange(Q_BLOCK)
        causal = key_pos[None, :] <= q_pos[:, None]
        s = jnp.where(causal, s, -jnp.inf)
        p = jax.nn.softmax(s, axis=-1)
        return jnp.einsum('bhqk,bhkd->bhqd', p.astype(vh.dtype), vh)

    out = lax.map(one_block, (jnp.arange(n_blk), qb, cb))
    return out.transpose(1, 0, 3, 2, 4).reshape(B, S, H * dh)


def retention_decays():
    return jnp.log(1.0 - 2.0 ** (-5.0 - jnp.arange(RET_HEADS, dtype=f32)))


def chunkwise_retention(q, k, v, log_gamma):
    B, S, H, dk = q.shape
    dv = v.shape[-1]
    C = RET_CHUNK
    N = S // C
    qc = q.astype(f32).reshape(B, N, C, H, dk)
    kc = k.astype(f32).reshape(B, N, C, H, dk)
    vc = v.astype(f32).reshape(B, N, C, H, dv)
    idx = jnp.arange(C, dtype=f32)
    diff = idx[:, None] - idx[None, :]
    d_intra = jnp.where(diff >= 0,
                        jnp.exp(jnp.maximum(diff, 0.0)[None] * log_gamma[:, None, None]),
                        0.0)
    scores = jnp.einsum('bnihd,bnjhd->bnhij', qc, kc) * d_intra
    intra = jnp.einsum('bnhij,bnjhe->bnihe', scores, vc)
    k_dec = jnp.exp((C - 1 - idx)[:, None] * log_gamma[None, :])
    kv = jnp.einsum('bnjhd,bnjhe->bnhde', kc * k_dec[:, :, None], vc)
    chunk_decay = jnp.exp(C * log_gamma)[:, None, None]

    def step(state, kv_n):
        return chunk_decay * state + kv_n, state

    _, prev = lax.scan(step, jnp.zeros((B, H, dk, dv), f32), jnp.moveaxis(kv, 1, 0))
    prev = jnp.moveaxis(prev, 0, 1)
    q_dec = jnp.exp((idx + 1.0)[:, None] * log_gamma[None, :])
    cross = jnp.einsum('bnihd,bnhde->bnihe', qc * q_dec[:, :, None], prev)
    return (intra + cross).reshape(B, S, H, dv)


def head_rms_norm(y, g, out_dtype):
    B, S, H, dv = y.shape
    yf = y.astype(f32)
    yf = yf * lax.rsqrt(jnp.mean(yf * yf, axis=-1, keepdims=True) + RMS_EPS)
    return (yf.reshape(B, S, H * dv) * g.astype(f32)).astype(out_dtype)


def hybrid_mixer(x, positions, w_in, b_forget, ret_norm_g, w_branch_fox, w_branch_ret, w_out):
    B, S, _ = x.shape
    proj = x @ w_in
    o = 0
    q_a = proj[..., o:o + FOX_WIDTH]; o += FOX_WIDTH
    k_a = proj[..., o:o + FOX_WIDTH]; o += FOX_WIDTH
    v_a = proj[..., o:o + FOX_WIDTH]; o += FOX_WIDTH
    f_logit = proj[..., o:o + FOX_HEADS]; o += FOX_HEADS
    q_b = proj[..., o:o + RET_QK_WIDTH]; o += RET_QK_WIDTH
    k_b = proj[..., o:o + RET_QK_WIDTH]; o += RET_QK_WIDTH
    v_b = proj[..., o:o + RET_V_WIDTH]; o += RET_V_WIDTH
    g_b = proj[..., o:o + RET_V_WIDTH]; o += RET_V_WIDTH
    gate_logits = proj[..., o:o + N_BRANCH * D_MODEL]

    log_f = jax.nn.log_sigmoid(f_logit.astype(f32) + b_forget.astype(f32))
    o_a = forgetting_attention(q_a.reshape(B, S, FOX_HEADS, FOX_HEAD_DIM),
                               k_a.reshape(B, S, FOX_HEADS, FOX_HEAD_DIM),
                               v_a.reshape(B, S, FOX_HEADS, FOX_HEAD_DIM), log_f)

    qr = rotary(q_b.reshape(B, S, RET_HEADS, RET_QK_DIM), positions)
    kr = rotary(k_b.reshape(B, S, RET_HEADS, RET_QK_DIM), positions) * (RET_QK_DIM ** -0.5)
    y = chunkwise_retention(qr, kr, v_b.reshape(B, S, RET_HEADS, RET_V_DIM), retention_decays())
    o_b = jax.nn.silu(g_b) * head_rms_norm(y, ret_norm_g, x.dtype)

    gates = jax.nn.sigmoid(gate_logits).reshape(B, S, N_BRANCH, D_MODEL)
    merged = gates[:, :, 0] * (o_a @ w_branch_fox) + gates[:, :, 1] * (o_b @ w_branch_ret)
    return merged @ w_out


def swiglu(x, w_gate, w_up, w_down):
    return (jax.nn.silu(x @ w_gate) * (x @ w_up)) @ w_down


def moe_swiglu(x, w_router, e_gate, e_up, e_down):
    logits = (x @ w_router).astype(f32)
    top_val, top_idx = lax.top_k(logits, TOP_K)
    top_w = jax.nn.softmax(top_val, axis=-1)
    gate = jnp.sum(jax.nn.one_hot(top_idx, N_EXPERTS, dtype=f32) * top_w[..., None], axis=-2)
    gate = gate.astype(x.dtype)
    out = jnp.zeros_like(x)
    for e in range(N_EXPERTS):
        out = out + gate[..., e:e + 1] * swiglu(x, e_gate[e], e_up[e], e_down[e])
    return out


def setup_inputs(seed: int = 0) -> dict:
    key = jax.random.key(seed)
    ks = jax.random.split(key, 24)
    D = D_MODEL

    def nrm(k, shape, fan_in, gain=1.0):
        return jax.random.normal(k, shape, f32) * (fan_in ** -0.5) * gain

    x = jax.random.normal(ks[0], (BATCH, SEQ, D), f32)
    offsets = jax.random.randint(ks[1], (BATCH, 1), 0, 1024, dtype=jnp.int32)
    positions = (offsets + jnp.arange(SEQ, dtype=jnp.int32)[None, :]).astype(jnp.int32)

    col_scale = jnp.concatenate([
        jnp.ones((2 * FOX_WIDTH,), f32), jnp.full((FOX_WIDTH,), DN_BETA, f32),
        jnp.ones((FOX_HEADS,), f32), jnp.ones((2 * RET_QK_WIDTH,), f32),
        jnp.full((RET_V_WIDTH,), DN_BETA, f32), jnp.ones((RET_V_WIDTH + N_BRANCH * D,), f32)])
    w_in = nrm(ks[2], (DEPTH, D, IN_COLS), D) * col_scale
    b_forget = 1.0 + 3.0 * jax.random.uniform(ks[3], (DEPTH, FOX_HEADS), f32)
    ret_norm_g = 1.0 + 0.02 * jax.random.normal(ks[4], (DEPTH, RET_V_WIDTH), f32)
    w_branch_fox = nrm(ks[5], (DEPTH, FOX_WIDTH, D), FOX_WIDTH, DN_BETA)
    w_branch_ret = nrm(ks[6], (DEPTH, RET_V_WIDTH, D), RET_V_WIDTH, DN_BETA)
    w_out = nrm(ks[7], (DEPTH, D, D), D, DN_BETA)
    ln_mix_g = 1.0 + 0.02 * jax.random.normal(ks[8], (DEPTH, D), f32)
    ln_mix_b = 0.02 * jax.random.normal(ks[9], (DEPTH, D), f32)

    ffn_w_gate = nrm(ks[10], (N_DENSE, D, D_FF), D, DN_BETA)
    ffn_w_up = nrm(ks[11], (N_DENSE, D, D_FF), D, DN_BETA)
    ffn_w_down = nrm(ks[12], (N_DENSE, D_FF, D), D_FF, DN_BETA)

    moe_router = nrm(ks[13], (N_MOE, D, N_EXPERTS), D)
    moe_w_gate = nrm(ks[14], (N_MOE, N_EXPERTS, D, D_FF_EXPERT), D, DN_BETA)
    moe_w_up = nrm(ks[15], (N_MOE, N_EXPERTS, D, D_FF_EXPERT), D, DN_BETA)
    moe_w_down = nrm(ks[16], (N_MOE, N_EXPERTS, D_FF_EXPERT, D), D_FF_EXPERT, DN_BETA)

    ln_ffn_g = 1.0 + 0.02 * jax.random.normal(ks[17], (DEPTH, D), f32)
    ln_ffn_b = 0.02 * jax.random.normal(ks[18], (DEPTH, D), f32)

    return {"x": x, "positions": positions, "w_in": w_in, "b_forget": b_forget,
            "ret_norm_g": ret_norm_g, "w_branch_fox": w_branch_fox, "w_branch_ret": w_branch_ret,
            "w_out": w_out, "ln_mix_g": ln_mix_g, "ln_mix_b": ln_mix_b,
            "ffn_w_gate": ffn_w_gate, "ffn_w_up": ffn_w_up, "ffn_w_down": ffn_w_down,
            "moe_router": moe_router, "moe_w_gate": moe_w_gate, "moe_w_up": moe_w_up,
            "moe_w_down": moe_w_down, "ln_ffn_g": ln_ffn_g, "ln_ffn_b": ln_ffn_b}


def reference(x, positions, w_in, b_forget, ret_norm_g, w_branch_fox, w_branch_ret, w_out,
              ln_mix_g, ln_mix_b, ffn_w_gate, ffn_w_up, ffn_w_down, moe_router, moe_w_gate,
              moe_w_up, moe_w_down, ln_ffn_g, ln_ffn_b):
    for layer in range(DEPTH):
        h = hybrid_mixer(x, positions, w_in[layer], b_forget[layer], ret_norm_g[layer],
                         w_branch_fox[layer], w_branch_ret[layer], w_out[layer])
        x = layer_norm(DN_ALPHA * x + h, ln_mix_g[layer], ln_mix_b[layer])
        i = layer // 2
        if layer % 2 == 0:
            h = swiglu(x, ffn_w_gate[i], ffn_w_up[i], ffn_w_down[i])
        else:
            h = moe_swiglu(x, moe_router[i], moe_w_gate[i], moe_w_up[i], moe_w_down[i])
        x = layer_norm(DN_ALPHA * x + h, ln_ffn_g[layer], ln_ffn_b[layer])
    return x
```

```python
import math
from contextlib import ExitStack

import numpy as np
import concourse.bass as bass
import concourse.mybir as mybir
from concourse.bass_utils import run_bass_kernel_spmd

F32 = mybir.dt.float32
BF16 = mybir.dt.bfloat16
I32 = mybir.dt.int32
AF = mybir.ActivationFunctionType
ALU = mybir.AluOpType
AX = mybir.AxisListType

S_LEN = 4096
D = 1024
NB = 32
NT = 8
DEPTH = 4
IN_COLS = 6664
D_FF = 2816
N_EXP = 8
D_FFE = 3584
DN_ALPHA = (2 * DEPTH) ** 0.25
LN_EPS = 1e-5
RMS_EPS = 1e-6
C_QA, C_KA, C_VA, C_F, C_QB, C_KB, C_VB, C_GB, C_GT = 0, 512, 1024, 1536, 1544, 2056, 2568, 3592, 4616
NEG_BIG = -30000.0
VAW = 8 * 65
NTILE_H = 24
SNG_H = 7
SPARSE = True
C_LEVEL = 4
C_NCH = NB


class Res:
    __slots__ = ("last_w", "readers")

    def __init__(self):
        self.last_w = None
        self.readers = {}


def RL(n):
    return [Res() for _ in range(n)]


class Op:
    __slots__ = ("eng", "fn", "deps", "signal", "count", "is_dma", "key")

    def __init__(self, eng, fn, is_dma=False):
        self.eng = eng
        self.fn = fn
        self.deps = []
        self.signal = False
        self.count = 0
        self.is_dma = is_dma
        self.key = None


class Sched:
    ENGS = ("pe", "act", "dve", "pool", "sp")
    ROLL = 30000

    def __init__(self, nc, n_chan=10):
        self.nc = nc
        self.q = {e: [] for e in self.ENGS}
        self.n_chan = n_chan
        self.chan_rr = {"sp": 0, "act": 0, "pool": 0}
        self.chan_last = {}

    def _track(self, op, reads, writes):
        deps = []
        for r in reads:
            if r.last_w is not None:
                deps.append(r.last_w)
        for w in writes:
            if w.last_w is not None:
                deps.append(w.last_w)
            deps.extend(w.readers.values())
        rk = op.key if op.is_dma else op.eng
        for r in reads:
            r.readers[rk] = op
        for w in writes:
            w.last_w = op
            w.readers = {}
        seen = set()
        for d in deps:
            if d is op or id(d) in seen:
                continue
            seen.add(id(d))
            if (not d.is_dma) and (not op.is_dma) and d.eng == "pe" and op.eng == "pe":
                continue
            op.deps.append(d)
            d.signal = True

    def op(self, eng, fn, reads=(), writes=()):
        o = Op(eng, fn)
        self._track(o, reads, writes)
        self.q[eng].append(o)
        return o

    def dma(self, queue, fn, reads=(), writes=()):
        o = Op(queue, fn, is_dma=True)
        ci = self.chan_rr[queue]
        self.chan_rr[queue] = (ci + 1) % self.n_chan
        o.key = ("c", queue, ci)
        prev = self.chan_last.get(o.key)
        self._track(o, reads, writes)
        if prev is not None and all(d is not prev for d in o.deps):
            o.deps.append(prev)
        self.chan_last[o.key] = o
        o.signal = True
        self.q[queue].append(o)
        return o

    def emit(self, stack, tag):
        nc = self.nc
        ccount = {}
        keys = []
        for e in self.ENGS:
            cnt = 0
            si = 0
            for o in self.q[e]:
                if o.is_dma:
                    c = ccount.get(o.key, 0) + 16
                    ccount[o.key] = c
                    o.count = c
                elif o.signal:
                    if cnt >= self.ROLL:
                        si += 1
                        cnt = 0
                    cnt += 1
                    o.count = cnt
                    o.key = ("e", e, si)
            for i in range(si + 1):
                keys.append(("e", e, i))
        keys.extend(ccount.keys())
        sems = {k: nc.alloc_semaphore(name="%s_%s_%s%d" % (tag, k[0], k[1], k[2])) for k in keys}
        bstack = ExitStack()
        block = bstack.enter_context(nc.Block())

        def run(ename, eng):
            waited = {}
            for o in self.q[ename]:
                need = {}
                for d in o.deps:
                    if d.count > need.get(d.key, 0):
                        need[d.key] = d.count
                for key, val in need.items():
                    if waited.get(key, 0) >= val:
                        continue
                    eng.wait_ge(sems[key], val)
                    waited[key] = val
                ins = o.fn(eng)
                if o.is_dma:
                    ins.then_inc(sems[o.key], 16)
                elif o.signal:
                    ins.then_inc(sems[o.key], 1)

        @block.tensor
        def _(e):
            run("pe", e)

        @block.scalar
        def _(e):
            run("act", e)

        @block.vector
        def _(e):
            run("dve", e)

        @block.gpsimd
        def _(e):
            run("pool", e)

        @block.sync
        def _(e):
            run("sp", e)
            for key, c in ccount.items():
                e.wait_ge(sems[key], c)

        bstack.close()
        nc.clear_and_free_semaphores(list(sems.values()))
        nc.all_engine_barrier()


class Stage:
    _uid = 0

    def __init__(self, nc, name):
        self.nc = nc
        self.name = name
        self.st = ExitStack()
        self.S = Sched(nc)
        Stage._uid += 1
        self.uid = Stage._uid
        self.n = 0

    def sb(self, shape, dt, nm="t"):
        self.n += 1
        return self.st.enter_context(self.nc.sbuf_tensor("%s%d_%s%d" % (self.name, self.uid, nm, self.n), list(shape), dt))

    def ps(self, shape, dt, nm="p"):
        self.n += 1
        return self.st.enter_context(self.nc.psum_tensor("%s%d_%s%d" % (self.name, self.uid, nm, self.n), list(shape), dt))

    def finish(self):
        self.S.emit(self.st, "%s%d" % (self.name, self.uid))
        self.st.close()


class Rot:
    def __init__(self, tiles):
        self.tiles = tiles
        self.res = RL(len(tiles))
        self.i = -1

    def next(self):
        self.i = (self.i + 1) % len(self.tiles)
        return self.tiles[self.i], self.res[self.i]


def host_consts():
    c = {}
    c["c_ident"] = np.eye(128, dtype=np.float32)
    r = np.arange(128)
    c["c_mq"] = np.where(r[:, None] > r[None, :], NEG_BIG, 0.0).astype(np.float32)
    half = 32
    inv_freq = (np.float32(10000.0) ** (-np.arange(half, dtype=np.float32) / np.float32(half))).astype(np.float32)
    c["c_invf"] = np.concatenate([inv_freq, inv_freq]).reshape(64, 1).astype(np.float32)
    c["c_sgn"] = np.concatenate([-np.ones(32), np.ones(32)]).reshape(64, 1).astype(np.float32)
    lg = np.log(1.0 - 2.0 ** (-5.0 - np.arange(8, dtype=np.float64)))
    j = np.arange(128, dtype=np.float64)
    dtp = np.exp(-(j[:, None, None] + 1.0) * lg[None, :, None]) * (j[:, None, None] <= j[None, None, :])
    perm = [2 * (hh % 4) + (hh // 4) for hh in range(8)]
    c["c_dtp"] = np.ascontiguousarray(dtp[:, perm, :]).astype(np.float32)
    c["c_qdec"] = np.ascontiguousarray(np.exp((j[:, None] + 1.0) * lg[None, :])[:, perm]).astype(np.float32)
    c["c_kdec"] = np.exp((127.0 - j[:, None]) * lg[None, :]).astype(np.float32)
    cd = np.zeros((128, 4, 128), dtype=np.float64)
    for two in range(2):
        for hp in range(4):
            cd[two * 64:(two + 1) * 64, hp, :] = math.exp(128.0 * lg[2 * hp + two])
    c["c_cd"] = cd.astype(np.float32)
    c["c_tri"] = (r[:, None] < r[None, :]).astype(np.float32)
    c["c_tlim"] = np.tile((np.arange(NTILE_H, dtype=np.float32) * 512.0)[None, :], (128, 1)).astype(np.float32)
    wb = np.arange(128, dtype=np.float32)[:, None, None] + (np.arange(SNG_H, dtype=np.float32) * 128.0)[None, None, :] + np.zeros((1, NTILE_H, 1), np.float32)
    c["c_wbase"] = np.ascontiguousarray(wb).astype(np.float32)
    return c


CONST_SHAPES = {"c_ident": [128, 128], "c_mq": [128, 128], "c_invf": [64, 1], "c_sgn": [64, 1],
                "c_dtp": [128, 8, 128], "c_qdec": [128, 8], "c_kdec": [128, 8], "c_cd": [128, 4, 128],
                "c_tri": [128, 128], "c_tlim": [128, NTILE_H], "c_wbase": [128, NTILE_H, SNG_H]}

INPUT_SHAPES = {
    "x": ([S_LEN, D], F32), "positions": ([1, S_LEN], I32),
    "w_in": ([DEPTH, D, IN_COLS], F32), "b_forget": ([DEPTH, 8], F32), "ret_norm_g": ([DEPTH, D], F32),
    "w_branch_fox": ([DEPTH, 512, D], F32), "w_branch_ret": ([DEPTH, D, D], F32), "w_out": ([DEPTH, D, D], F32),
    "ln_mix_g": ([DEPTH, D], F32), "ln_mix_b": ([DEPTH, D], F32),
    "ffn_w_gate": ([2, D, D_FF], F32), "ffn_w_up": ([2, D, D_FF], F32), "ffn_w_down": ([2, D_FF, D], F32),
    "moe_router": ([2, D, 8], F32), "moe_w_gate": ([2, 8, D, D_FFE], F32), "moe_w_up": ([2, 8, D, D_FFE], F32),
    "moe_w_down": ([2, 8, D_FFE, D], F32), "ln_ffn_g": ([DEPTH, D], F32), "ln_ffn_b": ([DEPTH, D], F32),
}


class Prog:
    def __init__(self, debug_out=()):
        self.nc = bass.Bass("TRN2", target_bir_lowering=False)
        self.t = {}
        self.used_inputs = []
        self.debug_out = set(debug_out)
        self.outputs = []

    def inp(self, name):
        if name not in self.t:
            if name in INPUT_SHAPES:
                shp, dt = INPUT_SHAPES[name]
            else:
                shp, dt = CONST_SHAPES[name], F32
            self.t[name] = self.nc.dram_tensor(name, list(shp), dt, kind="ExternalInput").ap()
            self.used_inputs.append(name)
        return self.t[name]

    def scratch(self, name, shape, dt):
        if name not in self.t:
            if name in self.debug_out:
                self.t[name] = self.nc.dram_tensor(name, list(shape), dt, kind="ExternalOutput").ap()
                self.outputs.append(name)
            else:
                self.t[name] = self.nc.dram_tensor(name, list(shape), dt).ap()
        return self.t[name]

    def out(self, name, shape, dt):
        if name not in self.t:
            self.t[name] = self.nc.dram_tensor(name, list(shape), dt, kind="ExternalOutput").ap()
            self.outputs.append(name)
        return self.t[name]

    def qaT(self): return self.scratch("qaT", [8, 70, S_LEN], BF16)
    def kaT(self): return self.scratch("kaT", [8, 70, S_LEN], BF16)
    def vA(self): return self.scratch("vA", [S_LEN, VAW], BF16)
    def qbT(self): return self.scratch("qbT", [512, S_LEN], BF16)
    def kbT(self): return self.scratch("kbT", [512, S_LEN], BF16)
    def vB(self): return self.scratch("vB", [S_LEN, D], BF16)
    def sgB(self): return self.scratch("sgB", [S_LEN, D], BF16)
    def gT(self): return self.scratch("gT", [2 * D, S_LEN], BF16)
    def oaT(self): return self.scratch("oaT", [512, S_LEN], BF16)
    def obT(self): return self.scratch("obT", [D, S_LEN], BF16)
    def cosT(self): return self.scratch("cosT", [64, S_LEN], F32)
    def sinT(self): return self.scratch("sinT", [64, S_LEN], F32)
    def xs(self, i): return self.scratch("xs%d" % i, [S_LEN, D], F32)


def load_ident(T, P, S):
    idt = T.sb([128, 128], BF16, "ident")
    r = Res()
    src = P.inp("c_ident")
    S.dma("pool", lambda e: e.dma_start(out=idt[:], in_=src[:, :]), writes=[r])
    return idt, r


def stage0(P):
    nc = P.nc
    T = Stage(nc, "s0")
    S = T.S
    pos = P.inp("positions")
    posi = T.sb([64, S_LEN], I32)
    ang = T.sb([64, S_LEN], F32)
    kk = T.sb([64, S_LEN], F32)
    r1 = T.sb([64, S_LEN], F32)
    r2 = T.sb([64, S_LEN], F32)
    mm = T.sb([64, S_LEN], F32)
    tb = T.sb([64, S_LEN], F32)
    invf = T.sb([64, 1], F32)
    sgn = T.sb([64, 1], F32)
    cst = T.sb([8, 3, 512], BF16)
    cstn = T.sb([8, 3, 512], BF16)
    R_posi, R_ang, R_kk, R_r1, R_r2, R_mm, R_tb, R_invf, R_sgn, R_cst = RL(10)
    c_invf, c_sgn = P.inp("c_invf"), P.inp("c_sgn")
    cosT, sinT = P.cosT(), P.sinT()
    S.dma("sp", lambda e: e.dma_start(out=posi[:], in_=pos[0:1, :].partition_broadcast(64)), writes=[R_posi])
    S.dma("sp", lambda e: e.dma_start(out=invf[:], in_=c_invf[:, :]), writes=[R_invf])
    S.dma("sp", lambda e: e.dma_start(out=sgn[:], in_=c_sgn[:, :]), writes=[R_sgn])
    TWO_PI = 2.0 * math.pi
    C1 = 6.28125
    C2 = float(np.float32(np.round((TWO_PI - C1) * 2.0 ** 20) / 2.0 ** 20))
    C3 = float(np.float32(TWO_PI - C1 - C2))
    MAGIC = 12582912.0
    PI_LO = float(np.nextafter(np.float32(math.pi), np.float32(0.0)))
    S.op("dve", lambda e: e.tensor_copy(out=ang[:], in_=posi[:]), reads=[R_posi], writes=[R_ang])
    S.op("dve", lambda e: e.tensor_scalar(out=ang[:], in0=ang[:], scalar1=invf[:, 0:1], scalar2=None, op0=ALU.mult),
         reads=[R_ang, R_invf], writes=[R_ang])
    S.op("dve", lambda e: e.tensor_scalar(out=kk[:], in0=ang[:], scalar1=1.0 / TWO_PI, scalar2=MAGIC, op0=ALU.mult, op1=ALU.add),
         reads=[R_ang], writes=[R_kk])
    S.op("dve", lambda e: e.tensor_scalar(out=kk[:], in0=kk[:], scalar1=-MAGIC, scalar2=None, op0=ALU.add),
         reads=[R_kk], writes=[R_kk])
    S.op("dve", lambda e: e.scalar_tensor_tensor(out=r1[:], in0=kk[:], scalar=-C1, in1=ang[:], op0=ALU.mult, op1=ALU.add),
         reads=[R_kk, R_ang], writes=[R_r1])
    S.op("dve", lambda e: e.scalar_tensor_tensor(out=r1[:], in0=kk[:], scalar=-C2, in1=r1[:], op0=ALU.mult, op1=ALU.add),
         reads=[R_kk, R_r1], writes=[R_r1])
    S.op("dve", lambda e: e.scalar_tensor_tensor(out=r1[:], in0=kk[:], scalar=-C3, in1=r1[:], op0=ALU.mult, op1=ALU.add),
         reads=[R_kk, R_r1], writes=[R_r1])
    S.op("dve", lambda e: e.tensor_scalar(out=r2[:], in0=r1[:], scalar1=0.5 * math.pi, scalar2=None, op0=ALU.add),
         reads=[R_r1], writes=[R_r2])
    S.op("dve", lambda e: e.tensor_scalar(out=mm[:], in0=r2[:], scalar1=math.pi, scalar2=None, op0=ALU.is_gt),
         reads=[R_r2], writes=[R_mm])
    S.op("dve", lambda e: e.scalar_tensor_tensor(out=r2[:], in0=mm[:], scalar=-TWO_PI, in1=r2[:], op0=ALU.mult, op1=ALU.add),
         reads=[R_mm, R_r2], writes=[R_r2])
    S.op("dve", lambda e: e.tensor_scalar(out=r1[:], in0=r1[:], scalar1=PI_LO, scalar2=-PI_LO, op0=ALU.min, op1=ALU.max),
         reads=[R_r1], writes=[R_r1])
    S.op("dve", lambda e: e.tensor_scalar(out=r2[:], in0=r2[:], scalar1=PI_LO, scalar2=-PI_LO, op0=ALU.min, op1=ALU.max),
         reads=[R_r2], writes=[R_r2])
    S.op("act", lambda e: e.activation(out=tb[:], in_=r1[:], func=AF.Sin, scale=sgn[:, 0:1]), reads=[R_r1, R_sgn], writes=[R_tb])
    S.dma("sp", lambda e: e.dma_start(out=sinT[:, :], in_=tb[:]), reads=[R_tb])
    S.op("act", lambda e: e.activation(out=mm[:], in_=r2[:], func=AF.Sin), reads=[R_r2], writes=[R_mm])
    S.dma("sp", lambda e: e.dma_start(out=cosT[:, :], in_=mm[:]), reads=[R_mm])
    S.op("pool", lambda e: e.memset(cst[:], 1.0), writes=[R_cst])
    S.op("pool", lambda e: e.memset(cstn[:], -1.0), writes=[R_cst])
    qaT, kaT = P.qaT(), P.kaT()
    for t in range(NT):
        S.dma("sp", lambda e, t=t: e.dma_start(out=qaT[:, 67:70, t * 512:(t + 1) * 512], in_=cst[:]), reads=[R_cst])
        S.dma("sp", lambda e, t=t: e.dma_start(out=kaT[:, 64:67, t * 512:(t + 1) * 512], in_=cstn[:]), reads=[R_cst])
    T.finish()


def emit_build_xT(T, S, src, tok0, nblk, xT, R_xT, ident, R_id, xTf=None, R_xTf=None, identf=None):
    xf = Rot([T.sb([128, D], F32, "xf") for _ in range(2)])
    xb = Rot([T.sb([128, D], BF16, "xb") for _ in range(2)])
    pT = Rot([T.ps([128, 8, 128], BF16, "pT") for _ in range(2)])
    for blk in range(nblk):
        f, rf = xf.next()
        b, rb = xb.next()
        p, rp = pT.next()
        r0 = tok0 + blk * 128
        S.dma("sp", lambda e, f=f, r0=r0: e.dma_start(out=f[:], in_=src[r0:r0 + 128, :]), writes=[rf])
        S.op("act", lambda e, f=f, b=b: e.copy(out=b[:], in_=f[:]), reads=[rf], writes=[rb])
        for k in range(8):
            S.op("pe", lambda e, p=p, b=b, k=k: e.transpose(out=p[:, k, :], in_=b[:, k * 128:(k + 1) * 128], identity=ident[:]),
                 reads=[rb, R_id], writes=[rp])
        S.op("dve", lambda e, p=p, blk=blk: e.tensor_copy(out=xT[:, :, blk * 128:(blk + 1) * 128], in_=p[:]),
             reads=[rp], writes=[R_xT[blk]])


def stageA(P, layer, x_src):
    nc = P.nc
    w_in = P.inp("w_in")
    OUT = ExitStack()
    T = Stage(nc, "a1")
    S = T.S
    xT = OUT.enter_context(nc.sbuf_tensor("xT_l%d" % layer, [128, 8, S_LEN], BF16))
    R_xT = RL(NB)
    ident, R_id = load_ident(T, P, S)
    emit_build_xT(T, S, x_src, 0, NB, xT, R_xT, ident, R_id)
    wsl = Rot([T.sb([128, 8, 512], BF16, "w") for _ in range(3)])
    stg = Rot([T.sb([128, S_LEN], BF16, "stg") for _ in range(2)])
    pp = Rot([T.ps([128, 512], F32, "pp") for _ in range(4)])

    def load_w(c0, width=512):
        w, rw = wsl.next()
        S.dma("pool", lambda e: e.dma_start(out=w[:, :, 0:width], in_=w_in[layer, :, c0:c0 + width].rearrange("(k p) n -> p k n", p=128)),
              writes=[rw])
        return w, rw

    def fm_project(w, rw, col, M, evac):
        for t in range(NT):
            p, rp = pp.next()
            for k in range(8):
                S.op("pe", lambda e, p=p, k=k, t=t: e.matmul(p[0:M, :], lhsT=w[:, k, col:col + M], rhs=xT[:, k, t * 512:(t + 1) * 512],
                                                            start=(k == 0), stop=(k == 7)),
                     reads=[rw] + R_xT[4 * t:4 * t + 4], writes=[rp])
            evac(p, rp, t)

    qaT, kaT, gT = P.qaT(), P.kaT(), P.gT()
    for which, c0, dst in (("q", C_QA, qaT), ("k", C_KA, kaT)):
        w, rw = load_w(c0)
        for hp in range(4):
            sg, rsg = stg.next()

            def evac(p, rp, t, sg=sg, rsg=rsg, which=which):
                if which == "q":
                    S.op("act", lambda e: e.mul(out=sg[:, t * 512:(t + 1) * 512], in_=p[:, :], mul=0.125), reads=[rp], writes=[rsg])
                else:
                    S.op("dve", lambda e: e.tensor_copy(out=sg[:, t * 512:(t + 1) * 512], in_=p[:, :]), reads=[rp], writes=[rsg])
            fm_project(w, rw, hp * 128, 128, evac)
            S.dma("sp", lambda e, sg=sg, hp=hp, dst=dst: e.dma_start(out=dst[2 * hp, 0:64, :], in_=sg[0:64, :]), reads=[rsg])
            S.dma("sp", lambda e, sg=sg, hp=hp, dst=dst: e.dma_start(out=dst[2 * hp + 1, 0:64, :], in_=sg[64:128, :]), reads=[rsg])
    for gq in range(4):
        w, rw = load_w(C_GT + gq * 512)
        for mc in range(4):
            sg, rsg = stg.next()

            def evac(p, rp, t, sg=sg, rsg=rsg):
                S.op("act", lambda e: e.activation(out=sg[:, t * 512:(t + 1) * 512], in_=p[:, :], func=AF.Sigmoid), reads=[rp], writes=[rsg])
            fm_project(w, rw, mc * 128, 128, evac)
            row0 = gq * 512 + mc * 128
            S.dma("sp", lambda e, sg=sg, row0=row0: e.dma_start(out=gT[row0:row0 + 128, :], in_=sg[:, :]), reads=[rsg])
    Fb = T.sb([8, S_LEN], F32, "Fb")
    Cb = T.sb([8, S_LEN], F32, "Cb")
    CS = T.sb([8, 3, S_LEN], BF16, "CS")
    nb = T.sb([8, 1], F32, "nb")
    one8 = T.sb([8, 1], F32, "one8")
    R_Fb, R_Cb, R_CS, R_nb, R_one = RL(5)
    b_forget = P.inp("b_forget")
    S.dma("sp", lambda e: e.dma_start(out=nb[:], in_=b_forget[layer:layer + 1, :].rearrange("o h -> h o")), writes=[R_nb])
    S.op("dve", lambda e: e.tensor_scalar(out=nb[:], in0=nb[:], scalar1=-1.0, scalar2=None, op0=ALU.mult), reads=[R_nb], writes=[R_nb])
    S.op("dve", lambda e: e.memset(one8[:], 1.0), writes=[R_one])
    w, rw = load_w(C_F, 8)

    def evac_f(p, rp, t):
        S.op("act", lambda e: e.activation(out=Fb[:, t * 512:(t + 1) * 512], in_=p[0:8, :], func=AF.Exp, bias=nb[:, 0:1], scale=-1.0),
             reads=[rp, R_nb], writes=[R_Fb])
    fm_project(w, rw, 0, 8, evac_f)
    S.op("act", lambda e: e.activation(out=Fb[:], in_=Fb[:], func=AF.Ln, bias=1.0), reads=[R_Fb], writes=[R_Fb])
    S.op("dve", lambda e: e.tensor_tensor_scan(out=Cb[:], data0=one8[:, 0:1].to_broadcast([8, S_LEN]), data1=Fb[:], initial=0.0,
                                               op0=ALU.mult, op1=ALU.add), reads=[R_one, R_Fb], writes=[R_Cb])
    S.op("dve", lambda e: e.tensor_copy(out=CS[:, 0, :], in_=Cb[:]), reads=[R_Cb], writes=[R_CS])
    S.op("dve", lambda e: e.tensor_tensor(out=Fb[:], in0=Cb[:], in1=CS[:, 0, :], op=ALU.subtract), reads=[R_Cb, R_CS], writes=[R_Fb])
    S.op("dve", lambda e: e.tensor_copy(out=CS[:, 1, :], in_=Fb[:]), reads=[R_Fb], writes=[R_CS])
    S.op("dve", lambda e: e.tensor_tensor(out=Cb[:], in0=Fb[:], in1=CS[:, 1, :], op=ALU.subtract), reads=[R_Fb, R_CS], writes=[R_Cb])
    S.op("dve", lambda e: e.tensor_copy(out=CS[:, 2, :], in_=Cb[:]), reads=[R_Cb], writes=[R_CS])
    S.dma("sp", lambda e: e.dma_start(out=qaT[:, 64:67, :], in_=CS[:]), reads=[R_CS])
    S.dma("sp", lambda e: e.dma_start(out=kaT[:, 67:70, :], in_=CS[:]), reads=[R_CS])
    T.finish()

    T = Stage(nc, "a2")
    S = T.S
    for r in R_xT:
        r.last_w = None
        r.readers = {}
    cosb = T.sb([128, S_LEN], F32, "cos")
    sinb = T.sb([128, S_LEN], F32, "sin")
    R_cos, R_sin = RL(2)
    cosT, sinT = P.cosT(), P.sinT()
    for two in range(2):
        S.dma("sp", lambda e, two=two: e.dma_start(out=cosb[two * 64:(two + 1) * 64, :], in_=cosT[:, :]), writes=[R_cos])
        S.dma("sp", lambda e, two=two: e.dma_start(out=sinb[two * 64:(two + 1) * 64, :], in_=sinT[:, :]), writes=[R_sin])
    wsl = Rot([T.sb([128, 8, 512], BF16, "w") for _ in range(2)])
    wsw = Rot([T.sb([128, 8, 512], BF16, "wsw") for _ in range(2)])
    stg = Rot([T.sb([128, S_LEN], BF16, "stg") for _ in range(2)])
    t1s = Rot([T.sb([128, 512], F32, "t1") for _ in range(3)])
    t2s = Rot([T.sb([128, 512], F32, "t2") for _ in range(3)])
    ppa = Rot([T.ps([128, 512], F32, "ppa") for _ in range(3)])
    ppb = Rot([T.ps([128, 512], F32, "ppb") for _ in range(3)])
    qbT, kbT = P.qbT(), P.kbT()
    for which, c0, dst, scl in (("q", C_QB, qbT, 1.0), ("k", C_KB, kbT, 0.125)):
        w, rw = wsl.next()
        ws, rws = wsw.next()
        S.dma("pool", lambda e, w=w, c0=c0: e.dma_start(out=w[:], in_=w_in[layer, :, c0:c0 + 512].rearrange("(k p) n -> p k n", p=128)), writes=[rw])
        wv = w[:].rearrange("p k (h two j) -> p k h two j", two=2, j=32)
        wsv = ws[:].rearrange("p k (h two j) -> p k h two j", two=2, j=32)
        for k in range(8):
            S.op("pool", lambda e, k=k, wv=wv, wsv=wsv: e.tensor_copy(out=wsv[:, k, :, 0, :], in_=wv[:, k, :, 1, :]), reads=[rw], writes=[rws])
            S.op("pool", lambda e, k=k, wv=wv, wsv=wsv: e.tensor_copy(out=wsv[:, k, :, 1, :], in_=wv[:, k, :, 0, :]), reads=[rw], writes=[rws])
        for hp in range(4):
            sg, rsg = stg.next()
            for t in range(NT):
                pa, rpa = ppa.next()
                pb, rpb = ppb.next()
                for k in range(8):
                    S.op("pe", lambda e, pa=pa, k=k, t=t, w=w, hp=hp: e.matmul(pa[:, :], lhsT=w[:, k, hp * 128:(hp + 1) * 128], rhs=xT[:, k, t * 512:(t + 1) * 512],
                                                                               start=(k == 0), stop=(k == 7)),
                         reads=[rw] + R_xT[4 * t:4 * t + 4], writes=[rpa])
                for k in range(8):
                    S.op("pe", lambda e, pb=pb, k=k, t=t, ws=ws, hp=hp: e.matmul(pb[:, :], lhsT=ws[:, k, hp * 128:(hp + 1) * 128], rhs=xT[:, k, t * 512:(t + 1) * 512],
                                                                                 start=(k == 0), stop=(k == 7)),
                         reads=[rws] + R_xT[4 * t:4 * t + 4], writes=[rpb])
                t1, rt1 = t1s.next()
                t2, rt2 = t2s.next()
                S.op("dve", lambda e, t1=t1, pa=pa, t=t, scl=scl: e.scalar_tensor_tensor(out=t1[:], in0=pa[:, :], scalar=scl, in1=cosb[:, t * 512:(t + 1) * 512],
                                                                                         op0=ALU.mult, op1=ALU.mult), reads=[rpa, R_cos], writes=[rt1])
                S.op("dve", lambda e, t2=t2, pb=pb, t=t, scl=scl: e.scalar_tensor_tensor(out=t2[:], in0=pb[:, :], scalar=scl, in1=sinb[:, t * 512:(t + 1) * 512],
                                                                                         op0=ALU.mult, op1=ALU.mult), reads=[rpb, R_sin], writes=[rt2])
                S.op("pool", lambda e, t1=t1, t2=t2, sg=sg, t=t: e.tensor_tensor(out=sg[:, t * 512:(t + 1) * 512], in0=t1[:], in1=t2[:], op=ALU.add),
                     reads=[rt1, rt2], writes=[rsg])
            S.dma("sp", lambda e, sg=sg, hp=hp, dst=dst: e.dma_start(out=dst[hp * 128:(hp + 1) * 128, :], in_=sg[:, :]), reads=[rsg])
    T.finish()

    T = Stage(nc, "a3")
    S = T.S
    for r in R_xT:
        r.last_w = None
        r.readers = {}
    wts = []
    for c0 in (C_VA, C_VB, C_VB + 512, C_GB, C_GB + 512):
        w = T.sb([128, 8, 512], BF16, "w")
        rw = Res()
        S.dma("pool", lambda e, w=w, c0=c0: e.dma_start(out=w[:], in_=w_in[layer, :, c0:c0 + 512].rearrange("(k p) n -> p k n", p=128)), writes=[rw])
        wts.append((w, rw))
    vas = Rot([T.sb([128, 8, 65], BF16, "vas") for _ in range(2)])
    vbs = Rot([T.sb([128, D], BF16, "vbs") for _ in range(2)])
    sgs = Rot([T.sb([128, D], BF16, "sgs") for _ in range(2)])
    pp = Rot([T.ps([128, 512], F32, "pp") for _ in range(6)])
    for tl, rs in zip(vas.tiles, vas.res):
        S.op("pool", lambda e, tl=tl: e.memset(tl[:], 1.0), writes=[rs])
    vA, vB, sgB = P.vA(), P.vB(), P.sgB()
    for blk in range(NB):
        va, rva = vas.next()
        vb, rvb = vbs.next()
        sg, rsg = sgs.next()
        for gi, (w, rw) in enumerate(wts):
            p, rp = pp.next()
            for k in range(8):
                S.op("pe", lambda e, p=p, k=k, w=w, blk=blk: e.matmul(p[:, :], lhsT=xT[:, k, blk * 128:(blk + 1) * 128], rhs=w[:, k, :],
                                                                     start=(k == 0), stop=(k == 7)),
                     reads=[rw, R_xT[blk]], writes=[rp])
            if gi == 0:
                S.op("dve", lambda e, p=p, va=va: e.tensor_copy(out=va[:, :, 0:64], in_=p[:, :].rearrange("p (h d) -> p h d", d=64)), reads=[rp], writes=[rva])
            elif gi in (1, 2):
                o = (gi - 1) * 512
                S.op("dve", lambda e, p=p, vb=vb, o=o: e.tensor_copy(out=vb[:, o:o + 512], in_=p[:, :]), reads=[rp], writes=[rvb])
            else:
                o = (gi - 3) * 512
                S.op("act", lambda e, p=p, sg=sg, o=o: e.activation(out=sg[:, o:o + 512], in_=p[:, :], func=AF.Silu), reads=[rp], writes=[rsg])
        r0 = blk * 128
        S.dma("sp", lambda e, va=va, r0=r0: e.dma_start(out=vA[r0:r0 + 128, :], in_=va[:].rearrange("p h d -> p (h d)")), reads=[rva])
        S.dma("sp", lambda e, vb=vb, r0=r0: e.dma_start(out=vB[r0:r0 + 128, :], in_=vb[:]), reads=[rvb])
        S.dma("sp", lambda e, sg=sg, r0=r0: e.dma_start(out=sgB[r0:r0 + 128, :], in_=sg[:]), reads=[rsg])
    T.finish()
    OUT.close()


def stageB(P, layer):
    nc = P.nc
    T = Stage(nc, "b")
    S = T.S
    qaT, kaT, vA, oaT = P.qaT(), P.kaT(), P.vA(), P.oaT()
    identb, R_id = load_ident(T, P, S)
    mqb = T.sb([128, 128], BF16, "mq")
    R_mq = Res()
    c_mq = P.inp("c_mq")
    S.dma("pool", lambda e: e.dma_start(out=mqb[:], in_=c_mq[:, :]), writes=[R_mq])
    onesf = T.sb([1, 64], F32, "ones")
    R_ones = Res()
    S.op("dve", lambda e: e.memset(onesf[:], 1.0), writes=[R_ones])
    vt = T.sb([128, NB, VAW], BF16, "vt")
    R_vt = RL(4)
    vsrc = vA.rearrange("(b p) c -> p b c", p=128)
    for g in range(4):
        S.dma("sp", lambda e, g=g: e.dma_start(out=vt[:, g * 8:(g + 1) * 8, :], in_=vsrc[:, g * 8:(g + 1) * 8, :]), writes=[R_vt[g]])
    qs = Rot([T.sb([70, S_LEN], BF16, "q") for _ in range(2)])
    ks = Rot([T.sb([70, S_LEN], BF16, "k") for _ in range(2)])
    oas = Rot([T.sb([64, S_LEN], BF16, "oa") for _ in range(2)])
    pts = Rot([T.sb([128, 512], BF16, "pt") for _ in range(5)])
    rls = Rot([T.sb([1, 512], F32, "rl") for _ in range(2)])
    bcs = Rot([T.sb([64, 512], F32, "bcs") for _ in range(2)])
    s_ps = Rot([T.ps([128, 512], F32, "s") for _ in range(4)])
    o_ps = Rot([T.ps([128, 512], F32, "o") for _ in range(2)])
    bc_ps = Rot([T.ps([128, 512], F32, "bc") for _ in range(1)])

    heads = {}

    def load_head(h):
        q, rq = qs.next()
        k, rk = ks.next()
        S.dma("sp", lambda e: e.dma_start(out=q[:], in_=qaT[h, :, :]), writes=[rq])
        S.dma("sp", lambda e: e.dma_start(out=k[:], in_=kaT[h, :, :]), writes=[rk])
        heads[h] = (q, rq, k, rk)

    units = [(h, I, j) for h in range(8) for I in range(NT) for j in range(4 * I + 4)]
    st = {}
    tiles = {}

    def emit_qk(u):
        h, I, j = u
        if I == 0 and j == 0:
            if h == 0:
                load_head(0)
            if h + 1 < 8:
                load_head(h + 1)
        q, rq, k, rk = heads[h]
        m = j - 4 * I
        c0 = 128 * m if m > 0 else 0
        sp, rsp = s_ps.next()
        S.op("pe", lambda e: e.matmul(sp[:, c0:512], lhsT=k[0:70, j * 128:(j + 1) * 128], rhs=q[0:70, I * 512 + c0:(I + 1) * 512],
                                      start=True, stop=(m < 0)), reads=[rq, rk], writes=[rsp])
        if m >= 0:
            S.op("pe", lambda e: e.matmul(sp[:, c0:c0 + 128], lhsT=identb[:, :], rhs=mqb[:, :], start=False, stop=True),
                 reads=[R_id, R_mq], writes=[rsp])
        st[u] = (sp, rsp, c0)

    def emit_pv(u):
        h, I, j = u
        sp, rsp, c0 = st.pop(u)
        nkb = 4 * I + 4
        if j == 0:
            tiles[(h, I)] = o_ps.next()
            if I == 0:
                tiles[("oa", h)] = oas.next()
        op_, rop = tiles[(h, I)]
        pt, rpt = pts.next()
        S.op("act", lambda e: e.activation(out=pt[:, c0:512], in_=sp[:, c0:512], func=AF.Exp), reads=[rsp], writes=[rpt])
        S.op("pe", lambda e: e.matmul(op_[0:65, c0:512], lhsT=vt[:, j, h * 65:(h + 1) * 65], rhs=pt[:, c0:512],
                                      start=(j == 0), stop=(j == nkb - 1)), reads=[rpt, R_vt[j // 8]], writes=[rop])
        if j == nkb - 1:
            oa, roa = tiles[("oa", h)]
            rl, rrl = rls.next()
            bc, rbc = bcs.next()
            bp, rbp = bc_ps.next()
            S.op("dve", lambda e: e.reciprocal(out=rl[0:1, :], in_=op_[64:65, :]), reads=[rop], writes=[rrl])
            S.op("pe", lambda e: e.matmul(bp[0:64, :], lhsT=onesf[0:1, 0:64], rhs=rl[0:1, :], start=True, stop=True),
                 reads=[R_ones, rrl], writes=[rbp])
            S.op("act", lambda e: e.copy(out=bc[:, :], in_=bp[0:64, :]), reads=[rbp], writes=[rbc])
            S.op("dve", lambda e: e.tensor_tensor(out=oa[:, I * 512:(I + 1) * 512], in0=op_[0:64, :], in1=bc[:, :], op=ALU.mult),
                 reads=[rop, rbc], writes=[roa])
            del tiles[(h, I)]
            if I == NT - 1:
                S.dma("sp", lambda e: e.dma_start(out=oaT[h * 64:(h + 1) * 64, :], in_=oa[:, :]), reads=[roa])

    LOOK = 3
    for i in range(min(LOOK, len(units))):
        emit_qk(units[i])
    for i, u in enumerate(units):
        if i + LOOK < len(units):
            emit_qk(units[i + LOOK])
        emit_pv(u)
    T.finish()


def stageC(P, layer):
    nc = P.nc
    T = Stage(nc, "c")
    S = T.S
    qbT, kbT, vB, sgB, obT = P.qbT(), P.kbT(), P.vB(), P.sgB(), P.obT()
    identb, R_id = load_ident(T, P, S)
    dtp = T.sb([128, 8, 128], F32, "dtp")
    qdec = T.sb([128, 8], F32, "qdec")
    kdec = T.sb([128, 8], F32, "kdec")
    cd = T.sb([128, 4, 128], F32, "cd")
    gn = T.sb([128, D], F32, "gn")
    R_dtp, R_qdec, R_kdec, R_cd, R_gn = RL(5)
    c_dtp, c_qdec, c_kdec, c_cd, rng = P.inp("c_dtp"), P.inp("c_qdec"), P.inp("c_kdec"), P.inp("c_cd"), P.inp("ret_norm_g")
    S.dma("sp", lambda e: e.dma_start(out=dtp[:], in_=c_dtp[:, :, :]), writes=[R_dtp])
    S.dma("sp", lambda e: e.dma_start(out=qdec[:], in_=c_qdec[:, :]), writes=[R_qdec])
    S.dma("sp", lambda e: e.dma_start(out=kdec[:], in_=c_kdec[:, :]), writes=[R_kdec])
    S.dma("sp", lambda e: e.dma_start(out=cd[:], in_=c_cd[:, :, :]), writes=[R_cd])
    S.dma("sp", lambda e: e.dma_start(out=gn[:], in_=rng[layer:layer + 1, :].partition_broadcast(128)), writes=[R_gn])
    state = T.sb([128, 4, 128], F32, "state")
    state_bf = T.sb([128, 4, 128], BF16, "statebf")
    R_state, R_sbf = RL(2)
    S.op("pool", lambda e: e.memset(state[:], 0.0), writes=[R_state])
    S.op("pool", lambda e: e.memset(state_bf[:], 0.0), writes=[R_sbf])
    q4s = Rot([T.sb([128, 4, 512], BF16, "q4") for _ in range(2)])
    k4s = Rot([T.sb([128, 4, 512], BF16, "k4") for _ in range(2)])
    vs = Rot([T.sb([128, D], BF16, "v") for _ in range(3)])
    sgs = Rot([T.sb([128, D], BF16, "sg") for _ in range(3)])
    sTs = Rot([T.sb([128, 8, 128], BF16, "sT") for _ in range(2)])
    kss = Rot([T.sb([128, 8, 64], BF16, "ks") for _ in range(2)])
    ys = Rot([T.sb([128, 8, 128], F32, "y") for _ in range(2)])
    ysq = Rot([T.sb([128, 8, 128], F32, "ysq") for _ in range(1)])
    sss = Rot([T.sb([128, 8], F32, "ss") for _ in range(2)])
    obs = Rot([T.sb([128, D], BF16, "ob") for _ in range(2)])
    obt = Rot([T.sb([128, 8, 512], BF16, "obt") for _ in range(2)])
    sc_ps = Rot([T.ps([128, 8, 128], F32, "sc") for _ in range(1)])
    kt_ps = Rot([T.ps([128, 8, 128], BF16, "kt") for _ in range(1)])
    kv_ps = Rot([T.ps([128, 4, 256], F32, "kv") for _ in range(1)])
    y_ps = Rot([T.ps([128, 8, 128], F32, "yp") for _ in range(1)])
    ot_ps = Rot([T.ps([128, 8, 128], BF16, "ot") for _ in range(1)])
    qsrc = qbT.rearrange("(hp q) n -> q hp n", q=128)
    ksrc = kbT.rearrange("(hp q) n -> q hp n", q=128)
    ctx = {}

    def front(n):
        cc = n % 4
        if cc == 0:
            t = n // 4
            q4, rq4 = q4s.next()
            k4, rk4 = k4s.next()
            S.dma("sp", lambda e: e.dma_start(out=q4[:], in_=qsrc[:, :, t * 512:(t + 1) * 512]), writes=[rq4])
            S.dma("sp", lambda e: e.dma_start(out=k4[:], in_=ksrc[:, :, t * 512:(t + 1) * 512]), writes=[rk4])
            ctx["q4"] = (q4, rq4, k4, rk4)
        q4, rq4, k4, rk4 = ctx["q4"]
        v, rv = vs.next()
        sg, rsg = sgs.next()
        S.dma("sp", lambda e: e.dma_start(out=v[:], in_=vB[n * 128:(n + 1) * 128, :]), writes=[rv])
        S.dma("sp", lambda e: e.dma_start(out=sg[:], in_=sgB[n * 128:(n + 1) * 128, :]), writes=[rsg])
        sc, rsc = sc_ps.next()
        cs = slice(cc * 128, (cc + 1) * 128)
        if C_LEVEL == 0:
            return
        for h in range(8):
            hp, two = h // 2, h % 2
            S.op("pe", lambda e, h=h, hp=hp, two=two: e.matmul(sc[:, two * 4 + hp, :], lhsT=k4[two * 64:(two + 1) * 64, hp, cs], rhs=q4[two * 64:(two + 1) * 64, hp, cs],
                                                               start=True, stop=True), reads=[rq4, rk4], writes=[rsc])
        sT, rsT = sTs.next()
        S.op("dve", lambda e: e.tensor_tensor(out=sT[:], in0=sc[:], in1=dtp[:], op=ALU.mult), reads=[rsc, R_dtp], writes=[rsT])
        kt, rkt = kt_ps.next()
        for hp in range(4):
            S.op("pe", lambda e, hp=hp: e.transpose(out=kt[:, hp, :], in_=k4[:, hp, cs], identity=identb[:]), reads=[rk4, R_id], writes=[rkt])
        ks_, rks = kss.next()
        S.op("dve", lambda e: e.tensor_tensor(out=ks_[:], in0=kt[:, 0:4, :].rearrange("p a (two d) -> p (a two) d", two=2),
                                              in1=kdec[:, :].unsqueeze(2).to_broadcast([128, 8, 64]), op=ALU.mult),
             reads=[rkt, R_kdec], writes=[rks])
        kv, rkv = kv_ps.next()
        for hp in range(4):
            S.op("pe", lambda e, hp=hp: e.matmul(kv[:, hp, :], lhsT=ks_[:, 2 * hp:2 * hp + 2, :].rearrange("p a d -> p (a d)"),
                                                 rhs=v[:, hp * 256:(hp + 1) * 256], start=True, stop=True), reads=[rks, rv], writes=[rkv])
        ctx[n] = dict(q4=q4, rq4=rq4, v=v, rv=rv, sg=sg, rsg=rsg, sT=sT, rsT=rsT, kv=kv, rkv=rkv, cs=cs, cc=cc)

    def su(n):
        c = ctx[n]
        kv, rkv = c["kv"], c["rkv"]
        S.op("pool", lambda e: e.tensor_tensor(out=state[:], in0=state[:], in1=cd[:], op=ALU.mult), reads=[R_state, R_cd], writes=[R_state])
        for two in range(2):
            ps_ = slice(two * 64, (two + 1) * 64)
            S.op("dve", lambda e, ps_=ps_, two=two: e.tensor_tensor(out=state[ps_, :, :], in0=state[ps_, :, :], in1=kv[ps_, :, two * 128:(two + 1) * 128], op=ALU.add),
                 reads=[R_state, rkv], writes=[R_state])

    def sbf_copy():
        S.op("act", lambda e: e.copy(out=state_bf[:], in_=state[:]), reads=[R_state], writes=[R_sbf])

    def mid(n):
        c = ctx[n]
        q4, rq4, v, rv, sT, rsT, cs = c["q4"], c["rq4"], c["v"], c["rv"], c["sT"], c["rsT"], c["cs"]
        yp, ryp = y_ps.next()
        for h in range(8):
            hp, two = h // 2, h % 2
            S.op("pe", lambda e, h=h, hp=hp, two=two: e.matmul(yp[:, two * 4 + hp, :], lhsT=sT[:, two * 4 + hp, :], rhs=v[:, h * 128:(h + 1) * 128], start=True, stop=False),
                 reads=[rsT, rv], writes=[ryp])
            S.op("pe", lambda e, h=h, hp=hp, two=two: e.matmul(yp[:, two * 4 + hp, :], lhsT=q4[two * 64:(two + 1) * 64, hp, cs],
                                                               rhs=state_bf[two * 64:(two + 1) * 64, hp, :], start=False, stop=True),
                 reads=[rq4, R_sbf], writes=[ryp])
        y, ry = ys.next()
        S.op("dve", lambda e: e.tensor_tensor(out=y[:].rearrange("p (hp two) e -> p two hp e", two=2),
                                              in0=yp[:].rearrange("p (two hp) e -> p two hp e", two=2),
                                              in1=qdec[:, :].rearrange("p (two hp) -> p two hp", two=2).unsqueeze(3).to_broadcast([128, 2, 4, 128]), op=ALU.mult),
             reads=[ryp, R_qdec], writes=[ry])
        c["y"], c["ry"] = y, ry

    def back(n):
        c = ctx.pop(n)
        y, ry, sg, rsg, cc = c["y"], c["ry"], c["sg"], c["rsg"], c["cc"]
        sq, rsq = ysq.next()
        ss, rss = sss.next()
        S.op("pool", lambda e: e.tensor_tensor(out=sq[:], in0=y[:], in1=y[:], op=ALU.mult), reads=[ry], writes=[rsq])
        S.op("dve", lambda e: e.tensor_reduce(out=ss[:], in_=sq[:], axis=AX.X, op=ALU.add), reads=[rsq], writes=[rss])
        S.op("dve", lambda e: e.tensor_scalar(out=ss[:], in0=ss[:], scalar1=1.0 / 128.0, scalar2=RMS_EPS, op0=ALU.mult, op1=ALU.add), reads=[rss], writes=[rss])
        S.op("act", lambda e: e.sqrt(out=ss[:], in_=ss[:]), reads=[rss], writes=[rss])
        S.op("dve", lambda e: e.reciprocal(out=ss[:], in_=ss[:]), reads=[rss], writes=[rss])
        S.op("dve", lambda e: e.tensor_tensor(out=y[:], in0=y[:], in1=ss[:, :].unsqueeze(2).to_broadcast([128, 8, 128]), op=ALU.mult), reads=[ry, rss], writes=[ry])
        yf = y[:].rearrange("p h e -> p (h e)")
        S.op("pool", lambda e: e.tensor_tensor(out=yf, in0=yf, in1=gn[:], op=ALU.mult), reads=[ry, R_gn], writes=[ry])
        ob, rob = obs.next()
        S.op("dve", lambda e: e.tensor_tensor(out=ob[:], in0=yf, in1=sg[:], op=ALU.mult), reads=[ry, rsg], writes=[rob])
        ot, rot = ot_ps.next()
        for k in range(8):
            S.op("pe", lambda e, k=k: e.transpose(out=ot[:, k, :], in_=ob[:, k * 128:(k + 1) * 128], identity=identb[:]), reads=[rob, R_id], writes=[rot])
        if cc == 0:
            ctx["obt"] = obt.next()
        ob4, rob4 = ctx["obt"]
        S.op("act", lambda e: e.copy(out=ob4[:, :, cc * 128:(cc + 1) * 128], in_=ot[:]), reads=[rot], writes=[rob4])
        if cc == 3:
            t = n // 4
            S.dma("sp", lambda e: e.dma_start(out=obT.rearrange("(c p) n -> p c n", p=128)[:, :, t * 512:(t + 1) * 512], in_=ob4[:]), reads=[rob4])

    LV = C_LEVEL
    NCH = C_NCH
    front(0)
    if LV >= 2:
        su(0)
    for n in range(NCH):
        if n + 1 < NCH:
            front(n + 1)
        if LV >= 3:
            mid(n)
        if n + 1 < NCH and LV >= 2:
            if LV >= 3:
                sbf_copy()
            su(n + 1)
        if n >= 1 and LV >= 4:
            back(n - 1)
    if LV >= 4:
        back(NCH - 1)
    T.finish()


class LNBufs:
    def __init__(self, T, S, P, gname, bname, layer):
        self.g_bc = T.sb([128, D], F32, "lng")
        self.b_bc = T.sb([128, D], F32, "lnb")
        self.R_g, self.R_b = RL(2)
        g, b = P.inp(gname), P.inp(bname)
        S.dma("sp", lambda e: e.dma_start(out=self.g_bc[:], in_=g[layer:layer + 1, :].partition_broadcast(128)), writes=[self.R_g])
        S.dma("sp", lambda e: e.dma_start(out=self.b_bc[:], in_=b[layer:layer + 1, :].partition_broadcast(128)), writes=[self.R_b])
        self.stats = Rot([T.sb([128, 2, 6], F32, "lnst") for _ in range(2)])
        self.mv = Rot([T.sb([128, 2], F32, "lnmv") for _ in range(2)])


def emit_ln(S, L, r, rr, dst_rows, gb_eng="pool"):
    st, rst = L.stats.next()
    mv, rmv = L.mv.next()
    S.op("dve", lambda e: e.bn_stats(out=st[:, 0, :], in_=r[:, 0:512]), reads=[rr], writes=[rst])
    S.op("dve", lambda e: e.bn_stats(out=st[:, 1, :], in_=r[:, 512:1024]), reads=[rr], writes=[rst])
    S.op("dve", lambda e: e.bn_aggr(out=mv[:], in_=st[:].rearrange("p a b -> p (a b)")), reads=[rst], writes=[rmv])
    S.op("dve", lambda e: e.tensor_scalar(out=mv[:, 1:2], in0=mv[:, 1:2], scalar1=LN_EPS, scalar2=None, op0=ALU.add), reads=[rmv], writes=[rmv])
    S.op("act", lambda e: e.sqrt(out=mv[:, 1:2], in_=mv[:, 1:2]), reads=[rmv], writes=[rmv])
    S.op("dve", lambda e: e.reciprocal(out=mv[:, 1:2], in_=mv[:, 1:2]), reads=[rmv], writes=[rmv])
    S.op("dve", lambda e: e.tensor_scalar(out=r[:], in0=r[:], scalar1=mv[:, 0:1], scalar2=mv[:, 1:2], op0=ALU.subtract, op1=ALU.mult),
         reads=[rr, rmv], writes=[rr])
    S.op(gb_eng, lambda e: e.tensor_tensor(out=r[:], in0=r[:], in1=L.g_bc[:], op=ALU.mult), reads=[rr, L.R_g], writes=[rr])
    S.op(gb_eng, lambda e: e.tensor_tensor(out=r[:], in0=r[:], in1=L.b_bc[:], op=ALU.add), reads=[rr, L.R_b], writes=[rr])
    S.dma("sp", lambda e: e.dma_start(out=dst_rows, in_=r[:]), reads=[rr])


def stageD(P, layer, x_src, x_dst):
    nc = P.nc
    T = Stage(nc, "d")
    S = T.S
    oaT, obT, gT = P.oaT(), P.obT(), P.gT()
    wf = T.sb([128, 4, D], BF16, "wf")
    wr = T.sb([128, 8, D], BF16, "wr")
    wo = T.sb([128, 8, D], BF16, "wo")
    R_wf, R_wr, R_wo = RL(3)
    w_fox, w_ret, w_out = P.inp("w_branch_fox"), P.inp("w_branch_ret"), P.inp("w_out")
    for k in range(4):
        S.dma("pool", lambda e, k=k: e.dma_start(out=wf[:, k, :], in_=w_fox[layer, k * 128:(k + 1) * 128, :]), writes=[R_wf])
    for k in range(8):
        S.dma("pool", lambda e, k=k: e.dma_start(out=wr[:, k, :], in_=w_ret[layer, k * 128:(k + 1) * 128, :]), writes=[R_wr])
    for k in range(8):
        S.dma("pool", lambda e, k=k: e.dma_start(out=wo[:, k, :], in_=w_out[layer, k * 128:(k + 1) * 128, :]), writes=[R_wo])
    L = LNBufs(T, S, P, "ln_mix_g", "ln_mix_b", layer)
    oas = Rot([T.sb([128, 4, 512], BF16, "oa") for _ in range(2)])
    obs = Rot([T.sb([128, 8, 512], BF16, "ob") for _ in range(2)])
    gs = Rot([T.sb([128, 16, 512], BF16, "g") for _ in range(2)])
    mTs = Rot([T.sb([128, 8, 512], BF16, "mT") for _ in range(2)])
    t1s = Rot([T.sb([128, 512], F32, "t1") for _ in range(2)])
    t2s = Rot([T.sb([128, 512], F32, "t2") for _ in range(2)])
    xs_ = Rot([T.sb([128, D], F32, "x") for _ in range(3)])
    pa_ = Rot([T.ps([128, 512], F32, "pa") for _ in range(2)])
    pb_ = Rot([T.ps([128, 512], F32, "pb") for _ in range(2)])
    ph_ = Rot([T.ps([128, 512], F32, "ph") for _ in range(2)])
    oasrc = oaT.rearrange("(k p) n -> p k n", p=128)
    obsrc = obT.rearrange("(k p) n -> p k n", p=128)
    gsrc = gT.rearrange("(k p) n -> p k n", p=128)
    mts = {}

    def emit_merge(t):
        ts_ = slice(t * 512, (t + 1) * 512)
        oa, roa = oas.next()
        ob, rob = obs.next()
        g, rg = gs.next()
        mT, rmT = mTs.next()
        S.dma("sp", lambda e: e.dma_start(out=oa[:], in_=oasrc[:, :, ts_]), writes=[roa])
        S.dma("sp", lambda e: e.dma_start(out=ob[:], in_=obsrc[:, :, ts_]), writes=[rob])
        S.dma("sp", lambda e: e.dma_start(out=g[:, 0:8, :], in_=gsrc[:, 0:8, ts_]), writes=[rg])
        S.dma("sp", lambda e: e.dma_start(out=g[:, 8:16, :], in_=gsrc[:, 8:16, ts_]), writes=[rg])
        for cc in range(8):
            pa, rpa = pa_.next()
            pb, rpb = pb_.next()
            for k in range(4):
                S.op("pe", lambda e, pa=pa, k=k, cc=cc: e.matmul(pa[:, :], lhsT=wf[:, k, cc * 128:(cc + 1) * 128], rhs=oa[:, k, :], start=(k == 0), stop=(k == 3)),
                     reads=[R_wf, roa], writes=[rpa])
            for k in range(8):
                S.op("pe", lambda e, pb=pb, k=k, cc=cc: e.matmul(pb[:, :], lhsT=wr[:, k, cc * 128:(cc + 1) * 128], rhs=ob[:, k, :], start=(k == 0), stop=(k == 7)),
                     reads=[R_wr, rob], writes=[rpb])
            t1, rt1 = t1s.next()
            t2, rt2 = t2s.next()
            S.op("dve", lambda e, t1=t1, pa=pa, cc=cc: e.tensor_tensor(out=t1[:], in0=pa[:, :], in1=g[:, cc, :], op=ALU.mult), reads=[rpa, rg], writes=[rt1])
            S.op("dve", lambda e, t2=t2, pb=pb, cc=cc: e.tensor_tensor(out=t2[:], in0=pb[:, :], in1=g[:, 8 + cc, :], op=ALU.mult), reads=[rpb, rg], writes=[rt2])
            S.op("pool", lambda e, t1=t1, t2=t2, cc=cc: e.tensor_tensor(out=mT[:, cc, :], in0=t1[:], in1=t2[:], op=ALU.add), reads=[rt1, rt2], writes=[rmT])
        mts[t] = (mT, rmT)

    def emit_out(t):
        mT, rmT = mts.pop(t)
        for tb in range(4):
            blk = t * 4 + tb
            x, rx = xs_.next()
            S.dma("sp", lambda e, x=x, blk=blk: e.dma_start(out=x[:], in_=x_src[blk * 128:(blk + 1) * 128, :]), writes=[rx])
            for hf in range(2):
                ph, rph = ph_.next()
                for cc in range(8):
                    S.op("pe", lambda e, ph=ph, cc=cc, tb=tb, hf=hf: e.matmul(ph[:, :], lhsT=mT[:, cc, tb * 128:(tb + 1) * 128], rhs=wo[:, cc, hf * 512:(hf + 1) * 512],
                                                                            start=(cc == 0), stop=(cc == 7)), reads=[rmT, R_wo], writes=[rph])
                S.op("dve", lambda e, x=x, ph=ph, hf=hf: e.scalar_tensor_tensor(out=x[:, hf * 512:(hf + 1) * 512], in0=x[:, hf * 512:(hf + 1) * 512], scalar=DN_ALPHA, in1=ph[:, :],
                                                                                op0=ALU.mult, op1=ALU.add), reads=[rx, rph], writes=[rx])
            emit_ln(S, L, x, rx, x_dst[blk * 128:(blk + 1) * 128, :])

    emit_merge(0)
    for t in range(NT):
        if t + 1 < NT:
            emit_merge(t + 1)
        emit_out(t)
    T.finish()


def stageE0(P, layer, x_src):
    nc = P.nc
    i = layer // 2
    T = Stage(nc, "r")
    S = T.S
    identf = T.sb([128, 128], F32, "identf")
    wrf = T.sb([128, 8, 8], F32, "wrf")
    gate_all = T.sb([128, NB, 8], F32, "gates")
    R_id, R_wr, R_ga = RL(3)
    c_ident, router = P.inp("c_ident"), P.inp("moe_router")
    S.dma("sp", lambda e: e.dma_start(out=identf[:], in_=c_ident[:, :]), writes=[R_id])
    S.dma("sp", lambda e: e.dma_start(out=wrf[:], in_=router[i].rearrange("(k p) n -> p k n", p=128)), writes=[R_wr])
    xfs = Rot([T.sb([128, D], F32, "xf") for _ in range(2)])
    xTs = Rot([T.sb([128, 8, 128], F32, "xTf") for _ in range(2)])
    pT_ = Rot([T.ps([128, 8, 128], F32, "pTf") for _ in range(2)])
    lg_ = Rot([T.ps([128, 512], F32, "lg") for _ in range(2)])
    sm = [Rot([T.sb([128, 8], F32, "sm%d" % j) for _ in range(2)]) for j in range(5)]
    sc1 = [Rot([T.sb([128, 1], F32, "sc%d" % j) for _ in range(2)]) for j in range(4)]
    gd = P.scratch("gates_d", [128, NB, 8], F32)
    for blk in range(NB):
        xf, rxf = xfs.next()
        S.dma("sp", lambda e, xf=xf, blk=blk: e.dma_start(out=xf[:], in_=x_src[blk * 128:(blk + 1) * 128, :]), writes=[rxf])
        pT, rpT = pT_.next()
        for k in range(8):
            S.op("pe", lambda e, pT=pT, xf=xf, k=k: e.transpose(out=pT[:, k, :], in_=xf[:, k * 128:(k + 1) * 128], identity=identf[:]), reads=[rxf, R_id], writes=[rpT])
        xT, rxT = xTs.next()
        S.op("act", lambda e, xT=xT, pT=pT: e.copy(out=xT[:, 0:4, :], in_=pT[:, 0:4, :]), reads=[rpT], writes=[rxT])
        S.op("dve", lambda e, xT=xT, pT=pT: e.tensor_copy(out=xT[:, 4:8, :], in_=pT[:, 4:8, :]), reads=[rpT], writes=[rxT])
        lg, rlg = lg_.next()
        for k in range(8):
            S.op("pe", lambda e, lg=lg, xT=xT, k=k: e.matmul(lg[:, 0:8], lhsT=xT[:, k, :], rhs=wrf[:, k, :], start=(k == 0), stop=(k == 7)), reads=[rxT, R_wr], writes=[rlg])
        (lgs, rlgs), (eq, req), (l2, rl2), (sel, rsel), (ex, rex) = [r_.next() for r_ in sm]
        (m1, rm1), (m2, rm2), (nm1, rnm1), (den, rden) = [r_.next() for r_ in sc1]
        S.op("dve", lambda e, lgs=lgs, lg=lg: e.tensor_copy(out=lgs[:], in_=lg[:, 0:8]), reads=[rlg], writes=[rlgs])
        S.op("dve", lambda e, m1=m1, lgs=lgs: e.tensor_reduce(out=m1[:], in_=lgs[:], axis=AX.X, op=ALU.max), reads=[rlgs], writes=[rm1])
        S.op("dve", lambda e, eq=eq, lgs=lgs, m1=m1: e.tensor_scalar(out=eq[:], in0=lgs[:], scalar1=m1[:, 0:1], scalar2=None, op0=ALU.is_equal), reads=[rlgs, rm1], writes=[req])
        S.op("dve", lambda e, l2=l2, eq=eq, lgs=lgs: e.scalar_tensor_tensor(out=l2[:], in0=eq[:], scalar=-1.0e30, in1=lgs[:], op0=ALU.mult, op1=ALU.add), reads=[req, rlgs], writes=[rl2])
        S.op("dve", lambda e, m2=m2, l2=l2: e.tensor_reduce(out=m2[:], in_=l2[:], axis=AX.X, op=ALU.max), reads=[rl2], writes=[rm2])
        S.op("dve", lambda e, sel=sel, lgs=lgs, m2=m2: e.tensor_scalar(out=sel[:], in0=lgs[:], scalar1=m2[:, 0:1], scalar2=None, op0=ALU.is_ge), reads=[rlgs, rm2], writes=[rsel])
        S.op("dve", lambda e, nm1=nm1, m1=m1: e.tensor_scalar(out=nm1[:], in0=m1[:], scalar1=-1.0, scalar2=None, op0=ALU.mult), reads=[rm1], writes=[rnm1])
        S.op("act", lambda e, ex=ex, lgs=lgs, nm1=nm1: e.activation(out=ex[:], in_=lgs[:], func=AF.Exp, bias=nm1[:, 0:1], scale=1.0), reads=[rlgs, rnm1], writes=[rex])
        S.op("dve", lambda e, ex=ex, sel=sel: e.tensor_tensor(out=ex[:], in0=ex[:], in1=sel[:], op=ALU.mult), reads=[rex, rsel], writes=[rex])
        S.op("dve", lambda e, den=den, ex=ex: e.tensor_reduce(out=den[:], in_=ex[:], axis=AX.X, op=ALU.add), reads=[rex], writes=[rden])
        S.op("dve", lambda e, den=den: e.reciprocal(out=den[:], in_=den[:]), reads=[rden], writes=[rden])
        S.op("dve", lambda e, ex=ex, den=den, blk=blk: e.tensor_scalar(out=gate_all[:, blk, :], in0=ex[:], scalar1=den[:, 0:1], scalar2=None, op0=ALU.mult), reads=[rex, rden], writes=[R_ga])
    S.dma("sp", lambda e: e.dma_start(out=gd[:, :, :], in_=gate_all[:]), reads=[R_ga])
    T.finish()


def stageE(P, layer, x_src, x_dst, moe, pump=None, pump_n=0):
    nc = P.nc
    i = layer // 2
    if moe:
        NE, FF = N_EXP, D_FFE
        wg_d, wu_d, wd_d = P.inp("moe_w_gate"), P.inp("moe_w_up"), P.inp("moe_w_down")
        wg = lambda e_: wg_d[i, e_]
        wu = lambda e_: wu_d[i, e_]
        wd = lambda e_: wd_d[i, e_]
    else:
        NE, FF = 1, D_FF
        wg_d, wu_d, wd_d = P.inp("ffn_w_gate"), P.inp("ffn_w_up"), P.inp("ffn_w_down")
        wg = lambda e_: wg_d[i]
        wu = lambda e_: wu_d[i]
        wd = lambda e_: wd_d[i]
    GC = 2
    NG = FF // (128 * GC)
    HB = NB // 2
    T = Stage(nc, "e")
    S = T.S
    ident, R_id = load_ident(T, P, S)
    L = LNBufs(T, S, P, "ln_ffn_g", "ln_ffn_b", layer)
    gate_all = None
    R_ga = Res()
    if moe:
        gate_all = T.sb([128, NB, 8], F32, "gates")
        gd = P.scratch("gates_d", [128, NB, 8], F32)
        S.dma("sp", lambda e: e.dma_start(out=gate_all[:], in_=gd[:, :, :]), writes=[R_ga])
    if pump is not None:
        pump.attach(T, 3)
    xT = T.sb([128, 8, HB * 128], BF16, "xT")
    acc = T.sb([128, HB, D], F32, "acc")
    R_xT = RL(HB)
    R_acc = RL(HB)
    wgs = Rot([T.sb([128, 8, GC * 128], BF16, "wg") for _ in range(2)])
    wus = Rot([T.sb([128, 8, GC * 128], BF16, "wu") for _ in range(2)])
    wds = Rot([T.sb([128, GC, D], BF16, "wd") for _ in range(2)])
    uTs = Rot([T.sb([128, GC, 512], BF16, "uT") for _ in range(3)])
    sgs = Rot([T.sb([128, 512], F32, "sg") for _ in range(2)])
    pg_ = Rot([T.ps([128, 512], F32, "pg") for _ in range(2)])
    pu_ = Rot([T.ps([128, 512], F32, "pu") for _ in range(2)])
    pd_ = Rot([T.ps([128, 512], F32, "pd") for _ in range(2)])
    xf = Rot([T.sb([128, D], F32, "xf") for _ in range(2)])
    xb = Rot([T.sb([128, D], BF16, "xb") for _ in range(2)])
    pT = Rot([T.ps([128, 8, 128], BF16, "pT") for _ in range(2)])
    for half in range(2):
        tok0 = half * HB * 128
        for bl in range(HB):
            f, rf = xf.next()
            b, rb = xb.next()
            p, rp = pT.next()
            r0 = tok0 + bl * 128
            S.dma("sp", lambda e, f=f, r0=r0: e.dma_start(out=f[:], in_=x_src[r0:r0 + 128, :]), writes=[rf])
            S.op("act", lambda e, f=f, b=b: e.copy(out=b[:], in_=f[:]), reads=[rf], writes=[rb])
            for k in range(8):
                S.op("pe", lambda e, p=p, b=b, k=k: e.transpose(out=p[:, k, :], in_=b[:, k * 128:(k + 1) * 128], identity=ident[:]), reads=[rb, R_id], writes=[rp])
            S.op("dve", lambda e, p=p, bl=bl: e.tensor_copy(out=xT[:, :, bl * 128:(bl + 1) * 128], in_=p[:]), reads=[rp], writes=[R_xT[bl]])
        items = [(ex, g, t) for ex in range(NE) for g in range(NG) for t in range(HB // 4)]
        wcur = {}
        pend = {}

        def emit_gu(it):
            ex, g, t = it
            ff0 = g * GC * 128
            if t == 0:
                wgb, rwg = wgs.next()
                wub, rwu = wus.next()
                wdb, rwd = wds.next()
                S.dma("pool", lambda e: e.dma_start(out=wgb[:], in_=wg(ex)[:, ff0:ff0 + GC * 128].rearrange("(k p) n -> p k n", p=128)), writes=[rwg])
                S.dma("pool", lambda e: e.dma_start(out=wub[:], in_=wu(ex)[:, ff0:ff0 + GC * 128].rearrange("(k p) n -> p k n", p=128)), writes=[rwu])
                S.dma("pool", lambda e: e.dma_start(out=wdb[:], in_=wd(ex)[ff0:ff0 + GC * 128, :].rearrange("(c p) n -> p c n", p=128)), writes=[rwd])
                wcur[(ex, g)] = (wgb, rwg, wub, rwu, wdb, rwd)
            wgb, rwg, wub, rwu, wdb, rwd = wcur[(ex, g)]
            uT, ruT = uTs.next()
            for c in range(GC):
                pg, rpg = pg_.next()
                pu, rpu = pu_.next()
                for k in range(8):
                    S.op("pe", lambda e, pg=pg, k=k, c=c: e.matmul(pg[:, :], lhsT=wgb[:, k, c * 128:(c + 1) * 128], rhs=xT[:, k, t * 512:(t + 1) * 512],
                                                                 start=(k == 0), stop=(k == 7)), reads=[rwg] + R_xT[4 * t:4 * t + 4], writes=[rpg])
                for k in range(8):
                    S.op("pe", lambda e, pu=pu, k=k, c=c: e.matmul(pu[:, :], lhsT=wub[:, k, c * 128:(c + 1) * 128], rhs=xT[:, k, t * 512:(t + 1) * 512],
                                                                 start=(k == 0), stop=(k == 7)), reads=[rwu] + R_xT[4 * t:4 * t + 4], writes=[rpu])
                sg, rsg = sgs.next()
                S.op("act", lambda e, sg=sg, pg=pg: e.activation(out=sg[:], in_=pg[:, :], func=AF.Silu), reads=[rpg], writes=[rsg])
                S.op("dve", lambda e, c=c, sg=sg, pu=pu: e.tensor_tensor(out=uT[:, c, :], in0=pu[:, :], in1=sg[:], op=ALU.mult), reads=[rpu, rsg], writes=[ruT])
            pend[it] = (uT, ruT, wdb, rwd)

        def emit_down(it):
            ex, g, t = it
            uT, ruT, wdb, rwd = pend.pop(it)
            first = (ex == 0 and g == 0)
            for tb in range(4):
                bl = t * 4 + tb
                blk = half * HB + bl
                for hf in range(2):
                    pd, rpd = pd_.next()
                    for c in range(GC):
                        S.op("pe", lambda e, pd=pd, c=c, tb=tb, hf=hf: e.matmul(pd[:, :], lhsT=uT[:, c, tb * 128:(tb + 1) * 128], rhs=wdb[:, c, hf * 512:(hf + 1) * 512],
                                                                              start=(c == 0), stop=(c == GC - 1)), reads=[ruT, rwd], writes=[rpd])
                    a_ = acc[:, bl, hf * 512:(hf + 1) * 512]
                    if moe:
                        gsc = gate_all[:, blk, ex:ex + 1]
                        if first:
                            S.op("dve", lambda e, a_=a_, pd=pd, gsc=gsc: e.tensor_scalar(out=a_, in0=pd[:, :], scalar1=gsc, scalar2=None, op0=ALU.mult), reads=[rpd, R_ga], writes=[R_acc[bl]])
                        else:
                            S.op("dve", lambda e, a_=a_, pd=pd, gsc=gsc: e.scalar_tensor_tensor(out=a_, in0=pd[:, :], scalar=gsc, in1=a_, op0=ALU.mult, op1=ALU.add),
                                 reads=[rpd, R_ga, R_acc[bl]], writes=[R_acc[bl]])
                    else:
                        if first:
                            S.op("dve", lambda e, a_=a_, pd=pd: e.tensor_copy(out=a_, in_=pd[:, :]), reads=[rpd], writes=[R_acc[bl]])
                        else:
                            S.op("dve", lambda e, a_=a_, pd=pd: e.tensor_tensor(out=a_, in0=pd[:, :], in1=a_, op=ALU.add), reads=[rpd, R_acc[bl]], writes=[R_acc[bl]])

        emit_gu(items[0])
        for ii, it in enumerate(items):
            if ii + 1 < len(items):
                emit_gu(items[ii + 1])
            if pump is not None:
                pump.pump(S, pump_n)
            emit_down(it)
        for bl in range(HB):
            f, rf = xf.next()
            r0 = tok0 + bl * 128
            S.dma("sp", lambda e, f=f, r0=r0: e.dma_start(out=f[:], in_=x_src[r0:r0 + 128, :]), writes=[rf])
            S.op("dve", lambda e, f=f, bl=bl: e.scalar_tensor_tensor(out=f[:], in0=f[:], scalar=DN_ALPHA, in1=acc[:, bl, :], op0=ALU.mult, op1=ALU.add),
                 reads=[rf, R_acc[bl]], writes=[rf])
            emit_ln(S, L, f, rf, x_dst[r0:r0 + 128, :])
    T.finish()


TS = 512
NTILE = (2 * S_LEN) // TS + N_EXP
NSLOT = NTILE * TS
SGC = 4
SNG = D_FFE // (128 * SGC)
WROW = 8 * SGC * 128
IOA = bass.IndirectOffsetOnAxis


def sp_scratch(P, layer):
    return dict(
        xsort=P.scratch("xsort", [NSLOT, D], BF16),
        ysort=P.scratch("ysort", [NSLOT, D], F32),
        wgs=P.scratch("wgs%d" % layer, [N_EXP * SNG * 128, WROW], BF16),
        wus=P.scratch("wus%d" % layer, [N_EXP * SNG * 128, WROW], BF16),
        wds=P.scratch("wds%d" % layer, [N_EXP * SNG * 128, SGC * D], BF16),
        slotA=P.scratch("slotA", [128, NB], I32),
        slotB=P.scratch("slotB", [128, NB], I32),
        gA=P.scratch("gA", [128, NB], F32),
        gB=P.scratch("gB", [128, NB], F32),
        widx=P.scratch("widx", [128, NTILE * SNG], I32),
    )


class BgPump:
    def __init__(self, P, layer):
        i = layer // 2
        D_ = sp_scratch(P, layer)
        wg_d, wu_d, wd_d = P.inp("moe_w_gate"), P.inp("moe_w_up"), P.inp("moe_w_down")
        self.jobs = []
        for ex in range(N_EXP):
            for g in range(SNG):
                ff0 = g * SGC * 128
                r0 = (ex * SNG + g) * 128
                for src, dst in ((wg_d, D_["wgs"]), (wu_d, D_["wus"])):
                    self.jobs.append(("A", src[i, ex, :, ff0:ff0 + SGC * 128].rearrange("(k p) n -> p k n", p=128), dst[r0:r0 + 128, :]))
                self.jobs.append(("D", wd_d[i, ex, ff0:ff0 + SGC * 128, :].rearrange("(c p) n -> p c n", p=128), D_["wds"][r0:r0 + 128, :]))
        self.bufs = None

    def attach(self, T, n):
        self.bufs = Rot([T.sb([128, 8 * SGC * 128], BF16, "bg") for _ in range(n)])

    def pump(self, S, n):
        for _ in range(n):
            if not self.jobs:
                return
            kind, src, dst = self.jobs.pop(0)
            b, rb = self.bufs.next()
            view = b[:].rearrange("p (k n) -> p k n", k=8) if kind == "A" else b[:].rearrange("p (c n) -> p c n", c=SGC)
            S.dma("pool", lambda e, view=view, src=src: e.dma_start(out=view, in_=src), writes=[rb])
            S.dma("sp", lambda e, b=b, dst=dst: e.dma_start(out=dst, in_=b[:]), reads=[rb])


def stageW(P, pump):
    if not pump.jobs:
        return
    T = Stage(P.nc, "w")
    pump.attach(T, 6)
    pump.pump(T.S, len(pump.jobs))
    T.finish()


def stageR(P, layer, x_src):
    nc = P.nc
    i = layer // 2
    T = Stage(nc, "r")
    S = T.S
    D_ = sp_scratch(P, layer)
    identf = T.sb([128, 128], F32, "identf")
    wrf = T.sb([128, 8, 8], F32, "wrf")
    gate_all = T.sb([128, NB, 8], F32, "gates")
    sel_all = T.sb([128, NB, 8], F32, "sel")
    xb_all = T.sb([128, NB, D], BF16, "xball")
    R_id, R_wr, R_ga, R_sel = RL(4)
    R_xb = RL(NB)
    c_ident, router = P.inp("c_ident"), P.inp("moe_router")
    S.dma("sp", lambda e: e.dma_start(out=identf[:], in_=c_ident[:, :]), writes=[R_id])
    S.dma("sp", lambda e: e.dma_start(out=wrf[:], in_=router[i].rearrange("(k p) n -> p k n", p=128)), writes=[R_wr])
    xfs = Rot([T.sb([128, D], F32, "xf") for _ in range(2)])
    xTs = Rot([T.sb([128, 8, 128], F32, "xTf") for _ in range(2)])
    pT_ = Rot([T.ps([128, 8, 128], F32, "pTf") for _ in range(2)])
    lg_ = Rot([T.ps([128, 512], F32, "lg") for _ in range(2)])
    sm = [Rot([T.sb([128, 8], F32, "sm%d" % j) for _ in range(2)]) for j in range(4)]
    sc1 = [Rot([T.sb([128, 1], F32, "sc%d" % j) for _ in range(2)]) for j in range(4)]
    for blk in range(NB):
        xf, rxf = xfs.next()
        S.dma("sp", lambda e, xf=xf, blk=blk: e.dma_start(out=xf[:], in_=x_src[blk * 128:(blk + 1) * 128, :]), writes=[rxf])
        S.op("act", lambda e, xf=xf, blk=blk: e.copy(out=xb_all[:, blk, :], in_=xf[:]), reads=[rxf], writes=[R_xb[blk]])
        pT, rpT = pT_.next()
        for k in range(8):
            S.op("pe", lambda e, pT=pT, xf=xf, k=k: e.transpose(out=pT[:, k, :], in_=xf[:, k * 128:(k + 1) * 128], identity=identf[:]), reads=[rxf, R_id], writes=[rpT])
        xT, rxT = xTs.next()
        S.op("act", lambda e, xT=xT, pT=pT: e.copy(out=xT[:, 0:4, :], in_=pT[:, 0:4, :]), reads=[rpT], writes=[rxT])
        S.op("dve", lambda e, xT=xT, pT=pT: e.tensor_copy(out=xT[:, 4:8, :], in_=pT[:, 4:8, :]), reads=[rpT], writes=[rxT])
        lg, rlg = lg_.next()
        for k in range(8):
            S.op("pe", lambda e, lg=lg, xT=xT, k=k: e.matmul(lg[:, 0:8], lhsT=xT[:, k, :], rhs=wrf[:, k, :], start=(k == 0), stop=(k == 7)), reads=[rxT, R_wr], writes=[rlg])
        (lgs, rlgs), (eq, req), (l2, rl2), (ex, rex) = [r_.next() for r_ in sm]
        (m1, rm1), (m2, rm2), (nm1, rnm1), (den, rden) = [r_.next() for r_ in sc1]
        S.op("dve", lambda e, lgs=lgs, lg=lg: e.tensor_copy(out=lgs[:], in_=lg[:, 0:8]), reads=[rlg], writes=[rlgs])
        S.op("dve", lambda e, m1=m1, lgs=lgs: e.tensor_reduce(out=m1[:], in_=lgs[:], axis=AX.X, op=ALU.max), reads=[rlgs], writes=[rm1])
        S.op("dve", lambda e, eq=eq, lgs=lgs, m1=m1: e.tensor_scalar(out=eq[:], in0=lgs[:], scalar1=m1[:, 0:1], scalar2=None, op0=ALU.is_equal), reads=[rlgs, rm1], writes=[req])
        S.op("dve", lambda e, l2=l2, eq=eq, lgs=lgs: e.scalar_tensor_tensor(out=l2[:], in0=eq[:], scalar=-1.0e30, in1=lgs[:], op0=ALU.mult, op1=ALU.add), reads=[req, rlgs], writes=[rl2])
        S.op("dve", lambda e, m2=m2, l2=l2: e.tensor_reduce(out=m2[:], in_=l2[:], axis=AX.X, op=ALU.max), reads=[rl2], writes=[rm2])
        S.op("dve", lambda e, lgs=lgs, m2=m2, blk=blk: e.tensor_scalar(out=sel_all[:, blk, :], in0=lgs[:], scalar1=m2[:, 0:1], scalar2=None, op0=ALU.is_ge), reads=[rlgs, rm2], writes=[R_sel])
        S.op("dve", lambda e, nm1=nm1, m1=m1: e.tensor_scalar(out=nm1[:], in0=m1[:], scalar1=-1.0, scalar2=None, op0=ALU.mult), reads=[rm1], writes=[rnm1])
        S.op("act", lambda e, ex=ex, lgs=lgs, nm1=nm1: e.activation(out=ex[:], in_=lgs[:], func=AF.Exp, bias=nm1[:, 0:1], scale=1.0), reads=[rlgs, rnm1], writes=[rex])
        S.op("dve", lambda e, ex=ex, blk=blk: e.tensor_tensor(out=ex[:], in0=ex[:], in1=sel_all[:, blk, :], op=ALU.mult), reads=[rex, R_sel], writes=[rex])
        S.op("dve", lambda e, den=den, ex=ex: e.tensor_reduce(out=den[:], in_=ex[:], axis=AX.X, op=ALU.add), reads=[rex], writes=[rden])
        S.op("dve", lambda e, den=den: e.reciprocal(out=den[:], in_=den[:]), reads=[rden], writes=[rden])
        S.op("dve", lambda e, ex=ex, den=den, blk=blk: e.tensor_scalar(out=gate_all[:, blk, :], in0=ex[:], scalar1=den[:, 0:1], scalar2=None, op0=ALU.mult), reads=[rex, rden], writes=[R_ga])
    NBE = NB * 8
    tri = T.sb([128, 128], BF16, "tri")
    onesb = T.sb([128, 128], BF16, "onesb")
    selb = T.sb([128, NBE], BF16, "selb")
    tot = T.sb([128, NBE], F32, "tot")
    inc = T.sb([128, NBE], F32, "inc")
    slot = T.sb([128, NBE], F32, "slot")
    tmp = T.sb([128, NBE], F32, "tmp")
    one1 = T.sb([128, 1], F32, "one1")
    ne = T.sb([128, 8], F32, "ne")
    padn = T.sb([128, 8], F32, "padn")
    send = T.sb([128, 8], F32, "send")
    sstart = T.sb([128, 8], F32, "sstart")
    sa = T.sb([128, NB], F32, "sa")
    sb_ = T.sb([128, NB], F32, "sb")
    ga = T.sb([128, NB], F32, "ga")
    gb = T.sb([128, NB], F32, "gb")
    sai = T.sb([128, NB], I32, "sai")
    sbi = T.sb([128, NB], I32, "sbi")
    tlim = T.sb([128, NTILE], F32, "tlim")
    cmp_ = T.sb([128, NTILE, 8], F32, "cmp")
    ei = T.sb([128, NTILE], F32, "ei")
    wix = T.sb([128, NTILE, SNG], F32, "wix")
    wixi = T.sb([128, NTILE, SNG], I32, "wixi")
    R_tri, R_ones, R_selb, R_tot, R_inc, R_slot, R_tmp, R_one1, R_ne, R_padn, R_send, R_ss, R_sa, R_sb, R_gab, R_sai, R_sbi, R_tlim, R_cmp, R_ei, R_wix, R_wixi = RL(22)
    rk_ps = Rot([T.ps([128, 512], F32, "rk") for _ in range(1)])
    tt_ps = Rot([T.ps([128, 512], F32, "tt") for _ in range(1)])
    c_tri, c_tlim, c_wbase = P.inp("c_tri"), P.inp("c_tlim"), P.inp("c_wbase")
    S.dma("pool", lambda e: e.dma_start(out=tri[:], in_=c_tri[:, :]), writes=[R_tri])
    S.dma("sp", lambda e: e.dma_start(out=tlim[:], in_=c_tlim[:, :]), writes=[R_tlim])
    S.dma("sp", lambda e: e.dma_start(out=wix[:], in_=c_wbase[:, :, :]), writes=[R_wix])
    S.op("pool", lambda e: e.memset(onesb[:], 1.0), writes=[R_ones])
    S.op("pool", lambda e: e.memset(one1[:], 1.0), writes=[R_one1])
    self_f = sel_all[:].rearrange("p b e -> p (b e)")
    gate_f = gate_all[:].rearrange("p b e -> p (b e)")
    S.op("dve", lambda e: e.tensor_copy(out=selb[:], in_=self_f), reads=[R_sel], writes=[R_selb])
    rk, rrk = rk_ps.next()
    tt, rtt = tt_ps.next()
    S.op("pe", lambda e: e.matmul(rk[:, 0:NBE], lhsT=tri[:], rhs=selb[:], start=True, stop=True), reads=[R_tri, R_selb], writes=[rrk])
    S.op("pe", lambda e: e.matmul(tt[:, 0:NBE], lhsT=onesb[:], rhs=selb[:], start=True, stop=True), reads=[R_ones, R_selb], writes=[rtt])
    S.op("dve", lambda e: e.tensor_copy(out=tot[:], in_=tt[:, 0:NBE]), reads=[rtt], writes=[R_tot])
    tot_v = tot[:].rearrange("p (b e) -> p e b", e=8)
    inc_v = inc[:].rearrange("p (b e) -> p e b", e=8)
    for ee in range(8):
        S.op("dve", lambda e, ee=ee: e.tensor_tensor_scan(out=inc_v[:, ee, :], data0=one1[:, 0:1].to_broadcast([128, NB]), data1=tot_v[:, ee, :], initial=0.0,
                                                          op0=ALU.mult, op1=ALU.add), reads=[R_one1, R_tot], writes=[R_inc])
    S.op("dve", lambda e: e.tensor_copy(out=ne[:], in_=inc[:, (NB - 1) * 8:NB * 8]), reads=[R_inc], writes=[R_ne])
    MAGIC = 12582912.0
    S.op("dve", lambda e: e.tensor_scalar(out=padn[:], in0=ne[:], scalar1=1.0 / TS, scalar2=(TS - 1.0) / TS - 0.5 + 1.0 / (2 * TS), op0=ALU.mult, op1=ALU.add), reads=[R_ne], writes=[R_padn])
    S.op("dve", lambda e: e.tensor_scalar(out=padn[:], in0=padn[:], scalar1=MAGIC, scalar2=None, op0=ALU.add), reads=[R_padn], writes=[R_padn])
    S.op("dve", lambda e: e.tensor_scalar(out=padn[:], in0=padn[:], scalar1=-MAGIC, scalar2=float(TS), op0=ALU.add, op1=ALU.mult), reads=[R_padn], writes=[R_padn])
    S.op("dve", lambda e: e.tensor_tensor_scan(out=send[:], data0=one1[:, 0:1].to_broadcast([128, 8]), data1=padn[:], initial=0.0, op0=ALU.mult, op1=ALU.add),
         reads=[R_one1, R_padn], writes=[R_send])
    S.op("dve", lambda e: e.tensor_tensor(out=sstart[:], in0=send[:], in1=padn[:], op=ALU.subtract), reads=[R_send, R_padn], writes=[R_ss])
    S.op("dve", lambda e: e.tensor_tensor(out=slot[:], in0=rk[:, 0:NBE], in1=inc[:], op=ALU.add), reads=[rrk, R_inc], writes=[R_slot])
    S.op("dve", lambda e: e.tensor_tensor(out=slot[:], in0=slot[:], in1=tot[:], op=ALU.subtract), reads=[R_slot, R_tot], writes=[R_slot])
    slot3 = slot[:].rearrange("p (b e) -> p b e", e=8)
    tmp3 = tmp[:].rearrange("p (b e) -> p b e", e=8)
    S.op("dve", lambda e: e.tensor_tensor(out=slot3, in0=slot3, in1=sstart[:, :].unsqueeze(1).to_broadcast([128, NB, 8]), op=ALU.add), reads=[R_slot, R_ss], writes=[R_slot])
    S.op("dve", lambda e: e.tensor_tensor(out=tmp[:], in0=slot[:], in1=self_f, op=ALU.mult), reads=[R_slot, R_sel], writes=[R_tmp])
    S.op("dve", lambda e: e.tensor_reduce(out=sb_[:], in_=tmp3, axis=AX.X, op=ALU.max), reads=[R_tmp], writes=[R_sb])
    S.op("dve", lambda e: e.scalar_tensor_tensor(out=tmp[:], in0=self_f, scalar=-1.0e6, in1=slot[:], op0=ALU.mult, op1=ALU.add), reads=[R_sel, R_slot], writes=[R_tmp])
    S.op("dve", lambda e: e.tensor_scalar(out=tmp[:], in0=tmp[:], scalar1=1.0e6, scalar2=None, op0=ALU.add), reads=[R_tmp], writes=[R_tmp])
    S.op("dve", lambda e: e.tensor_reduce(out=sa[:], in_=tmp3, axis=AX.X, op=ALU.min), reads=[R_tmp], writes=[R_sa])
    S.op("dve", lambda e: e.tensor_tensor(out=tmp3, in0=slot3, in1=sa[:, :].unsqueeze(2).to_broadcast([128, NB, 8]), op=ALU.is_equal), reads=[R_slot, R_sa], writes=[R_tmp])
    S.op("dve", lambda e: e.tensor_tensor(out=tmp[:], in0=tmp[:], in1=gate_f, op=ALU.mult), reads=[R_tmp, R_ga], writes=[R_tmp])
    S.op("dve", lambda e: e.tensor_reduce(out=ga[:], in_=tmp3, axis=AX.X, op=ALU.add), reads=[R_tmp], writes=[R_gab])
    S.op("dve", lambda e: e.tensor_reduce(out=gb[:], in_=gate_all[:], axis=AX.X, op=ALU.add), reads=[R_ga], writes=[R_gab])
    S.op("dve", lambda e: e.tensor_tensor(out=gb[:], in0=gb[:], in1=ga[:], op=ALU.subtract), reads=[R_gab], writes=[R_gab])
    S.op("dve", lambda e: e.tensor_copy(out=sai[:], in_=sa[:]), reads=[R_sa], writes=[R_sai])
    S.op("dve", lambda e: e.tensor_copy(out=sbi[:], in_=sb_[:]), reads=[R_sb], writes=[R_sbi])
    S.op("dve", lambda e: e.tensor_tensor(out=cmp_[:], in0=send[:, :].unsqueeze(1).to_broadcast([128, NTILE, 8]),
                                          in1=tlim[:, :].unsqueeze(2).to_broadcast([128, NTILE, 8]), op=ALU.is_le), reads=[R_send, R_tlim], writes=[R_cmp])
    S.op("dve", lambda e: e.tensor_reduce(out=ei[:], in_=cmp_[:], axis=AX.X, op=ALU.add), reads=[R_cmp], writes=[R_ei])
    S.op("dve", lambda e: e.tensor_scalar(out=ei[:], in0=ei[:], scalar1=7.0, scalar2=float(SNG * 128), op0=ALU.min, op1=ALU.mult), reads=[R_ei], writes=[R_ei])
    S.op("dve", lambda e: e.tensor_tensor(out=wix[:], in0=wix[:], in1=ei[:, :].unsqueeze(2).to_broadcast([128, NTILE, SNG]), op=ALU.add), reads=[R_wix, R_ei], writes=[R_wix])
    S.op("dve", lambda e: e.tensor_copy(out=wixi[:], in_=wix[:]), reads=[R_wix], writes=[R_wixi])
    S.dma("sp", lambda e: e.dma_start(out=D_["slotA"][:, :], in_=sai[:]), reads=[R_sai])
    S.dma("sp", lambda e: e.dma_start(out=D_["slotB"][:, :], in_=sbi[:]), reads=[R_sbi])
    S.dma("sp", lambda e: e.dma_start(out=D_["gA"][:, :], in_=ga[:]), reads=[R_gab])
    S.dma("sp", lambda e: e.dma_start(out=D_["gB"][:, :], in_=gb[:]), reads=[R_gab])
    S.dma("sp", lambda e: e.dma_start(out=D_["widx"][:, :], in_=wixi[:].rearrange("p a b -> p (a b)")), reads=[R_wixi])
    xsort = D_["xsort"]
    zt = T.sb([128, 8, D], BF16, "zeros")
    R_zt, R_xz = RL(2)
    S.op("pool", lambda e: e.memset(zt[:], 0.0), writes=[R_zt])
    xz = xsort.rearrange("(n p) d -> p n d", p=128)
    for n0 in range(0, NSLOT // 128, 8):
        S.dma("sp", lambda e, n0=n0: e.dma_start(out=xz[:, n0:n0 + 8, :], in_=zt[:]), reads=[R_zt], writes=[R_xz])
    for blk in range(NB):
        for which, idx_t, r_idx in (("a", sai, R_sai), ("b", sbi, R_sbi)):
            S.dma("pool", lambda e, blk=blk, idx_t=idx_t: e.indirect_dma_start(out=xsort[:, :], out_offset=IOA(ap=idx_t[:, blk:blk + 1], axis=0),
                                                                             in_=xb_all[:, blk, :], in_offset=None), reads=[R_xb[blk], r_idx, R_xz])
    T.finish()


def stageES(P, layer, pump=None, pump_every=2):
    nc = P.nc
    T = Stage(nc, "es")
    S = T.S
    D_ = sp_scratch(P, layer)
    ident, R_id = load_ident(T, P, S)
    widx = T.sb([128, NTILE * SNG], I32, "widx")
    R_widx = Res()
    S.dma("sp", lambda e: e.dma_start(out=widx[:], in_=D_["widx"][:, :]), writes=[R_widx])
    xTs = Rot([T.sb([128, 8, TS], BF16, "xT") for _ in range(2)])
    accs = Rot([T.sb([128, 4, D], F32, "acc") for _ in range(2)])
    wgs = Rot([T.sb([128, 8, SGC * 128], BF16, "wg") for _ in range(2)])
    wus = Rot([T.sb([128, 8, SGC * 128], BF16, "wu") for _ in range(2)])
    wds = Rot([T.sb([128, SGC, D], BF16, "wd") for _ in range(2)])
    uTs = Rot([T.sb([128, SGC, 512], BF16, "uT") for _ in range(3)])
    sgs = Rot([T.sb([128, 512], F32, "sg") for _ in range(2)])
    xbs = Rot([T.sb([128, D], BF16, "xb") for _ in range(3)])
    pg_ = Rot([T.ps([128, 512], F32, "pg") for _ in range(2)])
    pu_ = Rot([T.ps([128, 512], F32, "pu") for _ in range(2)])
    pd_ = Rot([T.ps([128, 512], F32, "pd") for _ in range(2)])
    pT_ = Rot([T.ps([128, 8, 128], BF16, "pT") for _ in range(2)])
    xsort, ysort = D_["xsort"], D_["ysort"]
    if pump is not None:
        pump.attach(T, 4)
    items = [(ti, g) for ti in range(NTILE) for g in range(SNG)]
    cur = {}
    pend = {}

    def emit_gu(it):
        ti, g = it
        if g == 0:
            xT, rxT = xTs.next()
            acc, racc = accs.next()
            for b4 in range(4):
                xb, rxb = xbs.next()
                p, rp = pT_.next()
                r0 = ti * TS + b4 * 128
                S.dma("sp", lambda e, xb=xb, r0=r0: e.dma_start(out=xb[:], in_=xsort[r0:r0 + 128, :]), writes=[rxb])
                for k in range(8):
                    S.op("pe", lambda e, p=p, xb=xb, k=k: e.transpose(out=p[:, k, :], in_=xb[:, k * 128:(k + 1) * 128], identity=ident[:]), reads=[rxb, R_id], writes=[rp])
                S.op("dve", lambda e, p=p, b4=b4, xT=xT: e.tensor_copy(out=xT[:, :, b4 * 128:(b4 + 1) * 128], in_=p[:]), reads=[rp], writes=[rxT])
            cur[ti] = (xT, rxT, acc, racc)
        xT, rxT, acc, racc = cur[ti]
        col = ti * SNG + g
        wgb, rwg = wgs.next()
        wub, rwu = wus.next()
        wdb, rwd = wds.next()
        S.dma("pool", lambda e: e.indirect_dma_start(out=wgb[:].rearrange("p k n -> p (k n)"), out_offset=None, in_=D_["wgs"][:, :], in_offset=IOA(ap=widx[:, col:col + 1], axis=0)),
              reads=[R_widx], writes=[rwg])
        S.dma("pool", lambda e: e.indirect_dma_start(out=wub[:].rearrange("p k n -> p (k n)"), out_offset=None, in_=D_["wus"][:, :], in_offset=IOA(ap=widx[:, col:col + 1], axis=0)),
              reads=[R_widx], writes=[rwu])
        S.dma("pool", lambda e: e.indirect_dma_start(out=wdb[:].rearrange("p c n -> p (c n)"), out_offset=None, in_=D_["wds"][:, :], in_offset=IOA(ap=widx[:, col:col + 1], axis=0)),
              reads=[R_widx], writes=[rwd])
        uT, ruT = uTs.next()
        for c in range(SGC):
            pg, rpg = pg_.next()
            pu, rpu = pu_.next()
            for k in range(8):
                S.op("pe", lambda e, pg=pg, k=k, c=c: e.matmul(pg[:, :], lhsT=wgb[:, k, c * 128:(c + 1) * 128], rhs=xT[:, k, :], start=(k == 0), stop=(k == 7)),
                     reads=[rwg, rxT], writes=[rpg])
            for k in range(8):
                S.op("pe", lambda e, pu=pu, k=k, c=c: e.matmul(pu[:, :], lhsT=wub[:, k, c * 128:(c + 1) * 128], rhs=xT[:, k, :], start=(k == 0), stop=(k == 7)),
                     reads=[rwu, rxT], writes=[rpu])
            sg, rsg = sgs.next()
            S.op("act", lambda e, sg=sg, pg=pg: e.activation(out=sg[:], in_=pg[:, :], func=AF.Silu), reads=[rpg], writes=[rsg])
            S.op("dve", lambda e, c=c, sg=sg, pu=pu: e.tensor_tensor(out=uT[:, c, :], in0=pu[:, :], in1=sg[:], op=ALU.mult), reads=[rpu, rsg], writes=[ruT])
        pend[it] = (uT, ruT, wdb, rwd, acc, racc)

    def emit_down(it):
        ti, g = it
        uT, ruT, wdb, rwd, acc, racc = pend.pop(it)
        for tb in range(4):
            for hf in range(2):
                pd, rpd = pd_.next()
                for c in range(SGC):
                    S.op("pe", lambda e, pd=pd, c=c, tb=tb, hf=hf: e.matmul(pd[:, :], lhsT=uT[:, c, tb * 128:(tb + 1) * 128], rhs=wdb[:, c, hf * 512:(hf + 1) * 512],
                                                                          start=(c == 0), stop=(c == SGC - 1)), reads=[ruT, rwd], writes=[rpd])
                a_ = acc[:, tb, hf * 512:(hf + 1) * 512]
                if g == 0:
                    S.op("dve", lambda e, a_=a_, pd=pd: e.tensor_copy(out=a_, in_=pd[:, :]), reads=[rpd], writes=[racc])
                else:
                    S.op("dve", lambda e, a_=a_, pd=pd: e.tensor_tensor(out=a_, in0=pd[:, :], in1=a_, op=ALU.add), reads=[rpd, racc], writes=[racc])
        if g == SNG - 1:
            for tb in range(4):
                r0 = ti * TS + tb * 128
                S.dma("sp", lambda e, tb=tb, r0=r0: e.dma_start(out=ysort[r0:r0 + 128, :], in_=acc[:, tb, :]), reads=[racc])

    emit_gu(items[0])
    for ii, it in enumerate(items):
        if ii + 1 < len(items):
            emit_gu(items[ii + 1])
        if pump is not None and ii % pump_every == 0:
            pump.pump(S, 1)
        emit_down(it)
    T.finish()


def stageF(P, layer, x_src, x_dst):
    nc = P.nc
    T = Stage(nc, "f")
    S = T.S
    D_ = sp_scratch(P, layer)
    L = LNBufs(T, S, P, "ln_ffn_g", "ln_ffn_b", layer)
    sai = T.sb([128, NB], I32, "sai")
    sbi = T.sb([128, NB], I32, "sbi")
    ga = T.sb([128, NB], F32, "ga")
    gb = T.sb([128, NB], F32, "gb")
    R_sai, R_sbi, R_ga, R_gb = RL(4)
    S.dma("sp", lambda e: e.dma_start(out=sai[:], in_=D_["slotA"][:, :]), writes=[R_sai])
    S.dma("sp", lambda e: e.dma_start(out=sbi[:], in_=D_["slotB"][:, :]), writes=[R_sbi])
    S.dma("sp", lambda e: e.dma_start(out=ga[:], in_=D_["gA"][:, :]), writes=[R_ga])
    S.dma("sp", lambda e: e.dma_start(out=gb[:], in_=D_["gB"][:, :]), writes=[R_gb])
    yas = Rot([T.sb([128, D], F32, "ya") for _ in range(3)])
    ybs = Rot([T.sb([128, D], F32, "yb") for _ in range(3)])
    xfs = Rot([T.sb([128, D], F32, "xf") for _ in range(3)])
    ysort = D_["ysort"]
    for blk in range(NB):
        ya, rya = yas.next()
        yb, ryb = ybs.next()
        xf, rxf = xfs.next()
        S.dma("pool", lambda e, ya=ya, blk=blk: e.indirect_dma_start(out=ya[:], out_offset=None, in_=ysort[:, :], in_offset=IOA(ap=sai[:, blk:blk + 1], axis=0)),
              reads=[R_sai], writes=[rya])
        S.dma("pool", lambda e, yb=yb, blk=blk: e.indirect_dma_start(out=yb[:], out_offset=None, in_=ysort[:, :], in_offset=IOA(ap=sbi[:, blk:blk + 1], axis=0)),
              reads=[R_sbi], writes=[ryb])
        S.dma("sp", lambda e, xf=xf, blk=blk: e.dma_start(out=xf[:], in_=x_src[blk * 128:(blk + 1) * 128, :]), writes=[rxf])
        S.op("act", lambda e, ya=ya, blk=blk: e.activation(out=ya[:], in_=ya[:], func=AF.Copy, scale=ga[:, blk:blk + 1]), reads=[rya, R_ga], writes=[rya])
        S.op("dve", lambda e, ya=ya, yb=yb, blk=blk: e.scalar_tensor_tensor(out=ya[:], in0=yb[:], scalar=gb[:, blk:blk + 1], in1=ya[:], op0=ALU.mult, op1=ALU.add),
             reads=[rya, ryb, R_gb], writes=[rya])
        S.op("dve", lambda e, xf=xf, ya=ya: e.scalar_tensor_tensor(out=xf[:], in0=xf[:], scalar=DN_ALPHA, in1=ya[:], op0=ALU.mult, op1=ALU.add), reads=[rxf, rya], writes=[rxf])
        emit_ln(S, L, xf, rxf, x_dst[blk * 128:(blk + 1) * 128, :], gb_eng="dve")
    T.finish()


def build(n_layers=DEPTH, debug_out=(), stages="0ABCDWRE", first_layer=0, x_in=None):
    P = Prog(debug_out)
    y = P.out("y", [S_LEN, D], F32)
    if "0" in stages:
        stage0(P)
    pumps = {}
    if SPARSE and "W" in stages:
        for l in range(first_layer, first_layer + n_layers):
            if l % 2 == 1:
                pumps[l] = BgPump(P, l)
    for layer in range(first_layer, first_layer + n_layers):
        x_src = P.inp("x") if layer == 0 else P.xs(1)
        x_mid = P.xs(0)
        x_dst = y if layer == DEPTH - 1 else P.xs(1)
        if "A" in stages:
            stageA(P, layer, x_src)
        if "B" in stages:
            stageB(P, layer)
        if "C" in stages:
            stageC(P, layer)
        if "D" in stages:
            stageD(P, layer, x_src, x_mid)
        moe = (layer % 2 == 1)
        if x_in is not None:
            x_mid = P.inp(x_in)
        if moe and SPARSE:
            if "W" in stages:
                stageW(P, pumps[layer])
            if "R" in stages:
                stageR(P, layer, x_mid)
            if "E" in stages:
                stageES(P, layer, pump=pumps.get(layer + 2), pump_every=2)
                stageF(P, layer, x_mid, x_dst)
        else:
            if moe and "R" in stages:
                stageE0(P, layer, x_mid)
            if "E" in stages:
                nxt = pumps.get(layer + 1) if SPARSE else None
                pn = 0
                if nxt is not None:
                    pn = -(-len(nxt.jobs) // 88)
                stageE(P, layer, x_mid, x_dst, moe, pump=nxt, pump_n=pn)
    return P


def make_in_maps(P, inputs):
    consts = host_consts()
    maps = []
    for c in range(8):
        m = {}
        for name in P.used_inputs:
            if name == "x":
                m[name] = np.ascontiguousarray(inputs["x"][c])
            elif name == "positions":
                m[name] = np.ascontiguousarray(inputs["positions"][c].reshape(1, S_LEN)).astype(np.int32)
            elif name in consts:
                m[name] = consts[name]
            else:
                m[name] = np.ascontiguousarray(inputs[name])
        maps.append(m)
    return maps


def kernel(**inputs):
    P = build()
    maps = make_in_maps(P, inputs)
    res = run_bass_kernel_spmd(P.nc, maps, core_ids=list(range(8)))
    return np.stack([np.asarray(res.results[c]["y"]) for c in range(8)], axis=0).astype(np.float32)
```

```python
import math
from contextlib import ExitStack

import numpy as np
import concourse.bass as bass
import concourse.mybir as mybir
from concourse.bass_utils import run_bass_kernel_spmd

F32 = mybir.dt.float32
BF16 = mybir.dt.bfloat16
I32 = mybir.dt.int32
AF = mybir.ActivationFunctionType
ALU = mybir.AluOpType
AX = mybir.AxisListType

S_LEN = 4096
D = 1024
NB = 32
NT = 8
DEPTH = 4
IN_COLS = 6664
D_FF = 2816
N_EXP = 8
D_FFE = 3584
DN_ALPHA = (2 * DEPTH) ** 0.25
LN_EPS = 1e-5
RMS_EPS = 1e-6
C_QA, C_KA, C_VA, C_F, C_QB, C_KB, C_VB, C_GB, C_GT = 0, 512, 1024, 1536, 1544, 2056, 2568, 3592, 4616
NEG_BIG = -30000.0
VAW = 8 * 65
NTILE_H = 24
SNG_H = 7
SPARSE = True
C_LEVEL = 4
C_NCH = NB


class Res:
    __slots__ = ("last_w", "readers")

    def __init__(self):
        self.last_w = None
        self.readers = {}


def RL(n):
    return [Res() for _ in range(n)]


class Op:
    __slots__ = ("eng", "fn", "deps", "signal", "count", "is_dma", "key")

    def __init__(self, eng, fn, is_dma=False):
        self.eng = eng
        self.fn = fn
        self.deps = []
        self.signal = False
        self.count = 0
        self.is_dma = is_dma
        self.key = None


class Sched:
    ENGS = ("pe", "act", "dve", "pool", "sp")
    ROLL = 30000

    def __init__(self, nc, n_chan=10):
        self.nc = nc
        self.q = {e: [] for e in self.ENGS}
        self.n_chan = n_chan
        self.chan_rr = {"sp": 0, "act": 0, "pool": 0}
        self.chan_last = {}

    def _track(self, op, reads, writes):
        deps = []
        for r in reads:
            if r.last_w is not None:
                deps.append(r.last_w)
        for w in writes:
            if w.last_w is not None:
                deps.append(w.last_w)
            deps.extend(w.readers.values())
        rk = op.key if op.is_dma else op.eng
        for r in reads:
            r.readers[rk] = op
        for w in writes:
            w.last_w = op
            w.readers = {}
        seen = set()
        for d in deps:
            if d is op or id(d) in seen:
                continue
            seen.add(id(d))
            if (not d.is_dma) and (not op.is_dma) and d.eng == "pe" and op.eng == "pe":
                continue
            op.deps.append(d)
            d.signal = True

    def op(self, eng, fn, reads=(), writes=()):
        o = Op(eng, fn)
        self._track(o, reads, writes)
        self.q[eng].append(o)
        return o

    def dma(self, queue, fn, reads=(), writes=()):
        o = Op(queue, fn, is_dma=True)
        ci = self.chan_rr[queue]
        self.chan_rr[queue] = (ci + 1) % self.n_chan
        o.key = ("c", queue, ci)
        prev = self.chan_last.get(o.key)
        self._track(o, reads, writes)
        if prev is not None and all(d is not prev for d in o.deps):
            o.deps.append(prev)
        self.chan_last[o.key] = o
        o.signal = True
        self.q[queue].append(o)
        return o

    def emit(self, stack, tag):
        nc = self.nc
        ccount = {}
        keys = []
        for e in self.ENGS:
            cnt = 0
            si = 0
            for o in self.q[e]:
                if o.is_dma:
                    c = ccount.get(o.key, 0) + 16
                    ccount[o.key] = c
                    o.count = c
                elif o.signal:
                    if cnt >= self.ROLL:
                        si += 1
                        cnt = 0
                    cnt += 1
                    o.count = cnt
                    o.key = ("e", e, si)
            for i in range(si + 1):
                keys.append(("e", e, i))
        keys.extend(ccount.keys())
        sems = {k: nc.alloc_semaphore(name="%s_%s_%s%d" % (tag, k[0], k[1], k[2])) for k in keys}
        bstack = ExitStack()
        block = bstack.enter_context(nc.Block())

        def run(ename, eng):
            waited = {}
            for o in self.q[ename]:
                need = {}
                for d in o.deps:
                    if d.count > need.get(d.key, 0):
                        need[d.key] = d.count
                for key, val in need.items():
                    if waited.get(key, 0) >= val:
                        continue
                    eng.wait_ge(sems[key], val)
                    waited[key] = val
                ins = o.fn(eng)
                if o.is_dma:
                    ins.then_inc(sems[o.key], 16)
                elif o.signal:
                    ins.then_inc(sems[o.key], 1)

        @block.tensor
        def _(e):
            run("pe", e)

        @block.scalar
        def _(e):
            run("act", e)

        @block.vector
        def _(e):
            run("dve", e)

        @block.gpsimd
        def _(e):
            run("pool", e)

        @block.sync
        def _(e):
            run("sp", e)
            for key, c in ccount.items():
                e.wait_ge(sems[key], c)

        bstack.close()
        nc.clear_and_free_semaphores(list(sems.values()))
        nc.all_engine_barrier()


class Stage:
    _uid = 0

    def __init__(self, nc, name):
        self.nc = nc
        self.name = name
        self.st = ExitStack()
        self.S = Sched(nc)
        Stage._uid += 1
        self.uid = Stage._uid
        self.n = 0

    def sb(self, shape, dt, nm="t"):
        self.n += 1
        return self.st.enter_context(self.nc.sbuf_tensor("%s%d_%s%d" % (self.name, self.uid, nm, self.n), list(shape), dt))

    def ps(self, shape, dt, nm="p"):
        self.n += 1
        return self.st.enter_context(self.nc.psum_tensor("%s%d_%s%d" % (self.name, self.uid, nm, self.n), list(shape), dt))

    def finish(self):
        self.S.emit(self.st, "%s%d" % (self.name, self.uid))
        self.st.close()


class Rot:
    def __init__(self, tiles):
        self.tiles = tiles
        self.res = RL(len(tiles))
        self.i = -1

    def next(self):
        self.i = (self.i + 1) % len(self.tiles)
        return self.tiles[self.i], self.res[self.i]


def host_consts():
    c = {}
    c["c_ident"] = np.eye(128, dtype=np.float32)
    r = np.arange(128)
    c["c_mq"] = np.where(r[:, None] > r[None, :], NEG_BIG, 0.0).astype(np.float32)
    half = 32
    inv_freq = (np.float32(10000.0) ** (-np.arange(half, dtype=np.float32) / np.float32(half))).astype(np.float32)
    c["c_invf"] = np.concatenate([inv_freq, inv_freq]).reshape(64, 1).astype(np.float32)
    c["c_sgn"] = np.concatenate([-np.ones(32), np.ones(32)]).reshape(64, 1).astype(np.float32)
    lg = np.log(1.0 - 2.0 ** (-5.0 - np.arange(8, dtype=np.float64)))
    j = np.arange(128, dtype=np.float64)
    dtp = np.exp(-(j[:, None, None] + 1.0) * lg[None, :, None]) * (j[:, None, None] <= j[None, None, :])
    perm = [2 * (hh % 4) + (hh // 4) for hh in range(8)]
    c["c_dtp"] = np.ascontiguousarray(dtp[:, perm, :]).astype(np.float32)
    c["c_qdec"] = np.ascontiguousarray(np.exp((j[:, None] + 1.0) * lg[None, :])[:, perm]).astype(np.float32)
    c["c_kdec"] = np.exp((127.0 - j[:, None]) * lg[None, :]).astype(np.float32)
    cd = np.zeros((128, 4, 128), dtype=np.float64)
    for two in range(2):
        for hp in range(4):
            cd[two * 64:(two + 1) * 64, hp, :] = math.exp(128.0 * lg[2 * hp + two])
    c["c_cd"] = cd.astype(np.float32)
    c["c_tri"] = (r[:, None] < r[None, :]).astype(np.float32)
    c["c_tlim"] = np.tile((np.arange(NTILE_H, dtype=np.float32) * 512.0)[None, :], (128, 1)).astype(np.float32)
    wb = np.arange(128, dtype=np.float32)[:, None, None] + (np.arange(SNG_H, dtype=np.float32) * 128.0)[None, None, :] + np.zeros((1, NTILE_H, 1), np.float32)
    c["c_wbase"] = np.ascontiguousarray(wb).astype(np.float32)
    return c


CONST_SHAPES = {"c_ident": [128, 128], "c_mq": [128, 128], "c_invf": [64, 1], "c_sgn": [64, 1],
                "c_dtp": [128, 8, 128], "c_qdec": [128, 8], "c_kdec": [128, 8], "c_cd": [128, 4, 128],
                "c_tri": [128, 128], "c_tlim": [128, NTILE_H], "c_wbase": [128, NTILE_H, SNG_H]}

INPUT_SHAPES = {
    "x": ([S_LEN, D], F32), "positions": ([1, S_LEN], I32),
    "w_in": ([DEPTH, D, IN_COLS], F32), "b_forget": ([DEPTH, 8], F32), "ret_norm_g": ([DEPTH, D], F32),
    "w_branch_fox": ([DEPTH, 512, D], F32), "w_branch_ret": ([DEPTH, D, D], F32), "w_out": ([DEPTH, D, D], F32),
    "ln_mix_g": ([DEPTH, D], F32), "ln_mix_b": ([DEPTH, D], F32),
    "ffn_w_gate": ([2, D, D_FF], F32), "ffn_w_up": ([2, D, D_FF], F32), "ffn_w_down": ([2, D_FF, D], F32),
    "moe_router": ([2, D, 8], F32), "moe_w_gate": ([2, 8, D, D_FFE], F32), "moe_w_up": ([2, 8, D, D_FFE], F32),
    "moe_w_down": ([2, 8, D_FFE, D], F32), "ln_ffn_g": ([DEPTH, D], F32), "ln_ffn_b": ([DEPTH, D], F32),
}


class Prog:
    def __init__(self, debug_out=()):
        self.nc = bass.Bass("TRN2", target_bir_lowering=False)
        self.t = {}
        self.used_inputs = []
        self.debug_out = set(debug_out)
        self.outputs = []

    def inp(self, name):
        if name not in self.t:
            if name in INPUT_SHAPES:
                shp, dt = INPUT_SHAPES[name]
            else:
                shp, dt = CONST_SHAPES[name], F32
            self.t[name] = self.nc.dram_tensor(name, list(shp), dt, kind="ExternalInput").ap()
            self.used_inputs.append(name)
        return self.t[name]

    def scratch(self, name, shape, dt):
        if name not in self.t:
            if name in self.debug_out:
                self.t[name] = self.nc.dram_tensor(name, list(shape), dt, kind="ExternalOutput").ap()
                self.outputs.append(name)
            else:
                self.t[name] = self.nc.dram_tensor(name, list(shape), dt).ap()
        return self.t[name]

    def out(self, name, shape, dt):
        if name not in self.t:
            self.t[name] = self.nc.dram_tensor(name, list(shape), dt, kind="ExternalOutput").ap()
            self.outputs.append(name)
        return self.t[name]

    def qaT(self): return self.scratch("qaT", [8, 70, S_LEN], BF16)
    def kaT(self): return self.scratch("kaT", [8, 70, S_LEN], BF16)
    def vA(self): return self.scratch("vA", [S_LEN, VAW], BF16)
    def qbT(self): return self.scratch("qbT", [512, S_LEN], BF16)
    def kbT(self): return self.scratch("kbT", [512, S_LEN], BF16)
    def vB(self): return self.scratch("vB", [S_LEN, D], BF16)
    def sgB(self): return self.scratch("sgB", [S_LEN, D], BF16)
    def gT(self): return self.scratch("gT", [2 * D, S_LEN], BF16)
    def oaT(self): return self.scratch("oaT", [512, S_LEN], BF16)
    def obT(self): return self.scratch("obT", [D, S_LEN], BF16)
    def cosT(self): return self.scratch("cosT", [64, S_LEN], F32)
    def sinT(self): return self.scratch("sinT", [64, S_LEN], F32)
    def xs(self, i): return self.scratch("xs%d" % i, [S_LEN, D], F32)


def load_ident(T, P, S):
    idt = T.sb([128, 128], BF16, "ident")
    r = Res()
    src = P.inp("c_ident")
    S.dma("pool", lambda e: e.dma_start(out=idt[:], in_=src[:, :]), writes=[r])
    return idt, r


def stage0(P):
    nc = P.nc
    T = Stage(nc, "s0")
    S = T.S
    pos = P.inp("positions")
    posi = T.sb([64, S_LEN], I32)
    ang = T.sb([64, S_LEN], F32)
    kk = T.sb([64, S_LEN], F32)
    r1 = T.sb([64, S_LEN], F32)
    r2 = T.sb([64, S_LEN], F32)
    mm = T.sb([64, S_LEN], F32)
    tb = T.sb([64, S_LEN], F32)
    invf = T.sb([64, 1], F32)
    sgn = T.sb([64, 1], F32)
    cst = T.sb([8, 3, 512], BF16)
    cstn = T.sb([8, 3, 512], BF16)
    R_posi, R_ang, R_kk, R_r1, R_r2, R_mm, R_tb, R_invf, R_sgn, R_cst = RL(10)
    c_invf, c_sgn = P.inp("c_invf"), P.inp("c_sgn")
    cosT, sinT = P.cosT(), P.sinT()
    S.dma("sp", lambda e: e.dma_start(out=posi[:], in_=pos[0:1, :].partition_broadcast(64)), writes=[R_posi])
    S.dma("sp", lambda e: e.dma_start(out=invf[:], in_=c_invf[:, :]), writes=[R_invf])
    S.dma("sp", lambda e: e.dma_start(out=sgn[:], in_=c_sgn[:, :]), writes=[R_sgn])
    TWO_PI = 2.0 * math.pi
    C1 = 6.28125
    C2 = float(np.float32(np.round((TWO_PI - C1) * 2.0 ** 20) / 2.0 ** 20))
    C3 = float(np.float32(TWO_PI - C1 - C2))
    MAGIC = 12582912.0
    PI_LO = float(np.nextafter(np.float32(math.pi), np.float32(0.0)))
    S.op("dve", lambda e: e.tensor_copy(out=ang[:], in_=posi[:]), reads=[R_posi], writes=[R_ang])
    S.op("dve", lambda e: e.tensor_scalar(out=ang[:], in0=ang[:], scalar1=invf[:, 0:1], scalar2=None, op0=ALU.mult),
         reads=[R_ang, R_invf], writes=[R_ang])
    S.op("dve", lambda e: e.tensor_scalar(out=kk[:], in0=ang[:], scalar1=1.0 / TWO_PI, scalar2=MAGIC, op0=ALU.mult, op1=ALU.add),
         reads=[R_ang], writes=[R_kk])
    S.op("dve", lambda e: e.tensor_scalar(out=kk[:], in0=kk[:], scalar1=-MAGIC, scalar2=None, op0=ALU.add),
         reads=[R_kk], writes=[R_kk])
    S.op("dve", lambda e: e.scalar_tensor_tensor(out=r1[:], in0=kk[:], scalar=-C1, in1=ang[:], op0=ALU.mult, op1=ALU.add),
         reads=[R_kk, R_ang], writes=[R_r1])
    S.op("dve", lambda e: e.scalar_tensor_tensor(out=r1[:], in0=kk[:], scalar=-C2, in1=r1[:], op0=ALU.mult, op1=ALU.add),
         reads=[R_kk, R_r1], writes=[R_r1])
    S.op("dve", lambda e: e.scalar_tensor_tensor(out=r1[:], in0=kk[:], scalar=-C3, in1=r1[:], op0=ALU.mult, op1=ALU.add),
         reads=[R_kk, R_r1], writes=[R_r1])
    S.op("dve", lambda e: e.tensor_scalar(out=r2[:], in0=r1[:], scalar1=0.5 * math.pi, scalar2=None, op0=ALU.add),
         reads=[R_r1], writes=[R_r2])
    S.op("dve", lambda e: e.tensor_scalar(out=mm[:], in0=r2[:], scalar1=math.pi, scalar2=None, op0=ALU.is_gt),
         reads=[R_r2], writes=[R_mm])
    S.op("dve", lambda e: e.scalar_tensor_tensor(out=r2[:], in0=mm[:], scalar=-TWO_PI, in1=r2[:], op0=ALU.mult, op1=ALU.add),
         reads=[R_mm, R_r2], writes=[R_r2])
    S.op("dve", lambda e: e.tensor_scalar(out=r1[:], in0=r1[:], scalar1=PI_LO, scalar2=-PI_LO, op0=ALU.min, op1=ALU.max),
         reads=[R_r1], writes=[R_r1])
    S.op("dve", lambda e: e.tensor_scalar(out=r2[:], in0=r2[:], scalar1=PI_LO, scalar2=-PI_LO, op0=ALU.min, op1=ALU.max),
         reads=[R_r2], writes=[R_r2])
    S.op("act", lambda e: e.activation(out=tb[:], in_=r1[:], func=AF.Sin, scale=sgn[:, 0:1]), reads=[R_r1, R_sgn], writes=[R_tb])
    S.dma("sp", lambda e: e.dma_start(out=sinT[:, :], in_=tb[:]), reads=[R_tb])
    S.op("act", lambda e: e.activation(out=mm[:], in_=r2[:], func=AF.Sin), reads=[R_r2], writes=[R_mm])
    S.dma("sp", lambda e: e.dma_start(out=cosT[:, :], in_=mm[:]), reads=[R_mm])
    S.op("pool", lambda e: e.memset(cst[:], 1.0), writes=[R_cst])
    S.op("pool", lambda e: e.memset(cstn[:], -1.0), writes=[R_cst])
    qaT, kaT = P.qaT(), P.kaT()
    for t in range(NT):
        S.dma("sp", lambda e, t=t: e.dma_start(out=qaT[:, 67:70, t * 512:(t + 1) * 512], in_=cst[:]), reads=[R_cst])
        S.dma("sp", lambda e, t=t: e.dma_start(out=kaT[:, 64:67, t * 512:(t + 1) * 512], in_=cstn[:]), reads=[R_cst])
    T.finish()


def emit_build_xT(T, S, src, tok0, nblk, xT, R_xT, ident, R_id, xTf=None, R_xTf=None, identf=None):
    xf = Rot([T.sb([128, D], F32, "xf") for _ in range(2)])
    xb = Rot([T.sb([128, D], BF16, "xb") for _ in range(2)])
    pT = Rot([T.ps([128, 8, 128], BF16, "pT") for _ in range(2)])
    for blk in range(nblk):
        f, rf = xf.next()
        b, rb = xb.next()
        p, rp = pT.next()
        r0 = tok0 + blk * 128
        S.dma("sp", lambda e, f=f, r0=r0: e.dma_start(out=f[:], in_=src[r0:r0 + 128, :]), writes=[rf])
        S.op("act", lambda e, f=f, b=b: e.copy(out=b[:], in_=f[:]), reads=[rf], writes=[rb])
        for k in range(8):
            S.op("pe", lambda e, p=p, b=b, k=k: e.transpose(out=p[:, k, :], in_=b[:, k * 128:(k + 1) * 128], identity=ident[:]),
                 reads=[rb, R_id], writes=[rp])
        S.op("dve", lambda e, p=p, blk=blk: e.tensor_copy(out=xT[:, :, blk * 128:(blk + 1) * 128], in_=p[:]),
             reads=[rp], writes=[R_xT[blk]])


def stageA(P, layer, x_src):
    nc = P.nc
    w_in = P.inp("w_in")
    OUT = ExitStack()
    T = Stage(nc, "a1")
    S = T.S
    xT = OUT.enter_context(nc.sbuf_tensor("xT_l%d" % layer, [128, 8, S_LEN], BF16))
    R_xT = RL(NB)
    ident, R_id = load_ident(T, P, S)
    emit_build_xT(T, S, x_src, 0, NB, xT, R_xT, ident, R_id)
    wsl = Rot([T.sb([128, 8, 512], BF16, "w") for _ in range(3)])
    stg = Rot([T.sb([128, S_LEN], BF16, "stg") for _ in range(2)])
    pp = Rot([T.ps([128, 512], F32, "pp") for _ in range(4)])

    def load_w(c0, width=512):
        w, rw = wsl.next()
        S.dma("pool", lambda e: e.dma_start(out=w[:, :, 0:width], in_=w_in[layer, :, c0:c0 + width].rearrange("(k p) n -> p k n", p=128)),
              writes=[rw])
        return w, rw

    def fm_project(w, rw, col, M, evac):
        for t in range(NT):
            p, rp = pp.next()
            for k in range(8):
                S.op("pe", lambda e, p=p, k=k, t=t: e.matmul(p[0:M, :], lhsT=w[:, k, col:col + M], rhs=xT[:, k, t * 512:(t + 1) * 512],
                                                            start=(k == 0), stop=(k == 7)),
                     reads=[rw] + R_xT[4 * t:4 * t + 4], writes=[rp])
            evac(p, rp, t)

    qaT, kaT, gT = P.qaT(), P.kaT(), P.gT()
    for which, c0, dst in (("q", C_QA, qaT), ("k", C_KA, kaT)):
        w, rw = load_w(c0)
        for hp in range(4):
            sg, rsg = stg.next()

            def evac(p, rp, t, sg=sg, rsg=rsg, which=which):
                if which == "q":
                    S.op("act", lambda e: e.mul(out=sg[:, t * 512:(t + 1) * 512], in_=p[:, :], mul=0.125), reads=[rp], writes=[rsg])
                else:
                    S.op("dve", lambda e: e.tensor_copy(out=sg[:, t * 512:(t + 1) * 512], in_=p[:, :]), reads=[rp], writes=[rsg])
            fm_project(w, rw, hp * 128, 128, evac)
            S.dma("sp", lambda e, sg=sg, hp=hp, dst=dst: e.dma_start(out=dst[2 * hp, 0:64, :], in_=sg[0:64, :]), reads=[rsg])
            S.dma("sp", lambda e, sg=sg, hp=hp, dst=dst: e.dma_start(out=dst[2 * hp + 1, 0:64, :], in_=sg[64:128, :]), reads=[rsg])
    for gq in range(4):
        w, rw = load_w(C_GT + gq * 512)
        for mc in range(4):
            sg, rsg = stg.next()

            def evac(p, rp, t, sg=sg, rsg=rsg):
                S.op("act", lambda e: e.activation(out=sg[:, t * 512:(t + 1) * 512], in_=p[:, :], func=AF.Sigmoid), reads=[rp], writes=[rsg])
            fm_project(w, rw, mc * 128, 128, evac)
            row0 = gq * 512 + mc * 128
            S.dma("sp", lambda e, sg=sg, row0=row0: e.dma_start(out=gT[row0:row0 + 128, :], in_=sg[:, :]), reads=[rsg])
    Fb = T.sb([8, S_LEN], F32, "Fb")
    Cb = T.sb([8, S_LEN], F32, "Cb")
    CS = T.sb([8, 3, S_LEN], BF16, "CS")
    nb = T.sb([8, 1], F32, "nb")
    one8 = T.sb([8, 1], F32, "one8")
    R_Fb, R_Cb, R_CS, R_nb, R_one = RL(5)
    b_forget = P.inp("b_forget")
    S.dma("sp", lambda e: e.dma_start(out=nb[:], in_=b_forget[layer:layer + 1, :].rearrange("o h -> h o")), writes=[R_nb])
    S.op("dve", lambda e: e.tensor_scalar(out=nb[:], in0=nb[:], scalar1=-1.0, scalar2=None, op0=ALU.mult), reads=[R_nb], writes=[R_nb])
    S.op("dve", lambda e: e.memset(one8[:], 1.0), writes=[R_one])
    w, rw = load_w(C_F, 8)

    def evac_f(p, rp, t):
        S.op("act", lambda e: e.activation(out=Fb[:, t * 512:(t + 1) * 512], in_=p[0:8, :], func=AF.Exp, bias=nb[:, 0:1], scale=-1.0),
             reads=[rp, R_nb], writes=[R_Fb])
    fm_project(w, rw, 0, 8, evac_f)
    S.op("act", lambda e: e.activation(out=Fb[:], in_=Fb[:], func=AF.Ln, bias=1.0), reads=[R_Fb], writes=[R_Fb])
    S.op("dve", lambda e: e.tensor_tensor_scan(out=Cb[:], data0=one8[:, 0:1].to_broadcast([8, S_LEN]), data1=Fb[:], initial=0.0,
                                               op0=ALU.mult, op1=ALU.add), reads=[R_one, R_Fb], writes=[R_Cb])
    S.op("dve", lambda e: e.tensor_copy(out=CS[:, 0, :], in_=Cb[:]), reads=[R_Cb], writes=[R_CS])
    S.op("dve", lambda e: e.tensor_tensor(out=Fb[:], in0=Cb[:], in1=CS[:, 0, :], op=ALU.subtract), reads=[R_Cb, R_CS], writes=[R_Fb])
    S.op("dve", lambda e: e.tensor_copy(out=CS[:, 1, :], in_=Fb[:]), reads=[R_Fb], writes=[R_CS])
    S.op("dve", lambda e: e.tensor_tensor(out=Cb[:], in0=Fb[:], in1=CS[:, 1, :], op=ALU.subtract), reads=[R_Fb, R_CS], writes=[R_Cb])
    S.op("dve", lambda e: e.tensor_copy(out=CS[:, 2, :], in_=Cb[:]), reads=[R_Cb], writes=[R_CS])
    S.dma("sp", lambda e: e.dma_start(out=qaT[:, 64:67, :], in_=CS[:]), reads=[R_CS])
    S.dma("sp", lambda e: e.dma_start(out=kaT[:, 67:70, :], in_=CS[:]), reads=[R_CS])
    T.finish()

    T = Stage(nc, "a2")
    S = T.S
    for r in R_xT:
        r.last_w = None
        r.readers = {}
    cosb = T.sb([128, S_LEN], F32, "cos")
    sinb = T.sb([128, S_LEN], F32, "sin")
    R_cos, R_sin = RL(2)
    cosT, sinT = P.cosT(), P.sinT()
    for two in range(2):
        S.dma("sp", lambda e, two=two: e.dma_start(out=cosb[two * 64:(two + 1) * 64, :], in_=cosT[:, :]), writes=[R_cos])
        S.dma("sp", lambda e, two=two: e.dma_start(out=sinb[two * 64:(two + 1) * 64, :], in_=sinT[:, :]), writes=[R_sin])
    wsl = Rot([T.sb([128, 8, 512], BF16, "w") for _ in range(2)])
    wsw = Rot([T.sb([128, 8, 512], BF16, "wsw") for _ in range(2)])
    stg = Rot([T.sb([128, S_LEN], BF16, "stg") for _ in range(2)])
    t1s = Rot([T.sb([128, 512], F32, "t1") for _ in range(3)])
    t2s = Rot([T.sb([128, 512], F32, "t2") for _ in range(3)])
    ppa = Rot([T.ps([128, 512], F32, "ppa") for _ in range(3)])
    ppb = Rot([T.ps([128, 512], F32, "ppb") for _ in range(3)])
    qbT, kbT = P.qbT(), P.kbT()
    for which, c0, dst, scl in (("q", C_QB, qbT, 1.0), ("k", C_KB, kbT, 0.125)):
        w, rw = wsl.next()
        ws, rws = wsw.next()
        S.dma("pool", lambda e, w=w, c0=c0: e.dma_start(out=w[:], in_=w_in[layer, :, c0:c0 + 512].rearrange("(k p) n -> p k n", p=128)), writes=[rw])
        wv = w[:].rearrange("p k (h two j) -> p k h two j", two=2, j=32)
        wsv = ws[:].rearrange("p k (h two j) -> p k h two j", two=2, j=32)
        for k in range(8):
            S.op("pool", lambda e, k=k, wv=wv, wsv=wsv: e.tensor_copy(out=wsv[:, k, :, 0, :], in_=wv[:, k, :, 1, :]), reads=[rw], writes=[rws])
            S.op("pool", lambda e, k=k, wv=wv, wsv=wsv: e.tensor_copy(out=wsv[:, k, :, 1, :], in_=wv[:, k, :, 0, :]), reads=[rw], writes=[rws])
        for hp in range(4):
            sg, rsg = stg.next()
            for t in range(NT):
                pa, rpa = ppa.next()
                pb, rpb = ppb.next()
                for k in range(8):
                    S.op("pe", lambda e, pa=pa, k=k, t=t, w=w, hp=hp: e.matmul(pa[:, :], lhsT=w[:, k, hp * 128:(hp + 1) * 128], rhs=xT[:, k, t * 512:(t + 1) * 512],
                                                                               start=(k == 0), stop=(k == 7)),
                         reads=[rw] + R_xT[4 * t:4 * t + 4], writes=[rpa])
                for k in range(8):
                    S.op("pe", lambda e, pb=pb, k=k, t=t, ws=ws, hp=hp: e.matmul(pb[:, :], lhsT=ws[:, k, hp * 128:(hp + 1) * 128], rhs=xT[:, k, t * 512:(t + 1) * 512],
                                                                                 start=(k == 0), stop=(k == 7)),
                         reads=[rws] + R_xT[4 * t:4 * t + 4], writes=[rpb])
                t1, rt1 = t1s.next()
                t2, rt2 = t2s.next()
                S.op("dve", lambda e, t1=t1, pa=pa, t=t, scl=scl: e.scalar_tensor_tensor(out=t1[:], in0=pa[:, :], scalar=scl, in1=cosb[:, t * 512:(t + 1) * 512],
                                                                                         op0=ALU.mult, op1=ALU.mult), reads=[rpa, R_cos], writes=[rt1])
                S.op("dve", lambda e, t2=t2, pb=pb, t=t, scl=scl: e.scalar_tensor_tensor(out=t2[:], in0=pb[:, :], scalar=scl, in1=sinb[:, t * 512:(t + 1) * 512],
                                                                                         op0=ALU.mult, op1=ALU.mult), reads=[rpb, R_sin], writes=[rt2])
                S.op("pool", lambda e, t1=t1, t2=t2, sg=sg, t=t: e.tensor_tensor(out=sg[:, t * 512:(t + 1) * 512], in0=t1[:], in1=t2[:], op=ALU.add),
                     reads=[rt1, rt2], writes=[rsg])
            S.dma("sp", lambda e, sg=sg, hp=hp, dst=dst: e.dma_start(out=dst[hp * 128:(hp + 1) * 128, :], in_=sg[:, :]), reads=[rsg])
    T.finish()

    T = Stage(nc, "a3")
    S = T.S
    for r in R_xT:
        r.last_w = None
        r.readers = {}
    wts = []
    for c0 in (C_VA, C_VB, C_VB + 512, C_GB, C_GB + 512):
        w = T.sb([128, 8, 512], BF16, "w")
        rw = Res()
        S.dma("pool", lambda e, w=w, c0=c0: e.dma_start(out=w[:], in_=w_in[layer, :, c0:c0 + 512].rearrange("(k p) n -> p k n", p=128)), writes=[rw])
        wts.append((w, rw))
    vas = Rot([T.sb([128, 8, 65], BF16, "vas") for _ in range(2)])
    vbs = Rot([T.sb([128, D], BF16, "vbs") for _ in range(2)])
    sgs = Rot([T.sb([128, D], BF16, "sgs") for _ in range(2)])
    pp = Rot([T.ps([128, 512], F32, "pp") for _ in range(6)])
    for tl, rs in zip(vas.tiles, vas.res):
        S.op("pool", lambda e, tl=tl: e.memset(tl[:], 1.0), writes=[rs])
    vA, vB, sgB = P.vA(), P.vB(), P.sgB()
    for blk in range(NB):
        va, rva = vas.next()
        vb, rvb = vbs.next()
        sg, rsg = sgs.next()
        for gi, (w, rw) in enumerate(wts):
            p, rp = pp.next()
            for k in range(8):
                S.op("pe", lambda e, p=p, k=k, w=w, blk=blk: e.matmul(p[:, :], lhsT=xT[:, k, blk * 128:(blk + 1) * 128], rhs=w[:, k, :],
                                                                     start=(k == 0), stop=(k == 7)),
                     reads=[rw, R_xT[blk]], writes=[rp])
            if gi == 0:
                S.op("dve", lambda e, p=p, va=va: e.tensor_copy(out=va[:, :, 0:64], in_=p[:, :].rearrange("p (h d) -> p h d", d=64)), reads=[rp], writes=[rva])
            elif gi in (1, 2):
                o = (gi - 1) * 512
                S.op("dve", lambda e, p=p, vb=vb, o=o: e.tensor_copy(out=vb[:, o:o + 512], in_=p[:, :]), reads=[rp], writes=[rvb])
            else:
                o = (gi - 3) * 512
                S.op("act", lambda e, p=p, sg=sg, o=o: e.activation(out=sg[:, o:o + 512], in_=p[:, :], func=AF.Silu), reads=[rp], writes=[rsg])
        r0 = blk * 128
        S.dma("sp", lambda e, va=va, r0=r0: e.dma_start(out=vA[r0:r0 + 128, :], in_=va[:].rearrange("p h d -> p (h d)")), reads=[rva])
        S.dma("sp", lambda e, vb=vb, r0=r0: e.dma_start(out=vB[r0:r0 + 128, :], in_=vb[:]), reads=[rvb])
        S.dma("sp", lambda e, sg=sg, r0=r0: e.dma_start(out=sgB[r0:r0 + 128, :], in_=sg[:]), reads=[rsg])
    T.finish()
    OUT.close()


def stageB(P, layer, pump=None, pump_every=14):
    nc = P.nc
    T = Stage(nc, "b")
    S = T.S
    qaT, kaT, vA, oaT = P.qaT(), P.kaT(), P.vA(), P.oaT()
    identb, R_id = load_ident(T, P, S)
    mqb = T.sb([128, 128], BF16, "mq")
    R_mq = Res()
    c_mq = P.inp("c_mq")
    S.dma("pool", lambda e: e.dma_start(out=mqb[:], in_=c_mq[:, :]), writes=[R_mq])
    onesf = T.sb([1, 64], F32, "ones")
    R_ones = Res()
    S.op("dve", lambda e: e.memset(onesf[:], 1.0), writes=[R_ones])
    vt = T.sb([128, NB, VAW], BF16, "vt")
    R_vt = RL(4)
    vsrc = vA.rearrange("(b p) c -> p b c", p=128)
    for g in range(4):
        S.dma("sp", lambda e, g=g: e.dma_start(out=vt[:, g * 8:(g + 1) * 8, :], in_=vsrc[:, g * 8:(g + 1) * 8, :]), writes=[R_vt[g]])
    qs = Rot([T.sb([70, S_LEN], BF16, "q") for _ in range(2)])
    ks = Rot([T.sb([70, S_LEN], BF16, "k") for _ in range(2)])
    oas = Rot([T.sb([64, S_LEN], BF16, "oa") for _ in range(2)])
    pts = Rot([T.sb([128, 512], BF16, "pt") for _ in range(5)])
    rls = Rot([T.sb([1, 512], F32, "rl") for _ in range(2)])
    bcs = Rot([T.sb([64, 512], F32, "bcs") for _ in range(2)])
    s_ps = Rot([T.ps([128, 512], F32, "s") for _ in range(4)])
    o_ps = Rot([T.ps([128, 512], F32, "o") for _ in range(2)])
    bc_ps = Rot([T.ps([128, 512], F32, "bc") for _ in range(1)])

    heads = {}

    def load_head(h):
        q, rq = qs.next()
        k, rk = ks.next()
        S.dma("sp", lambda e: e.dma_start(out=q[:], in_=qaT[h, :, :]), writes=[rq])
        S.dma("sp", lambda e: e.dma_start(out=k[:], in_=kaT[h, :, :]), writes=[rk])
        heads[h] = (q, rq, k, rk)

    units = [(h, I, j) for h in range(8) for I in range(NT) for j in range(4 * I + 4)]
    st = {}
    tiles = {}

    def emit_qk(u):
        h, I, j = u
        if I == 0 and j == 0:
            if h == 0:
                load_head(0)
            if h + 1 < 8:
                load_head(h + 1)
        q, rq, k, rk = heads[h]
        m = j - 4 * I
        c0 = 128 * m if m > 0 else 0
        sp, rsp = s_ps.next()
        S.op("pe", lambda e: e.matmul(sp[:, c0:512], lhsT=k[0:70, j * 128:(j + 1) * 128], rhs=q[0:70, I * 512 + c0:(I + 1) * 512],
                                      start=True, stop=(m < 0)), reads=[rq, rk], writes=[rsp])
        if m >= 0:
            S.op("pe", lambda e: e.matmul(sp[:, c0:c0 + 128], lhsT=identb[:, :], rhs=mqb[:, :], start=False, stop=True),
                 reads=[R_id, R_mq], writes=[rsp])
        st[u] = (sp, rsp, c0)

    def emit_pv(u):
        h, I, j = u
        sp, rsp, c0 = st.pop(u)
        nkb = 4 * I + 4
        if j == 0:
            tiles[(h, I)] = o_ps.next()
            if I == 0:
                tiles[("oa", h)] = oas.next()
        op_, rop = tiles[(h, I)]
        pt, rpt = pts.next()
        S.op("act", lambda e: e.activation(out=pt[:, c0:512], in_=sp[:, c0:512], func=AF.Exp), reads=[rsp], writes=[rpt])
        S.op("pe", lambda e: e.matmul(op_[0:65, c0:512], lhsT=vt[:, j, h * 65:(h + 1) * 65], rhs=pt[:, c0:512],
                                      start=(j == 0), stop=(j == nkb - 1)), reads=[rpt, R_vt[j // 8]], writes=[rop])
        if j == nkb - 1:
            oa, roa = tiles[("oa", h)]
            rl, rrl = rls.next()
            bc, rbc = bcs.next()
            bp, rbp = bc_ps.next()
            S.op("dve", lambda e: e.reciprocal(out=rl[0:1, :], in_=op_[64:65, :]), reads=[rop], writes=[rrl])
            S.op("pe", lambda e: e.matmul(bp[0:64, :], lhsT=onesf[0:1, 0:64], rhs=rl[0:1, :], start=True, stop=True),
                 reads=[R_ones, rrl], writes=[rbp])
            S.op("act", lambda e: e.copy(out=bc[:, :], in_=bp[0:64, :]), reads=[rbp], writes=[rbc])
            S.op("dve", lambda e: e.tensor_tensor(out=oa[:, I * 512:(I + 1) * 512], in0=op_[0:64, :], in1=bc[:, :], op=ALU.mult),
                 reads=[rop, rbc], writes=[roa])
            del tiles[(h, I)]
            if I == NT - 1:
                S.dma("sp", lambda e: e.dma_start(out=oaT[h * 64:(h + 1) * 64, :], in_=oa[:, :]), reads=[roa])

    if pump is not None:
        pump.attach(T, 4)
    LOOK = 3
    for i in range(min(LOOK, len(units))):
        emit_qk(units[i])
    for i, u in enumerate(units):
        if i + LOOK < len(units):
            emit_qk(units[i + LOOK])
        if pump is not None and i % pump_every == 0:
            pump.pump(S, 1)
        emit_pv(u)
    T.finish()


def stageC(P, layer):
    nc = P.nc
    T = Stage(nc, "c")
    S = T.S
    qbT, kbT, vB, sgB, obT = P.qbT(), P.kbT(), P.vB(), P.sgB(), P.obT()
    identb, R_id = load_ident(T, P, S)
    dtp = T.sb([128, 8, 128], F32, "dtp")
    qdec = T.sb([128, 8], F32, "qdec")
    kdec = T.sb([128, 8], F32, "kdec")
    cd = T.sb([128, 4, 128], F32, "cd")
    gn = T.sb([128, D], F32, "gn")
    R_dtp, R_qdec, R_kdec, R_cd, R_gn = RL(5)
    c_dtp, c_qdec, c_kdec, c_cd, rng = P.inp("c_dtp"), P.inp("c_qdec"), P.inp("c_kdec"), P.inp("c_cd"), P.inp("ret_norm_g")
    S.dma("sp", lambda e: e.dma_start(out=dtp[:], in_=c_dtp[:, :, :]), writes=[R_dtp])
    S.dma("sp", lambda e: e.dma_start(out=qdec[:], in_=c_qdec[:, :]), writes=[R_qdec])
    S.dma("sp", lambda e: e.dma_start(out=kdec[:], in_=c_kdec[:, :]), writes=[R_kdec])
    S.dma("sp", lambda e: e.dma_start(out=cd[:], in_=c_cd[:, :, :]), writes=[R_cd])
    S.dma("sp", lambda e: e.dma_start(out=gn[:], in_=rng[layer:layer + 1, :].partition_broadcast(128)), writes=[R_gn])
    state = T.sb([128, 4, 128], F32, "state")
    state_bf = T.sb([128, 4, 128], BF16, "statebf")
    R_state, R_sbf = RL(2)
    S.op("pool", lambda e: e.memset(state[:], 0.0), writes=[R_state])
    S.op("pool", lambda e: e.memset(state_bf[:], 0.0), writes=[R_sbf])
    q4s = Rot([T.sb([128, 4, 512], BF16, "q4") for _ in range(2)])
    k4s = Rot([T.sb([128, 4, 512], BF16, "k4") for _ in range(2)])
    vs = Rot([T.sb([128, D], BF16, "v") for _ in range(3)])
    sgs = Rot([T.sb([128, D], BF16, "sg") for _ in range(2)])
    sggs = Rot([T.sb([128, D], F32, "sgg") for _ in range(3)])
    sTs = Rot([T.sb([128, 8, 128], BF16, "sT") for _ in range(2)])
    kss = Rot([T.sb([128, 8, 64], BF16, "ks") for _ in range(2)])
    ys = Rot([T.sb([128, 8, 128], F32, "y") for _ in range(2)])
    ysq = Rot([T.sb([128, 8, 128], F32, "ysq") for _ in range(1)])
    sss = Rot([T.sb([128, 8], F32, "ss") for _ in range(2)])
    obs = Rot([T.sb([128, D], BF16, "ob") for _ in range(2)])
    obt = Rot([T.sb([128, 8, 512], BF16, "obt") for _ in range(2)])
    sc_ps = Rot([T.ps([128, 8, 128], F32, "sc") for _ in range(1)])
    kt_ps = Rot([T.ps([128, 8, 128], BF16, "kt") for _ in range(1)])
    kv_ps = Rot([T.ps([128, 4, 256], F32, "kv") for _ in range(1)])
    y_ps = Rot([T.ps([128, 8, 128], F32, "yp") for _ in range(1)])
    ot_ps = Rot([T.ps([128, 8, 128], BF16, "ot") for _ in range(1)])
    qsrc = qbT.rearrange("(hp q) n -> q hp n", q=128)
    ksrc = kbT.rearrange("(hp q) n -> q hp n", q=128)
    ctx = {}

    def front(n):
        cc = n % 4
        if cc == 0:
            t = n // 4
            q4, rq4 = q4s.next()
            k4, rk4 = k4s.next()
            S.dma("sp", lambda e: e.dma_start(out=q4[:], in_=qsrc[:, :, t * 512:(t + 1) * 512]), writes=[rq4])
            S.dma("sp", lambda e: e.dma_start(out=k4[:], in_=ksrc[:, :, t * 512:(t + 1) * 512]), writes=[rk4])
            ctx["q4"] = (q4, rq4, k4, rk4)
        q4, rq4, k4, rk4 = ctx["q4"]
        v, rv = vs.next()
        sg, rsg = sgs.next()
        S.dma("sp", lambda e: e.dma_start(out=v[:], in_=vB[n * 128:(n + 1) * 128, :]), writes=[rv])
        S.dma("sp", lambda e: e.dma_start(out=sg[:], in_=sgB[n * 128:(n + 1) * 128, :]), writes=[rsg])
        sgg, rsgg = sggs.next()
        S.op("pool", lambda e: e.tensor_tensor(out=sgg[:], in0=sg[:], in1=gn[:], op=ALU.mult), reads=[rsg, R_gn], writes=[rsgg])
        sc, rsc = sc_ps.next()
        cs = slice(cc * 128, (cc + 1) * 128)
        if C_LEVEL == 0:
            return
        for h in range(8):
            hp, two = h // 2, h % 2
            S.op("pe", lambda e, h=h, hp=hp, two=two: e.matmul(sc[:, two * 4 + hp, :], lhsT=k4[two * 64:(two + 1) * 64, hp, cs], rhs=q4[two * 64:(two + 1) * 64, hp, cs],
                                                               start=True, stop=True), reads=[rq4, rk4], writes=[rsc])
        sT, rsT = sTs.next()
        S.op("dve", lambda e: e.tensor_tensor(out=sT[:], in0=sc[:], in1=dtp[:], op=ALU.mult), reads=[rsc, R_dtp], writes=[rsT])
        kt, rkt = kt_ps.next()
        for hp in range(4):
            S.op("pe", lambda e, hp=hp: e.transpose(out=kt[:, hp, :], in_=k4[:, hp, cs], identity=identb[:]), reads=[rk4, R_id], writes=[rkt])
        ks_, rks = kss.next()
        S.op("dve", lambda e: e.tensor_tensor(out=ks_[:], in0=kt[:, 0:4, :].rearrange("p a (two d) -> p (a two) d", two=2),
                                              in1=kdec[:, :].unsqueeze(2).to_broadcast([128, 8, 64]), op=ALU.mult),
             reads=[rkt, R_kdec], writes=[rks])
        kv, rkv = kv_ps.next()
        for hp in range(4):
            S.op("pe", lambda e, hp=hp: e.matmul(kv[:, hp, :], lhsT=ks_[:, 2 * hp:2 * hp + 2, :].rearrange("p a d -> p (a d)"),
                                                 rhs=v[:, hp * 256:(hp + 1) * 256], start=True, stop=True), reads=[rks, rv], writes=[rkv])
        ctx[n] = dict(q4=q4, rq4=rq4, v=v, rv=rv, sg=sgg, rsg=rsgg, sT=sT, rsT=rsT, kv=kv, rkv=rkv, cs=cs, cc=cc)

    def su(n):
        c = ctx[n]
        kv, rkv = c["kv"], c["rkv"]
        S.op("pool", lambda e: e.tensor_tensor(out=state[:], in0=state[:], in1=cd[:], op=ALU.mult), reads=[R_state, R_cd], writes=[R_state])
        for two in range(2):
            ps_ = slice(two * 64, (two + 1) * 64)
            S.op("dve", lambda e, ps_=ps_, two=two: e.tensor_tensor(out=state[ps_, :, :], in0=state[ps_, :, :], in1=kv[ps_, :, two * 128:(two + 1) * 128], op=ALU.add),
                 reads=[R_state, rkv], writes=[R_state])

    def sbf_copy():
        S.op("act", lambda e: e.copy(out=state_bf[:], in_=state[:]), reads=[R_state], writes=[R_sbf])

    def mid(n):
        c = ctx[n]
        q4, rq4, v, rv, sT, rsT, cs = c["q4"], c["rq4"], c["v"], c["rv"], c["sT"], c["rsT"], c["cs"]
        yp, ryp = y_ps.next()
        for h in range(8):
            hp, two = h // 2, h % 2
            S.op("pe", lambda e, h=h, hp=hp, two=two: e.matmul(yp[:, two * 4 + hp, :], lhsT=sT[:, two * 4 + hp, :], rhs=v[:, h * 128:(h + 1) * 128], start=True, stop=False),
                 reads=[rsT, rv], writes=[ryp])
            S.op("pe", lambda e, h=h, hp=hp, two=two: e.matmul(yp[:, two * 4 + hp, :], lhsT=q4[two * 64:(two + 1) * 64, hp, cs],
                                                               rhs=state_bf[two * 64:(two + 1) * 64, hp, :], start=False, stop=True),
                 reads=[rq4, R_sbf], writes=[ryp])
        y, ry = ys.next()
        S.op("dve", lambda e: e.tensor_tensor(out=y[:].rearrange("p (hp two) e -> p two hp e", two=2),
                                              in0=yp[:].rearrange("p (two hp) e -> p two hp e", two=2),
                                              in1=qdec[:, :].rearrange("p (two hp) -> p two hp", two=2).unsqueeze(3).to_broadcast([128, 2, 4, 128]), op=ALU.mult),
             reads=[ryp, R_qdec], writes=[ry])
        c["y"], c["ry"] = y, ry

    def back(n):
        c = ctx.pop(n)
        y, ry, sg, rsg, cc = c["y"], c["ry"], c["sg"], c["rsg"], c["cc"]
        sq, rsq = ysq.next()
        ss, rss = sss.next()
        for h in range(8):
            S.op("act", lambda e, h=h: e.activation(out=sq[:, h, :], in_=y[:, h, :], func=AF.Square, accum_out=ss[:, h:h + 1]), reads=[ry], writes=[rsq, rss])
        S.op("dve", lambda e: e.tensor_scalar(out=ss[:], in0=ss[:], scalar1=1.0 / 128.0, scalar2=RMS_EPS, op0=ALU.mult, op1=ALU.add), reads=[rss], writes=[rss])
        S.op("act", lambda e: e.sqrt(out=ss[:], in_=ss[:]), reads=[rss], writes=[rss])
        S.op("dve", lambda e: e.reciprocal(out=ss[:], in_=ss[:]), reads=[rss], writes=[rss])
        S.op("dve", lambda e: e.tensor_tensor(out=y[:], in0=y[:], in1=ss[:, :].unsqueeze(2).to_broadcast([128, 8, 128]), op=ALU.mult), reads=[ry, rss], writes=[ry])
        yf = y[:].rearrange("p h e -> p (h e)")
        ob, rob = obs.next()
        S.op("dve", lambda e: e.tensor_tensor(out=ob[:], in0=yf, in1=sg[:], op=ALU.mult), reads=[ry, rsg], writes=[rob])
        ot, rot = ot_ps.next()
        for k in range(8):
            S.op("pe", lambda e, k=k: e.transpose(out=ot[:, k, :], in_=ob[:, k * 128:(k + 1) * 128], identity=identb[:]), reads=[rob, R_id], writes=[rot])
        if cc == 0:
            ctx["obt"] = obt.next()
        ob4, rob4 = ctx["obt"]
        S.op("act", lambda e: e.copy(out=ob4[:, :, cc * 128:(cc + 1) * 128], in_=ot[:]), reads=[rot], writes=[rob4])
        if cc == 3:
            t = n // 4
            S.dma("sp", lambda e: e.dma_start(out=obT.rearrange("(c p) n -> p c n", p=128)[:, :, t * 512:(t + 1) * 512], in_=ob4[:]), reads=[rob4])

    LV = C_LEVEL
    NCH = C_NCH
    front(0)
    if LV >= 2:
        su(0)
    for n in range(NCH):
        if n + 1 < NCH:
            front(n + 1)
        if LV >= 3:
            mid(n)
        if n + 1 < NCH and LV >= 2:
            if LV >= 3:
                sbf_copy()
            su(n + 1)
        if n >= 1 and LV >= 4:
            back(n - 1)
    if LV >= 4:
        back(NCH - 1)
    T.finish()


class LNBufs:
    def __init__(self, T, S, P, gname, bname, layer):
        self.g_bc = T.sb([128, D], F32, "lng")
        self.b_bc = T.sb([128, D], F32, "lnb")
        self.R_g, self.R_b = RL(2)
        g, b = P.inp(gname), P.inp(bname)
        S.dma("sp", lambda e: e.dma_start(out=self.g_bc[:], in_=g[layer:layer + 1, :].partition_broadcast(128)), writes=[self.R_g])
        S.dma("sp", lambda e: e.dma_start(out=self.b_bc[:], in_=b[layer:layer + 1, :].partition_broadcast(128)), writes=[self.R_b])
        self.stats = Rot([T.sb([128, 2, 6], F32, "lnst") for _ in range(2)])
        self.mv = Rot([T.sb([128, 2], F32, "lnmv") for _ in range(2)])


def emit_ln(S, L, r, rr, dst_rows, gb_eng="pool"):
    st, rst = L.stats.next()
    mv, rmv = L.mv.next()
    S.op("dve", lambda e: e.bn_stats(out=st[:, 0, :], in_=r[:, 0:512]), reads=[rr], writes=[rst])
    S.op("dve", lambda e: e.bn_stats(out=st[:, 1, :], in_=r[:, 512:1024]), reads=[rr], writes=[rst])
    S.op("dve", lambda e: e.bn_aggr(out=mv[:], in_=st[:].rearrange("p a b -> p (a b)")), reads=[rst], writes=[rmv])
    S.op("dve", lambda e: e.tensor_scalar(out=mv[:, 1:2], in0=mv[:, 1:2], scalar1=LN_EPS, scalar2=None, op0=ALU.add), reads=[rmv], writes=[rmv])
    S.op("act", lambda e: e.sqrt(out=mv[:, 1:2], in_=mv[:, 1:2]), reads=[rmv], writes=[rmv])
    S.op("dve", lambda e: e.reciprocal(out=mv[:, 1:2], in_=mv[:, 1:2]), reads=[rmv], writes=[rmv])
    S.op("dve", lambda e: e.tensor_scalar(out=r[:], in0=r[:], scalar1=mv[:, 0:1], scalar2=mv[:, 1:2], op0=ALU.subtract, op1=ALU.mult),
         reads=[rr, rmv], writes=[rr])
    S.op(gb_eng, lambda e: e.tensor_tensor(out=r[:], in0=r[:], in1=L.g_bc[:], op=ALU.mult), reads=[rr, L.R_g], writes=[rr])
    S.op(gb_eng, lambda e: e.tensor_tensor(out=r[:], in0=r[:], in1=L.b_bc[:], op=ALU.add), reads=[rr, L.R_b], writes=[rr])
    S.dma("sp", lambda e: e.dma_start(out=dst_rows, in_=r[:]), reads=[rr])


def stageD(P, layer, x_src, x_dst):
    nc = P.nc
    T = Stage(nc, "d")
    S = T.S
    oaT, obT, gT = P.oaT(), P.obT(), P.gT()
    wf = T.sb([128, 4, D], BF16, "wf")
    wr = T.sb([128, 8, D], BF16, "wr")
    wo = T.sb([128, 8, D], BF16, "wo")
    R_wf, R_wr, R_wo = RL(3)
    w_fox, w_ret, w_out = P.inp("w_branch_fox"), P.inp("w_branch_ret"), P.inp("w_out")
    for k in range(4):
        S.dma("pool", lambda e, k=k: e.dma_start(out=wf[:, k, :], in_=w_fox[layer, k * 128:(k + 1) * 128, :]), writes=[R_wf])
    for k in range(8):
        S.dma("pool", lambda e, k=k: e.dma_start(out=wr[:, k, :], in_=w_ret[layer, k * 128:(k + 1) * 128, :]), writes=[R_wr])
    for k in range(8):
        S.dma("pool", lambda e, k=k: e.dma_start(out=wo[:, k, :], in_=w_out[layer, k * 128:(k + 1) * 128, :]), writes=[R_wo])
    L = LNBufs(T, S, P, "ln_mix_g", "ln_mix_b", layer)
    oas = Rot([T.sb([128, 4, 512], BF16, "oa") for _ in range(2)])
    obs = Rot([T.sb([128, 8, 512], BF16, "ob") for _ in range(2)])
    gs = Rot([T.sb([128, 16, 512], BF16, "g") for _ in range(2)])
    mTs = Rot([T.sb([128, 8, 512], BF16, "mT") for _ in range(2)])
    t1s = Rot([T.sb([128, 512], F32, "t1") for _ in range(2)])
    t2s = Rot([T.sb([128, 512], F32, "t2") for _ in range(2)])
    xs_ = Rot([T.sb([128, D], F32, "x") for _ in range(3)])
    pa_ = Rot([T.ps([128, 512], F32, "pa") for _ in range(2)])
    pb_ = Rot([T.ps([128, 512], F32, "pb") for _ in range(2)])
    ph_ = Rot([T.ps([128, 512], F32, "ph") for _ in range(2)])
    oasrc = oaT.rearrange("(k p) n -> p k n", p=128)
    obsrc = obT.rearrange("(k p) n -> p k n", p=128)
    gsrc = gT.rearrange("(k p) n -> p k n", p=128)
    mts = {}

    def emit_merge(t):
        ts_ = slice(t * 512, (t + 1) * 512)
        oa, roa = oas.next()
        ob, rob = obs.next()
        g, rg = gs.next()
        mT, rmT = mTs.next()
        S.dma("sp", lambda e: e.dma_start(out=oa[:], in_=oasrc[:, :, ts_]), writes=[roa])
        S.dma("sp", lambda e: e.dma_start(out=ob[:], in_=obsrc[:, :, ts_]), writes=[rob])
        S.dma("sp", lambda e: e.dma_start(out=g[:, 0:8, :], in_=gsrc[:, 0:8, ts_]), writes=[rg])
        S.dma("sp", lambda e: e.dma_start(out=g[:, 8:16, :], in_=gsrc[:, 8:16, ts_]), writes=[rg])
        for cc in range(8):
            pa, rpa = pa_.next()
            pb, rpb = pb_.next()
            for k in range(4):
                S.op("pe", lambda e, pa=pa, k=k, cc=cc: e.matmul(pa[:, :], lhsT=wf[:, k, cc * 128:(cc + 1) * 128], rhs=oa[:, k, :], start=(k == 0), stop=(k == 3)),
                     reads=[R_wf, roa], writes=[rpa])
            for k in range(8):
                S.op("pe", lambda e, pb=pb, k=k, cc=cc: e.matmul(pb[:, :], lhsT=wr[:, k, cc * 128:(cc + 1) * 128], rhs=ob[:, k, :], start=(k == 0), stop=(k == 7)),
                     reads=[R_wr, rob], writes=[rpb])
            t1, rt1 = t1s.next()
            t2, rt2 = t2s.next()
            S.op("dve", lambda e, t1=t1, pa=pa, cc=cc: e.tensor_tensor(out=t1[:], in0=pa[:, :], in1=g[:, cc, :], op=ALU.mult), reads=[rpa, rg], writes=[rt1])
            S.op("dve", lambda e, t2=t2, pb=pb, cc=cc: e.tensor_tensor(out=t2[:], in0=pb[:, :], in1=g[:, 8 + cc, :], op=ALU.mult), reads=[rpb, rg], writes=[rt2])
            S.op("pool", lambda e, t1=t1, t2=t2, cc=cc: e.tensor_tensor(out=mT[:, cc, :], in0=t1[:], in1=t2[:], op=ALU.add), reads=[rt1, rt2], writes=[rmT])
        mts[t] = (mT, rmT)

    def emit_out(t):
        mT, rmT = mts.pop(t)
        for tb in range(4):
            blk = t * 4 + tb
            x, rx = xs_.next()
            S.dma("sp", lambda e, x=x, blk=blk: e.dma_start(out=x[:], in_=x_src[blk * 128:(blk + 1) * 128, :]), writes=[rx])
            for hf in range(2):
                ph, rph = ph_.next()
                for cc in range(8):
                    S.op("pe", lambda e, ph=ph, cc=cc, tb=tb, hf=hf: e.matmul(ph[:, :], lhsT=mT[:, cc, tb * 128:(tb + 1) * 128], rhs=wo[:, cc, hf * 512:(hf + 1) * 512],
                                                                            start=(cc == 0), stop=(cc == 7)), reads=[rmT, R_wo], writes=[rph])
                S.op("dve", lambda e, x=x, ph=ph, hf=hf: e.scalar_tensor_tensor(out=x[:, hf * 512:(hf + 1) * 512], in0=x[:, hf * 512:(hf + 1) * 512], scalar=DN_ALPHA, in1=ph[:, :],
                                                                                op0=ALU.mult, op1=ALU.add), reads=[rx, rph], writes=[rx])
            emit_ln(S, L, x, rx, x_dst[blk * 128:(blk + 1) * 128, :])

    emit_merge(0)
    for t in range(NT):
        if t + 1 < NT:
            emit_merge(t + 1)
        emit_out(t)
    T.finish()


def stageE0(P, layer, x_src):
    nc = P.nc
    i = layer // 2
    T = Stage(nc, "r")
    S = T.S
    identf = T.sb([128, 128], F32, "identf")
    wrf = T.sb([128, 8, 8], F32, "wrf")
    gate_all = T.sb([128, NB, 8], F32, "gates")
    R_id, R_wr, R_ga = RL(3)
    c_ident, router = P.inp("c_ident"), P.inp("moe_router")
    S.dma("sp", lambda e: e.dma_start(out=identf[:], in_=c_ident[:, :]), writes=[R_id])
    S.dma("sp", lambda e: e.dma_start(out=wrf[:], in_=router[i].rearrange("(k p) n -> p k n", p=128)), writes=[R_wr])
    xfs = Rot([T.sb([128, D], F32, "xf") for _ in range(2)])
    xTs = Rot([T.sb([128, 8, 128], F32, "xTf") for _ in range(2)])
    pT_ = Rot([T.ps([128, 8, 128], F32, "pTf") for _ in range(2)])
    lg_ = Rot([T.ps([128, 512], F32, "lg") for _ in range(2)])
    sm = [Rot([T.sb([128, 8], F32, "sm%d" % j) for _ in range(2)]) for j in range(5)]
    sc1 = [Rot([T.sb([128, 1], F32, "sc%d" % j) for _ in range(2)]) for j in range(4)]
    gd = P.scratch("gates_d", [128, NB, 8], F32)
    for blk in range(NB):
        xf, rxf = xfs.next()
        S.dma("sp", lambda e, xf=xf, blk=blk: e.dma_start(out=xf[:], in_=x_src[blk * 128:(blk + 1) * 128, :]), writes=[rxf])
        pT, rpT = pT_.next()
        for k in range(8):
            S.op("pe", lambda e, pT=pT, xf=xf, k=k: e.transpose(out=pT[:, k, :], in_=xf[:, k * 128:(k + 1) * 128], identity=identf[:]), reads=[rxf, R_id], writes=[rpT])
        xT, rxT = xTs.next()
        S.op("act", lambda e, xT=xT, pT=pT: e.copy(out=xT[:, 0:4, :], in_=pT[:, 0:4, :]), reads=[rpT], writes=[rxT])
        S.op("dve", lambda e, xT=xT, pT=pT: e.tensor_copy(out=xT[:, 4:8, :], in_=pT[:, 4:8, :]), reads=[rpT], writes=[rxT])
        lg, rlg = lg_.next()
        for k in range(8):
            S.op("pe", lambda e, lg=lg, xT=xT, k=k: e.matmul(lg[:, 0:8], lhsT=xT[:, k, :], rhs=wrf[:, k, :], start=(k == 0), stop=(k == 7)), reads=[rxT, R_wr], writes=[rlg])
        (lgs, rlgs), (eq, req), (l2, rl2), (sel, rsel), (ex, rex) = [r_.next() for r_ in sm]
        (m1, rm1), (m2, rm2), (nm1, rnm1), (den, rden) = [r_.next() for r_ in sc1]
        S.op("dve", lambda e, lgs=lgs, lg=lg: e.tensor_copy(out=lgs[:], in_=lg[:, 0:8]), reads=[rlg], writes=[rlgs])
        S.op("dve", lambda e, m1=m1, lgs=lgs: e.tensor_reduce(out=m1[:], in_=lgs[:], axis=AX.X, op=ALU.max), reads=[rlgs], writes=[rm1])
        S.op("dve", lambda e, eq=eq, lgs=lgs, m1=m1: e.tensor_scalar(out=eq[:], in0=lgs[:], scalar1=m1[:, 0:1], scalar2=None, op0=ALU.is_equal), reads=[rlgs, rm1], writes=[req])
        S.op("dve", lambda e, l2=l2, eq=eq, lgs=lgs: e.scalar_tensor_tensor(out=l2[:], in0=eq[:], scalar=-1.0e30, in1=lgs[:], op0=ALU.mult, op1=ALU.add), reads=[req, rlgs], writes=[rl2])
        S.op("dve", lambda e, m2=m2, l2=l2: e.tensor_reduce(out=m2[:], in_=l2[:], axis=AX.X, op=ALU.max), reads=[rl2], writes=[rm2])
        S.op("dve", lambda e, sel=sel, lgs=lgs, m2=m2: e.tensor_scalar(out=sel[:], in0=lgs[:], scalar1=m2[:, 0:1], scalar2=None, op0=ALU.is_ge), reads=[rlgs, rm2], writes=[rsel])
        S.op("dve", lambda e, nm1=nm1, m1=m1: e.tensor_scalar(out=nm1[:], in0=m1[:], scalar1=-1.0, scalar2=None, op0=ALU.mult), reads=[rm1], writes=[rnm1])
        S.op("act", lambda e, ex=ex, lgs=lgs, nm1=nm1: e.activation(out=ex[:], in_=lgs[:], func=AF.Exp, bias=nm1[:, 0:1], scale=1.0), reads=[rlgs, rnm1], writes=[rex])
        S.op("dve", lambda e, ex=ex, sel=sel: e.tensor_tensor(out=ex[:], in0=ex[:], in1=sel[:], op=ALU.mult), reads=[rex, rsel], writes=[rex])
        S.op("dve", lambda e, den=den, ex=ex: e.tensor_reduce(out=den[:], in_=ex[:], axis=AX.X, op=ALU.add), reads=[rex], writes=[rden])
        S.op("dve", lambda e, den=den: e.reciprocal(out=den[:], in_=den[:]), reads=[rden], writes=[rden])
        S.op("dve", lambda e, ex=ex, den=den, blk=blk: e.tensor_scalar(out=gate_all[:, blk, :], in0=ex[:], scalar1=den[:, 0:1], scalar2=None, op0=ALU.mult), reads=[rex, rden], writes=[R_ga])
    S.dma("sp", lambda e: e.dma_start(out=gd[:, :, :], in_=gate_all[:]), reads=[R_ga])
    T.finish()


def stageE(P, layer, x_src, x_dst, moe, pump=None, pump_n=0):
    nc = P.nc
    i = layer // 2
    if moe:
        NE, FF = N_EXP, D_FFE
        wg_d, wu_d, wd_d = P.inp("moe_w_gate"), P.inp("moe_w_up"), P.inp("moe_w_down")
        wg = lambda e_: wg_d[i, e_]
        wu = lambda e_: wu_d[i, e_]
        wd = lambda e_: wd_d[i, e_]
    else:
        NE, FF = 1, D_FF
        wg_d, wu_d, wd_d = P.inp("ffn_w_gate"), P.inp("ffn_w_up"), P.inp("ffn_w_down")
        wg = lambda e_: wg_d[i]
        wu = lambda e_: wu_d[i]
        wd = lambda e_: wd_d[i]
    GC = 2
    NG = FF // (128 * GC)
    HB = NB // 2
    T = Stage(nc, "e")
    S = T.S
    ident, R_id = load_ident(T, P, S)
    L = LNBufs(T, S, P, "ln_ffn_g", "ln_ffn_b", layer)
    gate_all = None
    R_ga = Res()
    if moe:
        gate_all = T.sb([128, NB, 8], F32, "gates")
        gd = P.scratch("gates_d", [128, NB, 8], F32)
        S.dma("sp", lambda e: e.dma_start(out=gate_all[:], in_=gd[:, :, :]), writes=[R_ga])
    if pump is not None:
        pump.attach(T, 3)
    xT = T.sb([128, 8, HB * 128], BF16, "xT")
    acc = T.sb([128, HB, D], F32, "acc")
    R_xT = RL(HB)
    R_acc = RL(HB)
    wgs = Rot([T.sb([128, 8, GC * 128], BF16, "wg") for _ in range(2)])
    wus = Rot([T.sb([128, 8, GC * 128], BF16, "wu") for _ in range(2)])
    wds = Rot([T.sb([128, GC, D], BF16, "wd") for _ in range(2)])
    uTs = Rot([T.sb([128, GC, 512], BF16, "uT") for _ in range(3)])
    sgs = Rot([T.sb([128, 512], F32, "sg") for _ in range(2)])
    pg_ = Rot([T.ps([128, 512], F32, "pg") for _ in range(2)])
    pu_ = Rot([T.ps([128, 512], F32, "pu") for _ in range(2)])
    pd_ = Rot([T.ps([128, 512], F32, "pd") for _ in range(2)])
    xf = Rot([T.sb([128, D], F32, "xf") for _ in range(2)])
    xb = Rot([T.sb([128, D], BF16, "xb") for _ in range(2)])
    pT = Rot([T.ps([128, 8, 128], BF16, "pT") for _ in range(2)])
    for half in range(2):
        tok0 = half * HB * 128
        for bl in range(HB):
            f, rf = xf.next()
            b, rb = xb.next()
            p, rp = pT.next()
            r0 = tok0 + bl * 128
            S.dma("sp", lambda e, f=f, r0=r0: e.dma_start(out=f[:], in_=x_src[r0:r0 + 128, :]), writes=[rf])
            S.op("act", lambda e, f=f, b=b: e.copy(out=b[:], in_=f[:]), reads=[rf], writes=[rb])
            for k in range(8):
                S.op("pe", lambda e, p=p, b=b, k=k: e.transpose(out=p[:, k, :], in_=b[:, k * 128:(k + 1) * 128], identity=ident[:]), reads=[rb, R_id], writes=[rp])
            S.op("dve", lambda e, p=p, bl=bl: e.tensor_copy(out=xT[:, :, bl * 128:(bl + 1) * 128], in_=p[:]), reads=[rp], writes=[R_xT[bl]])
        items = [(ex, g, t) for ex in range(NE) for g in range(NG) for t in range(HB // 4)]
        wcur = {}
        pend = {}

        def emit_gu(it):
            ex, g, t = it
            ff0 = g * GC * 128
            if t == 0:
                wgb, rwg = wgs.next()
                wub, rwu = wus.next()
                wdb, rwd = wds.next()
                S.dma("pool", lambda e: e.dma_start(out=wgb[:], in_=wg(ex)[:, ff0:ff0 + GC * 128].rearrange("(k p) n -> p k n", p=128)), writes=[rwg])
                S.dma("pool", lambda e: e.dma_start(out=wub[:], in_=wu(ex)[:, ff0:ff0 + GC * 128].rearrange("(k p) n -> p k n", p=128)), writes=[rwu])
                S.dma("pool", lambda e: e.dma_start(out=wdb[:], in_=wd(ex)[ff0:ff0 + GC * 128, :].rearrange("(c p) n -> p c n", p=128)), writes=[rwd])
                wcur[(ex, g)] = (wgb, rwg, wub, rwu, wdb, rwd)
            wgb, rwg, wub, rwu, wdb, rwd = wcur[(ex, g)]
            uT, ruT = uTs.next()
            for c in range(GC):
                pg, rpg = pg_.next()
                pu, rpu = pu_.next()
                for k in range(8):
                    S.op("pe", lambda e, pg=pg, k=k, c=c: e.matmul(pg[:, :], lhsT=wgb[:, k, c * 128:(c + 1) * 128], rhs=xT[:, k, t * 512:(t + 1) * 512],
                                                                 start=(k == 0), stop=(k == 7)), reads=[rwg] + R_xT[4 * t:4 * t + 4], writes=[rpg])
                for k in range(8):
                    S.op("pe", lambda e, pu=pu, k=k, c=c: e.matmul(pu[:, :], lhsT=wub[:, k, c * 128:(c + 1) * 128], rhs=xT[:, k, t * 512:(t + 1) * 512],
                                                                 start=(k == 0), stop=(k == 7)), reads=[rwu] + R_xT[4 * t:4 * t + 4], writes=[rpu])
                sg, rsg = sgs.next()
                S.op("act", lambda e, sg=sg, pg=pg: e.activation(out=sg[:], in_=pg[:, :], func=AF.Silu), reads=[rpg], writes=[rsg])
                S.op("dve", lambda e, c=c, sg=sg, pu=pu: e.tensor_tensor(out=uT[:, c, :], in0=pu[:, :], in1=sg[:], op=ALU.mult), reads=[rpu, rsg], writes=[ruT])
            pend[it] = (uT, ruT, wdb, rwd)

        def emit_down(it):
            ex, g, t = it
            uT, ruT, wdb, rwd = pend.pop(it)
            first = (ex == 0 and g == 0)
            for tb in range(4):
                bl = t * 4 + tb
                blk = half * HB + bl
                for hf in range(2):
                    pd, rpd = pd_.next()
                    for c in range(GC):
                        S.op("pe", lambda e, pd=pd, c=c, tb=tb, hf=hf: e.matmul(pd[:, :], lhsT=uT[:, c, tb * 128:(tb + 1) * 128], rhs=wdb[:, c, hf * 512:(hf + 1) * 512],
                                                                              start=(c == 0), stop=(c == GC - 1)), reads=[ruT, rwd], writes=[rpd])
                    a_ = acc[:, bl, hf * 512:(hf + 1) * 512]
                    if moe:
                        gsc = gate_all[:, blk, ex:ex + 1]
                        if first:
                            S.op("dve", lambda e, a_=a_, pd=pd, gsc=gsc: e.tensor_scalar(out=a_, in0=pd[:, :], scalar1=gsc, scalar2=None, op0=ALU.mult), reads=[rpd, R_ga], writes=[R_acc[bl]])
                        else:
                            S.op("dve", lambda e, a_=a_, pd=pd, gsc=gsc: e.scalar_tensor_tensor(out=a_, in0=pd[:, :], scalar=gsc, in1=a_, op0=ALU.mult, op1=ALU.add),
                                 reads=[rpd, R_ga, R_acc[bl]], writes=[R_acc[bl]])
                    else:
                        if first:
                            S.op("dve", lambda e, a_=a_, pd=pd: e.tensor_copy(out=a_, in_=pd[:, :]), reads=[rpd], writes=[R_acc[bl]])
                        else:
                            S.op("dve", lambda e, a_=a_, pd=pd: e.tensor_tensor(out=a_, in0=pd[:, :], in1=a_, op=ALU.add), reads=[rpd, R_acc[bl]], writes=[R_acc[bl]])

        emit_gu(items[0])
        for ii, it in enumerate(items):
            if ii + 1 < len(items):
                emit_gu(items[ii + 1])
            if pump is not None:
                pump.pump(S, pump_n)
            emit_down(it)
        for bl in range(HB):
            f, rf = xf.next()
            r0 = tok0 + bl * 128
            S.dma("sp", lambda e, f=f, r0=r0: e.dma_start(out=f[:], in_=x_src[r0:r0 + 128, :]), writes=[rf])
            S.op("dve", lambda e, f=f, bl=bl: e.scalar_tensor_tensor(out=f[:], in0=f[:], scalar=DN_ALPHA, in1=acc[:, bl, :], op0=ALU.mult, op1=ALU.add),
                 reads=[rf, R_acc[bl]], writes=[rf])
            emit_ln(S, L, f, rf, x_dst[r0:r0 + 128, :])
    T.finish()


TS = 512
NTILE = (2 * S_LEN) // TS + N_EXP
NSLOT = NTILE * TS
SGC = 4
SNG = D_FFE // (128 * SGC)
WROW = 8 * SGC * 128
IOA = bass.IndirectOffsetOnAxis


def sp_scratch(P, layer):
    return dict(
        xsort=P.scratch("xsort", [NSLOT, D], BF16),
        ysort=P.scratch("ysort", [NSLOT, D], F32),
        wgs=P.scratch("wgs%d" % layer, [N_EXP * SNG * 128, WROW], BF16),
        wus=P.scratch("wus%d" % layer, [N_EXP * SNG * 128, WROW], BF16),
        wds=P.scratch("wds%d" % layer, [N_EXP * SNG * 128, SGC * D], BF16),
        slotA=P.scratch("slotA", [128, NB], I32),
        slotB=P.scratch("slotB", [128, NB], I32),
        gA=P.scratch("gA", [128, NB], F32),
        gB=P.scratch("gB", [128, NB], F32),
        widx=P.scratch("widx", [128, NTILE * SNG], I32),
    )


class BgPump:
    def __init__(self, P, layer):
        i = layer // 2
        D_ = sp_scratch(P, layer)
        wg_d, wu_d, wd_d = P.inp("moe_w_gate"), P.inp("moe_w_up"), P.inp("moe_w_down")
        self.jobs = []
        for ex in range(N_EXP):
            for g in range(SNG):
                ff0 = g * SGC * 128
                r0 = (ex * SNG + g) * 128
                for src, dst in ((wg_d, D_["wgs"]), (wu_d, D_["wus"])):
                    self.jobs.append(("A", src[i, ex, :, ff0:ff0 + SGC * 128].rearrange("(k p) n -> p k n", p=128), dst[r0:r0 + 128, :]))
                self.jobs.append(("D", wd_d[i, ex, ff0:ff0 + SGC * 128, :].rearrange("(c p) n -> p c n", p=128), D_["wds"][r0:r0 + 128, :]))
        self.bufs = None

    def attach(self, T, n):
        self.bufs = Rot([T.sb([128, 8 * SGC * 128], BF16, "bg") for _ in range(n)])

    def pump(self, S, n):
        for _ in range(n):
            if not self.jobs:
                return
            kind, src, dst = self.jobs.pop(0)
            b, rb = self.bufs.next()
            view = b[:].rearrange("p (k n) -> p k n", k=8) if kind == "A" else b[:].rearrange("p (c n) -> p c n", c=SGC)
            S.dma("pool", lambda e, view=view, src=src: e.dma_start(out=view, in_=src), writes=[rb])
            S.dma("sp", lambda e, b=b, dst=dst: e.dma_start(out=dst, in_=b[:]), reads=[rb])


def stageW(P, pump):
    if not pump.jobs:
        return
    T = Stage(P.nc, "w")
    pump.attach(T, 6)
    pump.pump(T.S, len(pump.jobs))
    T.finish()


def stageR(P, layer, x_src):
    nc = P.nc
    i = layer // 2
    T = Stage(nc, "r")
    S = T.S
    D_ = sp_scratch(P, layer)
    identf = T.sb([128, 128], F32, "identf")
    wrf = T.sb([128, 8, 8], F32, "wrf")
    gate_all = T.sb([128, NB, 8], F32, "gates")
    sel_all = T.sb([128, NB, 8], F32, "sel")
    xb_all = T.sb([128, NB, D], BF16, "xball")
    R_id, R_wr, R_ga, R_sel = RL(4)
    R_xb = RL(NB)
    c_ident, router = P.inp("c_ident"), P.inp("moe_router")
    S.dma("sp", lambda e: e.dma_start(out=identf[:], in_=c_ident[:, :]), writes=[R_id])
    S.dma("sp", lambda e: e.dma_start(out=wrf[:], in_=router[i].rearrange("(k p) n -> p k n", p=128)), writes=[R_wr])
    xfs = Rot([T.sb([128, D], F32, "xf") for _ in range(2)])
    xTs = Rot([T.sb([128, 8, 128], F32, "xTf") for _ in range(2)])
    pT_ = Rot([T.ps([128, 8, 128], F32, "pTf") for _ in range(2)])
    lg_ = Rot([T.ps([128, 512], F32, "lg") for _ in range(2)])
    sm = [Rot([T.sb([128, 8], F32, "sm%d" % j) for _ in range(2)]) for j in range(4)]
    sc1 = [Rot([T.sb([128, 1], F32, "sc%d" % j) for _ in range(2)]) for j in range(4)]
    for blk in range(NB):
        xf, rxf = xfs.next()
        S.dma("sp", lambda e, xf=xf, blk=blk: e.dma_start(out=xf[:], in_=x_src[blk * 128:(blk + 1) * 128, :]), writes=[rxf])
        S.op("act", lambda e, xf=xf, blk=blk: e.copy(out=xb_all[:, blk, :], in_=xf[:]), reads=[rxf], writes=[R_xb[blk]])
        pT, rpT = pT_.next()
        for k in range(8):
            S.op("pe", lambda e, pT=pT, xf=xf, k=k: e.transpose(out=pT[:, k, :], in_=xf[:, k * 128:(k + 1) * 128], identity=identf[:]), reads=[rxf, R_id], writes=[rpT])
        xT, rxT = xTs.next()
        S.op("act", lambda e, xT=xT, pT=pT: e.copy(out=xT[:, 0:4, :], in_=pT[:, 0:4, :]), reads=[rpT], writes=[rxT])
        S.op("dve", lambda e, xT=xT, pT=pT: e.tensor_copy(out=xT[:, 4:8, :], in_=pT[:, 4:8, :]), reads=[rpT], writes=[rxT])
        lg, rlg = lg_.next()
        for k in range(8):
            S.op("pe", lambda e, lg=lg, xT=xT, k=k: e.matmul(lg[:, 0:8], lhsT=xT[:, k, :], rhs=wrf[:, k, :], start=(k == 0), stop=(k == 7)), reads=[rxT, R_wr], writes=[rlg])
        (lgs, rlgs), (eq, req), (l2, rl2), (ex, rex) = [r_.next() for r_ in sm]
        (m1, rm1), (m2, rm2), (nm1, rnm1), (den, rden) = [r_.next() for r_ in sc1]
        S.op("dve", lambda e, lgs=lgs, lg=lg: e.tensor_copy(out=lgs[:], in_=lg[:, 0:8]), reads=[rlg], writes=[rlgs])
        S.op("dve", lambda e, m1=m1, lgs=lgs: e.tensor_reduce(out=m1[:], in_=lgs[:], axis=AX.X, op=ALU.max), reads=[rlgs], writes=[rm1])
        S.op("dve", lambda e, eq=eq, lgs=lgs, m1=m1: e.tensor_scalar(out=eq[:], in0=lgs[:], scalar1=m1[:, 0:1], scalar2=None, op0=ALU.is_equal), reads=[rlgs, rm1], writes=[req])
        S.op("dve", lambda e, l2=l2, eq=eq, lgs=lgs: e.scalar_tensor_tensor(out=l2[:], in0=eq[:], scalar=-1.0e30, in1=lgs[:], op0=ALU.mult, op1=ALU.add), reads=[req, rlgs], writes=[rl2])
        S.op("dve", lambda e, m2=m2, l2=l2: e.tensor_reduce(out=m2[:], in_=l2[:], axis=AX.X, op=ALU.max), reads=[rl2], writes=[rm2])
        S.op("dve", lambda e, lgs=lgs, m2=m2, blk=blk: e.tensor_scalar(out=sel_all[:, blk, :], in0=lgs[:], scalar1=m2[:, 0:1], scalar2=None, op0=ALU.is_ge), reads=[rlgs, rm2], writes=[R_sel])
        S.op("dve", lambda e, nm1=nm1, m1=m1: e.tensor_scalar(out=nm1[:], in0=m1[:], scalar1=-1.0, scalar2=None, op0=ALU.mult), reads=[rm1], writes=[rnm1])
        S.op("act", lambda e, ex=ex, lgs=lgs, nm1=nm1: e.activation(out=ex[:], in_=lgs[:], func=AF.Exp, bias=nm1[:, 0:1], scale=1.0), reads=[rlgs, rnm1], writes=[rex])
        S.op("dve", lambda e, ex=ex, blk=blk: e.tensor_tensor(out=ex[:], in0=ex[:], in1=sel_all[:, blk, :], op=ALU.mult), reads=[rex, R_sel], writes=[rex])
        S.op("dve", lambda e, den=den, ex=ex: e.tensor_reduce(out=den[:], in_=ex[:], axis=AX.X, op=ALU.add), reads=[rex], writes=[rden])
        S.op("dve", lambda e, den=den: e.reciprocal(out=den[:], in_=den[:]), reads=[rden], writes=[rden])
        S.op("dve", lambda e, ex=ex, den=den, blk=blk: e.tensor_scalar(out=gate_all[:, blk, :], in0=ex[:], scalar1=den[:, 0:1], scalar2=None, op0=ALU.mult), reads=[rex, rden], writes=[R_ga])
    NBE = NB * 8
    tri = T.sb([128, 128], BF16, "tri")
    onesb = T.sb([128, 128], BF16, "onesb")
    selb = T.sb([128, NBE], BF16, "selb")
    tot = T.sb([128, NBE], F32, "tot")
    inc = T.sb([128, NBE], F32, "inc")
    slot = T.sb([128, NBE], F32, "slot")
    tmp = T.sb([128, NBE], F32, "tmp")
    one1 = T.sb([128, 1], F32, "one1")
    ne = T.sb([128, 8], F32, "ne")
    padn = T.sb([128, 8], F32, "padn")
    send = T.sb([128, 8], F32, "send")
    sstart = T.sb([128, 8], F32, "sstart")
    sa = T.sb([128, NB], F32, "sa")
    sb_ = T.sb([128, NB], F32, "sb")
    ga = T.sb([128, NB], F32, "ga")
    gb = T.sb([128, NB], F32, "gb")
    sai = T.sb([128, NB], I32, "sai")
    sbi = T.sb([128, NB], I32, "sbi")
    tlim = T.sb([128, NTILE], F32, "tlim")
    cmp_ = T.sb([128, NTILE, 8], F32, "cmp")
    ei = T.sb([128, NTILE], F32, "ei")
    wix = T.sb([128, NTILE, SNG], F32, "wix")
    wixi = T.sb([128, NTILE, SNG], I32, "wixi")
    R_tri, R_ones, R_selb, R_tot, R_inc, R_slot, R_tmp, R_one1, R_ne, R_padn, R_send, R_ss, R_sa, R_sb, R_gab, R_sai, R_sbi, R_tlim, R_cmp, R_ei, R_wix, R_wixi = RL(22)
    rk_ps = Rot([T.ps([128, 512], F32, "rk") for _ in range(1)])
    tt_ps = Rot([T.ps([128, 512], F32, "tt") for _ in range(1)])
    c_tri, c_tlim, c_wbase = P.inp("c_tri"), P.inp("c_tlim"), P.inp("c_wbase")
    S.dma("pool", lambda e: e.dma_start(out=tri[:], in_=c_tri[:, :]), writes=[R_tri])
    S.dma("sp", lambda e: e.dma_start(out=tlim[:], in_=c_tlim[:, :]), writes=[R_tlim])
    S.dma("sp", lambda e: e.dma_start(out=wix[:], in_=c_wbase[:, :, :]), writes=[R_wix])
    S.op("pool", lambda e: e.memset(onesb[:], 1.0), writes=[R_ones])
    S.op("pool", lambda e: e.memset(one1[:], 1.0), writes=[R_one1])
    self_f = sel_all[:].rearrange("p b e -> p (b e)")
    gate_f = gate_all[:].rearrange("p b e -> p (b e)")
    S.op("dve", lambda e: e.tensor_copy(out=selb[:], in_=self_f), reads=[R_sel], writes=[R_selb])
    rk, rrk = rk_ps.next()
    tt, rtt = tt_ps.next()
    S.op("pe", lambda e: e.matmul(rk[:, 0:NBE], lhsT=tri[:], rhs=selb[:], start=True, stop=True), reads=[R_tri, R_selb], writes=[rrk])
    S.op("pe", lambda e: e.matmul(tt[:, 0:NBE], lhsT=onesb[:], rhs=selb[:], start=True, stop=True), reads=[R_ones, R_selb], writes=[rtt])
    S.op("dve", lambda e: e.tensor_copy(out=tot[:], in_=tt[:, 0:NBE]), reads=[rtt], writes=[R_tot])
    tot_v = tot[:].rearrange("p (b e) -> p e b", e=8)
    inc_v = inc[:].rearrange("p (b e) -> p e b", e=8)
    for ee in range(8):
        S.op("dve", lambda e, ee=ee: e.tensor_tensor_scan(out=inc_v[:, ee, :], data0=one1[:, 0:1].to_broadcast([128, NB]), data1=tot_v[:, ee, :], initial=0.0,
                                                          op0=ALU.mult, op1=ALU.add), reads=[R_one1, R_tot], writes=[R_inc])
    S.op("dve", lambda e: e.tensor_copy(out=ne[:], in_=inc[:, (NB - 1) * 8:NB * 8]), reads=[R_inc], writes=[R_ne])
    MAGIC = 12582912.0
    S.op("dve", lambda e: e.tensor_scalar(out=padn[:], in0=ne[:], scalar1=1.0 / TS, scalar2=(TS - 1.0) / TS - 0.5 + 1.0 / (2 * TS), op0=ALU.mult, op1=ALU.add), reads=[R_ne], writes=[R_padn])
    S.op("dve", lambda e: e.tensor_scalar(out=padn[:], in0=padn[:], scalar1=MAGIC, scalar2=None, op0=ALU.add), reads=[R_padn], writes=[R_padn])
    S.op("dve", lambda e: e.tensor_scalar(out=padn[:], in0=padn[:], scalar1=-MAGIC, scalar2=float(TS), op0=ALU.add, op1=ALU.mult), reads=[R_padn], writes=[R_padn])
    S.op("dve", lambda e: e.tensor_tensor_scan(out=send[:], data0=one1[:, 0:1].to_broadcast([128, 8]), data1=padn[:], initial=0.0, op0=ALU.mult, op1=ALU.add),
         reads=[R_one1, R_padn], writes=[R_send])
    S.op("dve", lambda e: e.tensor_tensor(out=sstart[:], in0=send[:], in1=padn[:], op=ALU.subtract), reads=[R_send, R_padn], writes=[R_ss])
    S.op("dve", lambda e: e.tensor_tensor(out=slot[:], in0=rk[:, 0:NBE], in1=inc[:], op=ALU.add), reads=[rrk, R_inc], writes=[R_slot])
    S.op("dve", lambda e: e.tensor_tensor(out=slot[:], in0=slot[:], in1=tot[:], op=ALU.subtract), reads=[R_slot, R_tot], writes=[R_slot])
    slot3 = slot[:].rearrange("p (b e) -> p b e", e=8)
    tmp3 = tmp[:].rearrange("p (b e) -> p b e", e=8)
    S.op("dve", lambda e: e.tensor_tensor(out=slot3, in0=slot3, in1=sstart[:, :].unsqueeze(1).to_broadcast([128, NB, 8]), op=ALU.add), reads=[R_slot, R_ss], writes=[R_slot])
    S.op("dve", lambda e: e.tensor_tensor(out=tmp[:], in0=slot[:], in1=self_f, op=ALU.mult), reads=[R_slot, R_sel], writes=[R_tmp])
    S.op("dve", lambda e: e.tensor_reduce(out=sb_[:], in_=tmp3, axis=AX.X, op=ALU.max), reads=[R_tmp], writes=[R_sb])
    S.op("dve", lambda e: e.scalar_tensor_tensor(out=tmp[:], in0=self_f, scalar=-1.0e6, in1=slot[:], op0=ALU.mult, op1=ALU.add), reads=[R_sel, R_slot], writes=[R_tmp])
    S.op("dve", lambda e: e.tensor_scalar(out=tmp[:], in0=tmp[:], scalar1=1.0e6, scalar2=None, op0=ALU.add), reads=[R_tmp], writes=[R_tmp])
    S.op("dve", lambda e: e.tensor_reduce(out=sa[:], in_=tmp3, axis=AX.X, op=ALU.min), reads=[R_tmp], writes=[R_sa])
    S.op("dve", lambda e: e.tensor_tensor(out=tmp3, in0=slot3, in1=sa[:, :].unsqueeze(2).to_broadcast([128, NB, 8]), op=ALU.is_equal), reads=[R_slot, R_sa], writes=[R_tmp])
    S.op("dve", lambda e: e.tensor_tensor(out=tmp[:], in0=tmp[:], in1=gate_f, op=ALU.mult), reads=[R_tmp, R_ga], writes=[R_tmp])
    S.op("dve", lambda e: e.tensor_reduce(out=ga[:], in_=tmp3, axis=AX.X, op=ALU.add), reads=[R_tmp], writes=[R_gab])
    S.op("dve", lambda e: e.tensor_reduce(out=gb[:], in_=gate_all[:], axis=AX.X, op=ALU.add), reads=[R_ga], writes=[R_gab])
    S.op("dve", lambda e: e.tensor_tensor(out=gb[:], in0=gb[:], in1=ga[:], op=ALU.subtract), reads=[R_gab], writes=[R_gab])
    S.op("dve", lambda e: e.tensor_copy(out=sai[:], in_=sa[:]), reads=[R_sa], writes=[R_sai])
    S.op("dve", lambda e: e.tensor_copy(out=sbi[:], in_=sb_[:]), reads=[R_sb], writes=[R_sbi])
    S.op("dve", lambda e: e.tensor_tensor(out=cmp_[:], in0=send[:, :].unsqueeze(1).to_broadcast([128, NTILE, 8]),
                                          in1=tlim[:, :].unsqueeze(2).to_broadcast([128, NTILE, 8]), op=ALU.is_le), reads=[R_send, R_tlim], writes=[R_cmp])
    S.op("dve", lambda e: e.tensor_reduce(out=ei[:], in_=cmp_[:], axis=AX.X, op=ALU.add), reads=[R_cmp], writes=[R_ei])
    S.op("dve", lambda e: e.tensor_scalar(out=ei[:], in0=ei[:], scalar1=7.0, scalar2=float(SNG * 128), op0=ALU.min, op1=ALU.mult), reads=[R_ei], writes=[R_ei])
    S.op("dve", lambda e: e.tensor_tensor(out=wix[:], in0=wix[:], in1=ei[:, :].unsqueeze(2).to_broadcast([128, NTILE, SNG]), op=ALU.add), reads=[R_wix, R_ei], writes=[R_wix])
    S.op("dve", lambda e: e.tensor_copy(out=wixi[:], in_=wix[:]), reads=[R_wix], writes=[R_wixi])
    S.dma("sp", lambda e: e.dma_start(out=D_["slotA"][:, :], in_=sai[:]), reads=[R_sai])
    S.dma("sp", lambda e: e.dma_start(out=D_["slotB"][:, :], in_=sbi[:]), reads=[R_sbi])
    S.dma("sp", lambda e: e.dma_start(out=D_["gA"][:, :], in_=ga[:]), reads=[R_gab])
    S.dma("sp", lambda e: e.dma_start(out=D_["gB"][:, :], in_=gb[:]), reads=[R_gab])
    S.dma("sp", lambda e: e.dma_start(out=D_["widx"][:, :], in_=wixi[:].rearrange("p a b -> p (a b)")), reads=[R_wixi])
    xsort = D_["xsort"]
    zt = T.sb([128, 8, D], BF16, "zeros")
    R_zt, R_xz = RL(2)
    S.op("pool", lambda e: e.memset(zt[:], 0.0), writes=[R_zt])
    xz = xsort.rearrange("(n p) d -> p n d", p=128)
    for n0 in range(0, NSLOT // 128, 8):
        S.dma("sp", lambda e, n0=n0: e.dma_start(out=xz[:, n0:n0 + 8, :], in_=zt[:]), reads=[R_zt], writes=[R_xz])
    for blk in range(NB):
        for which, idx_t, r_idx in (("a", sai, R_sai), ("b", sbi, R_sbi)):
            S.dma("pool", lambda e, blk=blk, idx_t=idx_t: e.indirect_dma_start(out=xsort[:, :], out_offset=IOA(ap=idx_t[:, blk:blk + 1], axis=0),
                                                                             in_=xb_all[:, blk, :], in_offset=None), reads=[R_xb[blk], r_idx, R_xz])
    T.finish()


def stageES(P, layer, pump=None, pump_every=2):
    nc = P.nc
    T = Stage(nc, "es")
    S = T.S
    D_ = sp_scratch(P, layer)
    ident, R_id = load_ident(T, P, S)
    widx = T.sb([128, NTILE * SNG], I32, "widx")
    R_widx = Res()
    S.dma("sp", lambda e: e.dma_start(out=widx[:], in_=D_["widx"][:, :]), writes=[R_widx])
    xTs = Rot([T.sb([128, 8, TS], BF16, "xT") for _ in range(2)])
    accs = Rot([T.sb([128, 4, D], F32, "acc") for _ in range(2)])
    wgs = Rot([T.sb([128, 8, SGC * 128], BF16, "wg") for _ in range(2)])
    wus = Rot([T.sb([128, 8, SGC * 128], BF16, "wu") for _ in range(2)])
    wds = Rot([T.sb([128, SGC, D], BF16, "wd") for _ in range(2)])
    uTs = Rot([T.sb([128, SGC, 512], BF16, "uT") for _ in range(3)])
    sgs = Rot([T.sb([128, 512], F32, "sg") for _ in range(2)])
    xbs = Rot([T.sb([128, D], BF16, "xb") for _ in range(3)])
    pg_ = Rot([T.ps([128, 512], F32, "pg") for _ in range(2)])
    pu_ = Rot([T.ps([128, 512], F32, "pu") for _ in range(2)])
    pd_ = Rot([T.ps([128, 512], F32, "pd") for _ in range(2)])
    pT_ = Rot([T.ps([128, 8, 128], BF16, "pT") for _ in range(2)])
    xsort, ysort = D_["xsort"], D_["ysort"]
    if pump is not None:
        pump.attach(T, 4)
    items = [(ti, g) for ti in range(NTILE) for g in range(SNG)]
    cur = {}
    pend = {}

    def emit_gu(it):
        ti, g = it
        if g == 0:
            xT, rxT = xTs.next()
            acc, racc = accs.next()
            for b4 in range(4):
                xb, rxb = xbs.next()
                p, rp = pT_.next()
                r0 = ti * TS + b4 * 128
                S.dma("sp", lambda e, xb=xb, r0=r0: e.dma_start(out=xb[:], in_=xsort[r0:r0 + 128, :]), writes=[rxb])
                for k in range(8):
                    S.op("pe", lambda e, p=p, xb=xb, k=k: e.transpose(out=p[:, k, :], in_=xb[:, k * 128:(k + 1) * 128], identity=ident[:]), reads=[rxb, R_id], writes=[rp])
                S.op("dve", lambda e, p=p, b4=b4, xT=xT: e.tensor_copy(out=xT[:, :, b4 * 128:(b4 + 1) * 128], in_=p[:]), reads=[rp], writes=[rxT])
            cur[ti] = (xT, rxT, acc, racc)
        xT, rxT, acc, racc = cur[ti]
        col = ti * SNG + g
        wgb, rwg = wgs.next()
        wub, rwu = wus.next()
        wdb, rwd = wds.next()
        S.dma("pool", lambda e: e.indirect_dma_start(out=wgb[:].rearrange("p k n -> p (k n)"), out_offset=None, in_=D_["wgs"][:, :], in_offset=IOA(ap=widx[:, col:col + 1], axis=0)),
              reads=[R_widx], writes=[rwg])
        S.dma("pool", lambda e: e.indirect_dma_start(out=wub[:].rearrange("p k n -> p (k n)"), out_offset=None, in_=D_["wus"][:, :], in_offset=IOA(ap=widx[:, col:col + 1], axis=0)),
              reads=[R_widx], writes=[rwu])
        S.dma("pool", lambda e: e.indirect_dma_start(out=wdb[:].rearrange("p c n -> p (c n)"), out_offset=None, in_=D_["wds"][:, :], in_offset=IOA(ap=widx[:, col:col + 1], axis=0)),
              reads=[R_widx], writes=[rwd])
        uT, ruT = uTs.next()
        for c in range(SGC):
            pg, rpg = pg_.next()
            pu, rpu = pu_.next()
            for k in range(8):
                S.op("pe", lambda e, pg=pg, k=k, c=c: e.matmul(pg[:, :], lhsT=wgb[:, k, c * 128:(c + 1) * 128], rhs=xT[:, k, :], start=(k == 0), stop=(k == 7)),
                     reads=[rwg, rxT], writes=[rpg])
            for k in range(8):
                S.op("pe", lambda e, pu=pu, k=k, c=c: e.matmul(pu[:, :], lhsT=wub[:, k, c * 128:(c + 1) * 128], rhs=xT[:, k, :], start=(k == 0), stop=(k == 7)),
                     reads=[rwu, rxT], writes=[rpu])
            sg, rsg = sgs.next()
            S.op("act", lambda e, sg=sg, pg=pg: e.activation(out=sg[:], in_=pg[:, :], func=AF.Silu), reads=[rpg], writes=[rsg])
            S.op("dve", lambda e, c=c, sg=sg, pu=pu: e.tensor_tensor(out=uT[:, c, :], in0=pu[:, :], in1=sg[:], op=ALU.mult), reads=[rpu, rsg], writes=[ruT])
        pend[it] = (uT, ruT, wdb, rwd, acc, racc)

    def emit_down(it):
        ti, g = it
        uT, ruT, wdb, rwd, acc, racc = pend.pop(it)
        for tb in range(4):
            for hf in range(2):
                pd, rpd = pd_.next()
                for c in range(SGC):
                    S.op("pe", lambda e, pd=pd, c=c, tb=tb, hf=hf: e.matmul(pd[:, :], lhsT=uT[:, c, tb * 128:(tb + 1) * 128], rhs=wdb[:, c, hf * 512:(hf + 1) * 512],
                                                                          start=(c == 0), stop=(c == SGC - 1)), reads=[ruT, rwd], writes=[rpd])
                a_ = acc[:, tb, hf * 512:(hf + 1) * 512]
                if g == 0:
                    S.op("dve", lambda e, a_=a_, pd=pd: e.tensor_copy(out=a_, in_=pd[:, :]), reads=[rpd], writes=[racc])
                else:
                    S.op("dve", lambda e, a_=a_, pd=pd: e.tensor_tensor(out=a_, in0=pd[:, :], in1=a_, op=ALU.add), reads=[rpd, racc], writes=[racc])
        if g == SNG - 1:
            for tb in range(4):
                r0 = ti * TS + tb * 128
                S.dma("sp", lambda e, tb=tb, r0=r0: e.dma_start(out=ysort[r0:r0 + 128, :], in_=acc[:, tb, :]), reads=[racc])

    emit_gu(items[0])
    for ii, it in enumerate(items):
        if ii + 1 < len(items):
            emit_gu(items[ii + 1])
        if pump is not None and ii % pump_every == 0:
            pump.pump(S, 1)
        emit_down(it)
    T.finish()


def stageF(P, layer, x_src, x_dst):
    nc = P.nc
    T = Stage(nc, "f")
    S = T.S
    D_ = sp_scratch(P, layer)
    L = LNBufs(T, S, P, "ln_ffn_g", "ln_ffn_b", layer)
    sai = T.sb([128, NB], I32, "sai")
    sbi = T.sb([128, NB], I32, "sbi")
    ga = T.sb([128, NB], F32, "ga")
    gb = T.sb([128, NB], F32, "gb")
    R_sai, R_sbi, R_ga, R_gb = RL(4)
    S.dma("sp", lambda e: e.dma_start(out=sai[:], in_=D_["slotA"][:, :]), writes=[R_sai])
    S.dma("sp", lambda e: e.dma_start(out=sbi[:], in_=D_["slotB"][:, :]), writes=[R_sbi])
    S.dma("sp", lambda e: e.dma_start(out=ga[:], in_=D_["gA"][:, :]), writes=[R_ga])
    S.dma("sp", lambda e: e.dma_start(out=gb[:], in_=D_["gB"][:, :]), writes=[R_gb])
    yas = Rot([T.sb([128, D], F32, "ya") for _ in range(3)])
    ybs = Rot([T.sb([128, D], F32, "yb") for _ in range(3)])
    xfs = Rot([T.sb([128, D], F32, "xf") for _ in range(3)])
    ysort = D_["ysort"]
    for blk in range(NB):
        ya, rya = yas.next()
        yb, ryb = ybs.next()
        xf, rxf = xfs.next()
        S.dma("pool", lambda e, ya=ya, blk=blk: e.indirect_dma_start(out=ya[:], out_offset=None, in_=ysort[:, :], in_offset=IOA(ap=sai[:, blk:blk + 1], axis=0)),
              reads=[R_sai], writes=[rya])
        S.dma("pool", lambda e, yb=yb, blk=blk: e.indirect_dma_start(out=yb[:], out_offset=None, in_=ysort[:, :], in_offset=IOA(ap=sbi[:, blk:blk + 1], axis=0)),
              reads=[R_sbi], writes=[ryb])
        S.dma("sp", lambda e, xf=xf, blk=blk: e.dma_start(out=xf[:], in_=x_src[blk * 128:(blk + 1) * 128, :]), writes=[rxf])
        S.op("act", lambda e, ya=ya, blk=blk: e.activation(out=ya[:], in_=ya[:], func=AF.Copy, scale=ga[:, blk:blk + 1]), reads=[rya, R_ga], writes=[rya])
        S.op("dve", lambda e, ya=ya, yb=yb, blk=blk: e.scalar_tensor_tensor(out=ya[:], in0=yb[:], scalar=gb[:, blk:blk + 1], in1=ya[:], op0=ALU.mult, op1=ALU.add),
             reads=[rya, ryb, R_gb], writes=[rya])
        S.op("dve", lambda e, xf=xf, ya=ya: e.scalar_tensor_tensor(out=xf[:], in0=xf[:], scalar=DN_ALPHA, in1=ya[:], op0=ALU.mult, op1=ALU.add), reads=[rxf, rya], writes=[rxf])
        emit_ln(S, L, xf, rxf, x_dst[blk * 128:(blk + 1) * 128, :], gb_eng="dve")
    T.finish()


def build(n_layers=DEPTH, debug_out=(), stages="0ABCDWRE", first_layer=0, x_in=None):
    P = Prog(debug_out)
    y = P.out("y", [S_LEN, D], F32)
    if "0" in stages:
        stage0(P)
    pumps = {}
    if SPARSE and "W" in stages:
        for l in range(first_layer, first_layer + n_layers):
            if l % 2 == 1:
                pumps[l] = BgPump(P, l)
    for layer in range(first_layer, first_layer + n_layers):
        x_src = P.inp("x") if layer == 0 else P.xs(1)
        x_mid = P.xs(0)
        x_dst = y if layer == DEPTH - 1 else P.xs(1)
        if "A" in stages:
            stageA(P, layer, x_src)
        if "B" in stages:
            bp = pumps.get(layer) if layer % 2 == 1 else pumps.get(layer + 1)
            stageB(P, layer, pump=bp, pump_every=13)
        if "C" in stages:
            stageC(P, layer)
        if "D" in stages:
            stageD(P, layer, x_src, x_mid)
        moe = (layer % 2 == 1)
        if x_in is not None:
            x_mid = P.inp(x_in)
        if moe and SPARSE:
            if "W" in stages:
                stageW(P, pumps[layer])
            if "R" in stages:
                stageR(P, layer, x_mid)
            if "E" in stages:
                stageES(P, layer)
                stageF(P, layer, x_mid, x_dst)
        else:
            if moe and "R" in stages:
                stageE0(P, layer, x_mid)
            if "E" in stages:
                stageE(P, layer, x_mid, x_dst, moe)
    return P


def make_in_maps(P, inputs):
    consts = host_consts()
    maps = []
    for c in range(8):
        m = {}
        for name in P.used_inputs:
            if name == "x":
                m[name] = np.ascontiguousarray(inputs["x"][c])
            elif name == "positions":
                m[name] = np.ascontiguousarray(inputs["positions"][c].reshape(1, S_LEN)).astype(np.int32)
            elif name in consts:
                m[name] = consts[name]
            else:
                m[name] = np.ascontiguousarray(inputs[name])
        maps.append(m)
    return maps


def kernel(**inputs):
    P = build()
    maps = make_in_maps(P, inputs)
    res = run_bass_kernel_spmd(P.nc, maps, core_ids=list(range(8)))
    return np.stack([np.asarray(res.results[c]["y"]) for c in range(8)], axis=0).astype(np.float32)
```

```python
import math
from contextlib import ExitStack

import numpy as np
import concourse.bass as bass
import concourse.mybir as mybir
from concourse.bass_utils import run_bass_kernel_spmd

F32 = mybir.dt.float32
BF16 = mybir.dt.bfloat16
I32 = mybir.dt.int32
AF = mybir.ActivationFunctionType
ALU = mybir.AluOpType
AX = mybir.AxisListType

S_LEN = 4096
D = 1024
NB = 32
NT = 8
DEPTH = 4
IN_COLS = 6664
D_FF = 2816
N_EXP = 8
D_FFE = 3584
DN_ALPHA = (2 * DEPTH) ** 0.25
LN_EPS = 1e-5
RMS_EPS = 1e-6
C_QA, C_KA, C_VA, C_F, C_QB, C_KB, C_VB, C_GB, C_GT = 0, 512, 1024, 1536, 1544, 2056, 2568, 3592, 4616
NEG_BIG = -30000.0
VAW = 8 * 65
NTILE_H = 24
SNG_H = 7
SPARSE = True
C_LEVEL = 4
C_NCH = NB


class Res:
    __slots__ = ("last_w", "readers")

    def __init__(self):
        self.last_w = None
        self.readers = {}


def RL(n):
    return [Res() for _ in range(n)]


class Op:
    __slots__ = ("eng", "fn", "deps", "signal", "count", "is_dma", "key")

    def __init__(self, eng, fn, is_dma=False):
        self.eng = eng
        self.fn = fn
        self.deps = []
        self.signal = False
        self.count = 0
        self.is_dma = is_dma
        self.key = None


class Sched:
    ENGS = ("pe", "act", "dve", "pool", "sp")
    ROLL = 30000

    def __init__(self, nc, n_chan=10):
        self.nc = nc
        self.q = {e: [] for e in self.ENGS}
        self.n_chan = n_chan
        self.chan_rr = {"sp": 0, "act": 0, "pool": 0}
        self.chan_last = {}

    def _track(self, op, reads, writes):
        deps = []
        for r in reads:
            if r.last_w is not None:
                deps.append(r.last_w)
        for w in writes:
            if w.last_w is not None:
                deps.append(w.last_w)
            deps.extend(w.readers.values())
        rk = op.key if op.is_dma else op.eng
        for r in reads:
            r.readers[rk] = op
        for w in writes:
            w.last_w = op
            w.readers = {}
        seen = set()
        for d in deps:
            if d is op or id(d) in seen:
                continue
            seen.add(id(d))
            if (not d.is_dma) and (not op.is_dma) and d.eng == "pe" and op.eng == "pe":
                continue
            op.deps.append(d)
            d.signal = True

    def op(self, eng, fn, reads=(), writes=()):
        o = Op(eng, fn)
        self._track(o, reads, writes)
        self.q[eng].append(o)
        return o

    def dma(self, queue, fn, reads=(), writes=()):
        o = Op(queue, fn, is_dma=True)
        ci = self.chan_rr[queue]
        self.chan_rr[queue] = (ci + 1) % self.n_chan
        o.key = ("c", queue, ci)
        prev = self.chan_last.get(o.key)
        self._track(o, reads, writes)
        if prev is not None and all(d is not prev for d in o.deps):
            o.deps.append(prev)
        self.chan_last[o.key] = o
        o.signal = True
        self.q[queue].append(o)
        return o

    def emit(self, stack, tag):
        nc = self.nc
        ccount = {}
        keys = []
        for e in self.ENGS:
            cnt = 0
            si = 0
            for o in self.q[e]:
                if o.is_dma:
                    c = ccount.get(o.key, 0) + 16
                    ccount[o.key] = c
                    o.count = c
                elif o.signal:
                    if cnt >= self.ROLL:
                        si += 1
                        cnt = 0
                    cnt += 1
                    o.count = cnt
                    o.key = ("e", e, si)
            for i in range(si + 1):
                keys.append(("e", e, i))
        keys.extend(ccount.keys())
        sems = {k: nc.alloc_semaphore(name="%s_%s_%s%d" % (tag, k[0], k[1], k[2])) for k in keys}
        bstack = ExitStack()
        block = bstack.enter_context(nc.Block())

        def run(ename, eng):
            waited = {}
            for o in self.q[ename]:
                need = {}
                for d in o.deps:
                    if d.count > need.get(d.key, 0):
                        need[d.key] = d.count
                for key, val in need.items():
                    if waited.get(key, 0) >= val:
                        continue
                    eng.wait_ge(sems[key], val)
                    waited[key] = val
                ins = o.fn(eng)
                if o.is_dma:
                    ins.then_inc(sems[o.key], 16)
                elif o.signal:
                    ins.then_inc(sems[o.key], 1)

        @block.tensor
        def _(e):
            run("pe", e)

        @block.scalar
        def _(e):
            run("act", e)

        @block.vector
        def _(e):
            run("dve", e)

        @block.gpsimd
        def _(e):
            run("pool", e)

        @block.sync
        def _(e):
            run("sp", e)
            for key, c in ccount.items():
                e.wait_ge(sems[key], c)

        bstack.close()
        nc.clear_and_free_semaphores(list(sems.values()))
        nc.all_engine_barrier()


class Stage:
    _uid = 0

    def __init__(self, nc, name):
        self.nc = nc
        self.name = name
        self.st = ExitStack()
        self.S = Sched(nc)
        Stage._uid += 1
        self.uid = Stage._uid
        self.n = 0

    def sb(self, shape, dt, nm="t"):
        self.n += 1
        return self.st.enter_context(self.nc.sbuf_tensor("%s%d_%s%d" % (self.name, self.uid, nm, self.n), list(shape), dt))

    def ps(self, shape, dt, nm="p"):
        self.n += 1
        return self.st.enter_context(self.nc.psum_tensor("%s%d_%s%d" % (self.name, self.uid, nm, self.n), list(shape), dt))

    def finish(self):
        self.S.emit(self.st, "%s%d" % (self.name, self.uid))
        self.st.close()


class Rot:
    def __init__(self, tiles):
        self.tiles = tiles
        self.res = RL(len(tiles))
        self.i = -1

    def next(self):
        self.i = (self.i + 1) % len(self.tiles)
        return self.tiles[self.i], self.res[self.i]


def host_consts():
    c = {}
    c["c_ident"] = np.eye(128, dtype=np.float32)
    r = np.arange(128)
    c["c_mq"] = np.where(r[:, None] > r[None, :], NEG_BIG, 0.0).astype(np.float32)
    half = 32
    inv_freq = (np.float32(10000.0) ** (-np.arange(half, dtype=np.float32) / np.float32(half))).astype(np.float32)
    c["c_invf"] = np.concatenate([inv_freq, inv_freq]).reshape(64, 1).astype(np.float32)
    c["c_sgn"] = np.concatenate([-np.ones(32), np.ones(32)]).reshape(64, 1).astype(np.float32)
    lg = np.log(1.0 - 2.0 ** (-5.0 - np.arange(8, dtype=np.float64)))
    j = np.arange(128, dtype=np.float64)
    dtp = np.exp(-(j[:, None, None] + 1.0) * lg[None, :, None]) * (j[:, None, None] <= j[None, None, :])
    perm = [2 * (hh % 4) + (hh // 4) for hh in range(8)]
    c["c_dtp"] = np.ascontiguousarray(dtp[:, perm, :]).astype(np.float32)
    c["c_qdec"] = np.ascontiguousarray(np.exp((j[:, None] + 1.0) * lg[None, :])[:, perm]).astype(np.float32)
    c["c_kdec"] = np.exp((127.0 - j[:, None]) * lg[None, :]).astype(np.float32)
    cd = np.zeros((128, 4, 128), dtype=np.float64)
    for two in range(2):
        for hp in range(4):
            cd[two * 64:(two + 1) * 64, hp, :] = math.exp(128.0 * lg[2 * hp + two])
    c["c_cd"] = cd.astype(np.float32)
    c["c_tri"] = (r[:, None] < r[None, :]).astype(np.float32)
    c["c_tlim"] = np.tile((np.arange(NTILE_H, dtype=np.float32) * 512.0)[None, :], (128, 1)).astype(np.float32)
    wb = np.arange(128, dtype=np.float32)[:, None, None] + (np.arange(SNG_H, dtype=np.float32) * 128.0)[None, None, :] + np.zeros((1, NTILE_H, 1), np.float32)
    c["c_wbase"] = np.ascontiguousarray(wb).astype(np.float32)
    return c


CONST_SHAPES = {"c_ident": [128, 128], "c_mq": [128, 128], "c_invf": [64, 1], "c_sgn": [64, 1],
                "c_dtp": [128, 8, 128], "c_qdec": [128, 8], "c_kdec": [128, 8], "c_cd": [128, 4, 128],
                "c_tri": [128, 128], "c_tlim": [128, NTILE_H], "c_wbase": [128, NTILE_H, SNG_H]}

INPUT_SHAPES = {
    "x": ([S_LEN, D], F32), "positions": ([1, S_LEN], I32),
    "w_in": ([DEPTH, D, IN_COLS], F32), "b_forget": ([DEPTH, 8], F32), "ret_norm_g": ([DEPTH, D], F32),
    "w_branch_fox": ([DEPTH, 512, D], F32), "w_branch_ret": ([DEPTH, D, D], F32), "w_out": ([DEPTH, D, D], F32),
    "ln_mix_g": ([DEPTH, D], F32), "ln_mix_b": ([DEPTH, D], F32),
    "ffn_w_gate": ([2, D, D_FF], F32), "ffn_w_up": ([2, D, D_FF], F32), "ffn_w_down": ([2, D_FF, D], F32),
    "moe_router": ([2, D, 8], F32), "moe_w_gate": ([2, 8, D, D_FFE], F32), "moe_w_up": ([2, 8, D, D_FFE], F32),
    "moe_w_down": ([2, 8, D_FFE, D], F32), "ln_ffn_g": ([DEPTH, D], F32), "ln_ffn_b": ([DEPTH, D], F32),
}


class Prog:
    def __init__(self, debug_out=()):
        self.nc = bass.Bass("TRN2", target_bir_lowering=False)
        self.t = {}
        self.used_inputs = []
        self.debug_out = set(debug_out)
        self.outputs = []

    def inp(self, name):
        if name not in self.t:
            if name in INPUT_SHAPES:
                shp, dt = INPUT_SHAPES[name]
            else:
                shp, dt = CONST_SHAPES[name], F32
            self.t[name] = self.nc.dram_tensor(name, list(shp), dt, kind="ExternalInput").ap()
            self.used_inputs.append(name)
        return self.t[name]

    def scratch(self, name, shape, dt):
        if name not in self.t:
            if name in self.debug_out:
                self.t[name] = self.nc.dram_tensor(name, list(shape), dt, kind="ExternalOutput").ap()
                self.outputs.append(name)
            else:
                self.t[name] = self.nc.dram_tensor(name, list(shape), dt).ap()
        return self.t[name]

    def out(self, name, shape, dt):
        if name not in self.t:
            self.t[name] = self.nc.dram_tensor(name, list(shape), dt, kind="ExternalOutput").ap()
            self.outputs.append(name)
        return self.t[name]

    def qaT(self): return self.scratch("qaT", [8, 70, S_LEN], BF16)
    def kaT(self): return self.scratch("kaT", [8, 70, S_LEN], BF16)
    def vA(self): return self.scratch("vA", [S_LEN, VAW], BF16)
    def qbT(self): return self.scratch("qbT", [512, S_LEN], BF16)
    def kbT(self): return self.scratch("kbT", [512, S_LEN], BF16)
    def vB(self): return self.scratch("vB", [S_LEN, D], BF16)
    def sgB(self): return self.scratch("sgB", [S_LEN, D], BF16)
    def gT(self): return self.scratch("gT", [2 * D, S_LEN], BF16)
    def oaT(self): return self.scratch("oaT", [512, S_LEN], BF16)
    def obT(self): return self.scratch("obT", [D, S_LEN], BF16)
    def cosT(self): return self.scratch("cosT", [64, S_LEN], F32)
    def sinT(self): return self.scratch("sinT", [64, S_LEN], F32)
    def xs(self, i): return self.scratch("xs%d" % i, [S_LEN, D], F32)


def load_ident(T, P, S):
    idt = T.sb([128, 128], BF16, "ident")
    r = Res()
    src = P.inp("c_ident")
    S.dma("pool", lambda e: e.dma_start(out=idt[:], in_=src[:, :]), writes=[r])
    return idt, r


def stage0(P):
    nc = P.nc
    T = Stage(nc, "s0")
    S = T.S
    pos = P.inp("positions")
    posi = T.sb([64, S_LEN], I32)
    ang = T.sb([64, S_LEN], F32)
    kk = T.sb([64, S_LEN], F32)
    r1 = T.sb([64, S_LEN], F32)
    r2 = T.sb([64, S_LEN], F32)
    mm = T.sb([64, S_LEN], F32)
    tb = T.sb([64, S_LEN], F32)
    invf = T.sb([64, 1], F32)
    sgn = T.sb([64, 1], F32)
    cst = T.sb([8, 3, 512], BF16)
    cstn = T.sb([8, 3, 512], BF16)
    R_posi, R_ang, R_kk, R_r1, R_r2, R_mm, R_tb, R_invf, R_sgn, R_cst = RL(10)
    c_invf, c_sgn = P.inp("c_invf"), P.inp("c_sgn")
    cosT, sinT = P.cosT(), P.sinT()
    S.dma("sp", lambda e: e.dma_start(out=posi[:], in_=pos[0:1, :].partition_broadcast(64)), writes=[R_posi])
    S.dma("sp", lambda e: e.dma_start(out=invf[:], in_=c_invf[:, :]), writes=[R_invf])
    S.dma("sp", lambda e: e.dma_start(out=sgn[:], in_=c_sgn[:, :]), writes=[R_sgn])
    TWO_PI = 2.0 * math.pi
    C1 = 6.28125
    C2 = float(np.float32(np.round((TWO_PI - C1) * 2.0 ** 20) / 2.0 ** 20))
    C3 = float(np.float32(TWO_PI - C1 - C2))
    MAGIC = 12582912.0
    PI_LO = float(np.nextafter(np.float32(math.pi), np.float32(0.0)))
    S.op("dve", lambda e: e.tensor_copy(out=ang[:], in_=posi[:]), reads=[R_posi], writes=[R_ang])
    S.op("dve", lambda e: e.tensor_scalar(out=ang[:], in0=ang[:], scalar1=invf[:, 0:1], scalar2=None, op0=ALU.mult),
         reads=[R_ang, R_invf], writes=[R_ang])
    S.op("dve", lambda e: e.tensor_scalar(out=kk[:], in0=ang[:], scalar1=1.0 / TWO_PI, scalar2=MAGIC, op0=ALU.mult, op1=ALU.add),
         reads=[R_ang], writes=[R_kk])
    S.op("dve", lambda e: e.tensor_scalar(out=kk[:], in0=kk[:], scalar1=-MAGIC, scalar2=None, op0=ALU.add),
         reads=[R_kk], writes=[R_kk])
    S.op("dve", lambda e: e.scalar_tensor_tensor(out=r1[:], in0=kk[:], scalar=-C1, in1=ang[:], op0=ALU.mult, op1=ALU.add),
         reads=[R_kk, R_ang], writes=[R_r1])
    S.op("dve", lambda e: e.scalar_tensor_tensor(out=r1[:], in0=kk[:], scalar=-C2, in1=r1[:], op0=ALU.mult, op1=ALU.add),
         reads=[R_kk, R_r1], writes=[R_r1])
    S.op("dve", lambda e: e.scalar_tensor_tensor(out=r1[:], in0=kk[:], scalar=-C3, in1=r1[:], op0=ALU.mult, op1=ALU.add),
         reads=[R_kk, R_r1], writes=[R_r1])
    S.op("dve", lambda e: e.tensor_scalar(out=r2[:], in0=r1[:], scalar1=0.5 * math.pi, scalar2=None, op0=ALU.add),
         reads=[R_r1], writes=[R_r2])
    S.op("dve", lambda e: e.tensor_scalar(out=mm[:], in0=r2[:], scalar1=math.pi, scalar2=None, op0=ALU.is_gt),
         reads=[R_r2], writes=[R_mm])
    S.op("dve", lambda e: e.scalar_tensor_tensor(out=r2[:], in0=mm[:], scalar=-TWO_PI, in1=r2[:], op0=ALU.mult, op1=ALU.add),
         reads=[R_mm, R_r2], writes=[R_r2])
    S.op("dve", lambda e: e.tensor_scalar(out=r1[:], in0=r1[:], scalar1=PI_LO, scalar2=-PI_LO, op0=ALU.min, op1=ALU.max),
         reads=[R_r1], writes=[R_r1])
    S.op("dve", lambda e: e.tensor_scalar(out=r2[:], in0=r2[:], scalar1=PI_LO, scalar2=-PI_LO, op0=ALU.min, op1=ALU.max),
         reads=[R_r2], writes=[R_r2])
    S.op("act", lambda e: e.activation(out=tb[:], in_=r1[:], func=AF.Sin, scale=sgn[:, 0:1]), reads=[R_r1, R_sgn], writes=[R_tb])
    S.dma("sp", lambda e: e.dma_start(out=sinT[:, :], in_=tb[:]), reads=[R_tb])
    S.op("act", lambda e: e.activation(out=mm[:], in_=r2[:], func=AF.Sin), reads=[R_r2], writes=[R_mm])
    S.dma("sp", lambda e: e.dma_start(out=cosT[:, :], in_=mm[:]), reads=[R_mm])
    S.op("pool", lambda e: e.memset(cst[:], 1.0), writes=[R_cst])
    S.op("pool", lambda e: e.memset(cstn[:], -1.0), writes=[R_cst])
    qaT, kaT = P.qaT(), P.kaT()
    for t in range(NT):
        S.dma("sp", lambda e, t=t: e.dma_start(out=qaT[:, 67:70, t * 512:(t + 1) * 512], in_=cst[:]), reads=[R_cst])
        S.dma("sp", lambda e, t=t: e.dma_start(out=kaT[:, 64:67, t * 512:(t + 1) * 512], in_=cstn[:]), reads=[R_cst])
    T.finish()


def emit_build_xT(T, S, src, tok0, nblk, xT, R_xT, ident, R_id, xTf=None, R_xTf=None, identf=None):
    xf = Rot([T.sb([128, D], F32, "xf") for _ in range(2)])
    xb = Rot([T.sb([128, D], BF16, "xb") for _ in range(2)])
    pT = Rot([T.ps([128, 8, 128], BF16, "pT") for _ in range(2)])
    for blk in range(nblk):
        f, rf = xf.next()
        b, rb = xb.next()
        p, rp = pT.next()
        r0 = tok0 + blk * 128
        S.dma("sp", lambda e, f=f, r0=r0: e.dma_start(out=f[:], in_=src[r0:r0 + 128, :]), writes=[rf])
        S.op("act", lambda e, f=f, b=b: e.copy(out=b[:], in_=f[:]), reads=[rf], writes=[rb])
        for k in range(8):
            S.op("pe", lambda e, p=p, b=b, k=k: e.transpose(out=p[:, k, :], in_=b[:, k * 128:(k + 1) * 128], identity=ident[:]),
                 reads=[rb, R_id], writes=[rp])
        S.op("dve", lambda e, p=p, blk=blk: e.tensor_copy(out=xT[:, :, blk * 128:(blk + 1) * 128], in_=p[:]),
             reads=[rp], writes=[R_xT[blk]])


def stageA(P, layer, x_src):
    nc = P.nc
    w_in = P.inp("w_in")
    OUT = ExitStack()
    T = Stage(nc, "a1")
    S = T.S
    xT = OUT.enter_context(nc.sbuf_tensor("xT_l%d" % layer, [128, 8, S_LEN], BF16))
    R_xT = RL(NB)
    ident, R_id = load_ident(T, P, S)
    emit_build_xT(T, S, x_src, 0, NB, xT, R_xT, ident, R_id)
    wsl = Rot([T.sb([128, 8, 512], BF16, "w") for _ in range(3)])
    stg = Rot([T.sb([128, S_LEN], BF16, "stg") for _ in range(2)])
    pp = Rot([T.ps([128, 512], F32, "pp") for _ in range(4)])

    def load_w(c0, width=512):
        w, rw = wsl.next()
        S.dma("pool", lambda e: e.dma_start(out=w[:, :, 0:width], in_=w_in[layer, :, c0:c0 + width].rearrange("(k p) n -> p k n", p=128)),
              writes=[rw])
        return w, rw

    def fm_project(w, rw, col, M, evac):
        for t in range(NT):
            p, rp = pp.next()
            for k in range(8):
                S.op("pe", lambda e, p=p, k=k, t=t: e.matmul(p[0:M, :], lhsT=w[:, k, col:col + M], rhs=xT[:, k, t * 512:(t + 1) * 512],
                                                            start=(k == 0), stop=(k == 7)),
                     reads=[rw] + R_xT[4 * t:4 * t + 4], writes=[rp])
            evac(p, rp, t)

    qaT, kaT, gT = P.qaT(), P.kaT(), P.gT()
    for which, c0, dst in (("q", C_QA, qaT), ("k", C_KA, kaT)):
        w, rw = load_w(c0)
        for hp in range(4):
            sg, rsg = stg.next()

            def evac(p, rp, t, sg=sg, rsg=rsg, which=which):
                if which == "q":
                    S.op("act", lambda e: e.mul(out=sg[:, t * 512:(t + 1) * 512], in_=p[:, :], mul=0.125), reads=[rp], writes=[rsg])
                else:
                    S.op("dve", lambda e: e.tensor_copy(out=sg[:, t * 512:(t + 1) * 512], in_=p[:, :]), reads=[rp], writes=[rsg])
            fm_project(w, rw, hp * 128, 128, evac)
            S.dma("sp", lambda e, sg=sg, hp=hp, dst=dst: e.dma_start(out=dst[2 * hp, 0:64, :], in_=sg[0:64, :]), reads=[rsg])
            S.dma("sp", lambda e, sg=sg, hp=hp, dst=dst: e.dma_start(out=dst[2 * hp + 1, 0:64, :], in_=sg[64:128, :]), reads=[rsg])
    for gq in range(4):
        w, rw = load_w(C_GT + gq * 512)
        for mc in range(4):
            sg, rsg = stg.next()

            def evac(p, rp, t, sg=sg, rsg=rsg):
                S.op("act", lambda e: e.activation(out=sg[:, t * 512:(t + 1) * 512], in_=p[:, :], func=AF.Sigmoid), reads=[rp], writes=[rsg])
            fm_project(w, rw, mc * 128, 128, evac)
            row0 = gq * 512 + mc * 128
            S.dma("sp", lambda e, sg=sg, row0=row0: e.dma_start(out=gT[row0:row0 + 128, :], in_=sg[:, :]), reads=[rsg])
    Fb = T.sb([8, S_LEN], F32, "Fb")
    Cb = T.sb([8, S_LEN], F32, "Cb")
    CS = T.sb([8, 3, S_LEN], BF16, "CS")
    nb = T.sb([8, 1], F32, "nb")
    one8 = T.sb([8, 1], F32, "one8")
    R_Fb, R_Cb, R_CS, R_nb, R_one = RL(5)
    b_forget = P.inp("b_forget")
    S.dma("sp", lambda e: e.dma_start(out=nb[:], in_=b_forget[layer:layer + 1, :].rearrange("o h -> h o")), writes=[R_nb])
    S.op("dve", lambda e: e.tensor_scalar(out=nb[:], in0=nb[:], scalar1=-1.0, scalar2=None, op0=ALU.mult), reads=[R_nb], writes=[R_nb])
    S.op("dve", lambda e: e.memset(one8[:], 1.0), writes=[R_one])
    w, rw = load_w(C_F, 8)

    def evac_f(p, rp, t):
        S.op("act", lambda e: e.activation(out=Fb[:, t * 512:(t + 1) * 512], in_=p[0:8, :], func=AF.Exp, bias=nb[:, 0:1], scale=-1.0),
             reads=[rp, R_nb], writes=[R_Fb])
    fm_project(w, rw, 0, 8, evac_f)
    S.op("act", lambda e: e.activation(out=Fb[:], in_=Fb[:], func=AF.Ln, bias=1.0), reads=[R_Fb], writes=[R_Fb])
    S.op("dve", lambda e: e.tensor_tensor_scan(out=Cb[:], data0=one8[:, 0:1].to_broadcast([8, S_LEN]), data1=Fb[:], initial=0.0,
                                               op0=ALU.mult, op1=ALU.add), reads=[R_one, R_Fb], writes=[R_Cb])
    S.op("dve", lambda e: e.tensor_copy(out=CS[:, 0, :], in_=Cb[:]), reads=[R_Cb], writes=[R_CS])
    S.op("dve", lambda e: e.tensor_tensor(out=Fb[:], in0=Cb[:], in1=CS[:, 0, :], op=ALU.subtract), reads=[R_Cb, R_CS], writes=[R_Fb])
    S.op("dve", lambda e: e.tensor_copy(out=CS[:, 1, :], in_=Fb[:]), reads=[R_Fb], writes=[R_CS])
    S.op("dve", lambda e: e.tensor_tensor(out=Cb[:], in0=Fb[:], in1=CS[:, 1, :], op=ALU.subtract), reads=[R_Fb, R_CS], writes=[R_Cb])
    S.op("dve", lambda e: e.tensor_copy(out=CS[:, 2, :], in_=Cb[:]), reads=[R_Cb], writes=[R_CS])
    S.dma("sp", lambda e: e.dma_start(out=qaT[:, 64:67, :], in_=CS[:]), reads=[R_CS])
    S.dma("sp", lambda e: e.dma_start(out=kaT[:, 67:70, :], in_=CS[:]), reads=[R_CS])
    T.finish()

    T = Stage(nc, "a2")
    S = T.S
    for r in R_xT:
        r.last_w = None
        r.readers = {}
    cosb = T.sb([128, S_LEN], F32, "cos")
    sinb = T.sb([128, S_LEN], F32, "sin")
    R_cos, R_sin = RL(2)
    cosT, sinT = P.cosT(), P.sinT()
    for two in range(2):
        S.dma("sp", lambda e, two=two: e.dma_start(out=cosb[two * 64:(two + 1) * 64, :], in_=cosT[:, :]), writes=[R_cos])
        S.dma("sp", lambda e, two=two: e.dma_start(out=sinb[two * 64:(two + 1) * 64, :], in_=sinT[:, :]), writes=[R_sin])
    wsl = Rot([T.sb([128, 8, 512], BF16, "w") for _ in range(2)])
    wsw = Rot([T.sb([128, 8, 512], BF16, "wsw") for _ in range(2)])
    stg = Rot([T.sb([128, S_LEN], BF16, "stg") for _ in range(2)])
    t1s = Rot([T.sb([128, 512], F32, "t1") for _ in range(3)])
    t2s = Rot([T.sb([128, 512], F32, "t2") for _ in range(3)])
    ppa = Rot([T.ps([128, 512], F32, "ppa") for _ in range(3)])
    ppb = Rot([T.ps([128, 512], F32, "ppb") for _ in range(3)])
    qbT, kbT = P.qbT(), P.kbT()
    for which, c0, dst, scl in (("q", C_QB, qbT, 1.0), ("k", C_KB, kbT, 0.125)):
        w, rw = wsl.next()
        ws, rws = wsw.next()
        S.dma("pool", lambda e, w=w, c0=c0: e.dma_start(out=w[:], in_=w_in[layer, :, c0:c0 + 512].rearrange("(k p) n -> p k n", p=128)), writes=[rw])
        wv = w[:].rearrange("p k (h two j) -> p k h two j", two=2, j=32)
        wsv = ws[:].rearrange("p k (h two j) -> p k h two j", two=2, j=32)
        for k in range(8):
            S.op("pool", lambda e, k=k, wv=wv, wsv=wsv: e.tensor_copy(out=wsv[:, k, :, 0, :], in_=wv[:, k, :, 1, :]), reads=[rw], writes=[rws])
            S.op("pool", lambda e, k=k, wv=wv, wsv=wsv: e.tensor_copy(out=wsv[:, k, :, 1, :], in_=wv[:, k, :, 0, :]), reads=[rw], writes=[rws])
        for hp in range(4):
            sg, rsg = stg.next()
            for t in range(NT):
                pa, rpa = ppa.next()
                pb, rpb = ppb.next()
                for k in range(8):
                    S.op("pe", lambda e, pa=pa, k=k, t=t, w=w, hp=hp: e.matmul(pa[:, :], lhsT=w[:, k, hp * 128:(hp + 1) * 128], rhs=xT[:, k, t * 512:(t + 1) * 512],
                                                                               start=(k == 0), stop=(k == 7)),
                         reads=[rw] + R_xT[4 * t:4 * t + 4], writes=[rpa])
                for k in range(8):
                    S.op("pe", lambda e, pb=pb, k=k, t=t, ws=ws, hp=hp: e.matmul(pb[:, :], lhsT=ws[:, k, hp * 128:(hp + 1) * 128], rhs=xT[:, k, t * 512:(t + 1) * 512],
                                                                                 start=(k == 0), stop=(k == 7)),
                         reads=[rws] + R_xT[4 * t:4 * t + 4], writes=[rpb])
                t1, rt1 = t1s.next()
                t2, rt2 = t2s.next()
                S.op("dve", lambda e, t1=t1, pa=pa, t=t, scl=scl: e.scalar_tensor_tensor(out=t1[:], in0=pa[:, :], scalar=scl, in1=cosb[:, t * 512:(t + 1) * 512],
                                                                                         op0=ALU.mult, op1=ALU.mult), reads=[rpa, R_cos], writes=[rt1])
                S.op("dve", lambda e, t2=t2, pb=pb, t=t, scl=scl: e.scalar_tensor_tensor(out=t2[:], in0=pb[:, :], scalar=scl, in1=sinb[:, t * 512:(t + 1) * 512],
                                                                                         op0=ALU.mult, op1=ALU.mult), reads=[rpb, R_sin], writes=[rt2])
                S.op("pool", lambda e, t1=t1, t2=t2, sg=sg, t=t: e.tensor_tensor(out=sg[:, t * 512:(t + 1) * 512], in0=t1[:], in1=t2[:], op=ALU.add),
                     reads=[rt1, rt2], writes=[rsg])
            S.dma("sp", lambda e, sg=sg, hp=hp, dst=dst: e.dma_start(out=dst[hp * 128:(hp + 1) * 128, :], in_=sg[:, :]), reads=[rsg])
    T.finish()

    T = Stage(nc, "a3")
    S = T.S
    for r in R_xT:
        r.last_w = None
        r.readers = {}
    wts = []
    for c0 in (C_VA, C_VB, C_VB + 512, C_GB, C_GB + 512):
        w = T.sb([128, 8, 512], BF16, "w")
        rw = Res()
        S.dma("pool", lambda e, w=w, c0=c0: e.dma_start(out=w[:], in_=w_in[layer, :, c0:c0 + 512].rearrange("(k p) n -> p k n", p=128)), writes=[rw])
        wts.append((w, rw))
    vas = Rot([T.sb([128, 8, 65], BF16, "vas") for _ in range(2)])
    vbs = Rot([T.sb([128, D], BF16, "vbs") for _ in range(2)])
    sgs = Rot([T.sb([128, D], BF16, "sgs") for _ in range(2)])
    pp = Rot([T.ps([128, 512], F32, "pp") for _ in range(6)])
    for tl, rs in zip(vas.tiles, vas.res):
        S.op("pool", lambda e, tl=tl: e.memset(tl[:], 1.0), writes=[rs])
    vA, vB, sgB = P.vA(), P.vB(), P.sgB()
    for blk in range(NB):
        va, rva = vas.next()
        vb, rvb = vbs.next()
        sg, rsg = sgs.next()
        for gi, (w, rw) in enumerate(wts):
            p, rp = pp.next()
            for k in range(8):
                S.op("pe", lambda e, p=p, k=k, w=w, blk=blk: e.matmul(p[:, :], lhsT=xT[:, k, blk * 128:(blk + 1) * 128], rhs=w[:, k, :],
                                                                     start=(k == 0), stop=(k == 7)),
                     reads=[rw, R_xT[blk]], writes=[rp])
            if gi == 0:
                S.op("dve", lambda e, p=p, va=va: e.tensor_copy(out=va[:, :, 0:64], in_=p[:, :].rearrange("p (h d) -> p h d", d=64)), reads=[rp], writes=[rva])
            elif gi in (1, 2):
                o = (gi - 1) * 512
                S.op("dve", lambda e, p=p, vb=vb, o=o: e.tensor_copy(out=vb[:, o:o + 512], in_=p[:, :]), reads=[rp], writes=[rvb])
            else:
                o = (gi - 3) * 512
                S.op("act", lambda e, p=p, sg=sg, o=o: e.activation(out=sg[:, o:o + 512], in_=p[:, :], func=AF.Silu), reads=[rp], writes=[rsg])
        r0 = blk * 128
        S.dma("sp", lambda e, va=va, r0=r0: e.dma_start(out=vA[r0:r0 + 128, :], in_=va[:].rearrange("p h d -> p (h d)")), reads=[rva])
        S.dma("sp", lambda e, vb=vb, r0=r0: e.dma_start(out=vB[r0:r0 + 128, :], in_=vb[:]), reads=[rvb])
        S.dma("sp", lambda e, sg=sg, r0=r0: e.dma_start(out=sgB[r0:r0 + 128, :], in_=sg[:]), reads=[rsg])
    T.finish()
    OUT.close()


def stageB(P, layer, pump=None, pump_every=14):
    nc = P.nc
    T = Stage(nc, "b")
    S = T.S
    qaT, kaT, vA, oaT = P.qaT(), P.kaT(), P.vA(), P.oaT()
    identb, R_id = load_ident(T, P, S)
    mqb = T.sb([128, 128], BF16, "mq")
    R_mq = Res()
    c_mq = P.inp("c_mq")
    S.dma("pool", lambda e: e.dma_start(out=mqb[:], in_=c_mq[:, :]), writes=[R_mq])
    onesf = T.sb([1, 64], F32, "ones")
    R_ones = Res()
    S.op("dve", lambda e: e.memset(onesf[:], 1.0), writes=[R_ones])
    vt = T.sb([128, NB, VAW], BF16, "vt")
    R_vt = RL(4)
    vsrc = vA.rearrange("(b p) c -> p b c", p=128)
    for g in range(4):
        S.dma("sp", lambda e, g=g: e.dma_start(out=vt[:, g * 8:(g + 1) * 8, :], in_=vsrc[:, g * 8:(g + 1) * 8, :]), writes=[R_vt[g]])
    qs = Rot([T.sb([70, S_LEN], BF16, "q") for _ in range(2)])
    ks = Rot([T.sb([70, S_LEN], BF16, "k") for _ in range(2)])
    oas = Rot([T.sb([64, S_LEN], BF16, "oa") for _ in range(2)])
    pts = Rot([T.sb([128, 512], BF16, "pt") for _ in range(5)])
    rls = Rot([T.sb([1, 512], F32, "rl") for _ in range(2)])
    bcs = Rot([T.sb([64, 512], F32, "bcs") for _ in range(2)])
    s_ps = Rot([T.ps([128, 512], F32, "s") for _ in range(4)])
    o_ps = Rot([T.ps([128, 512], F32, "o") for _ in range(2)])
    bc_ps = Rot([T.ps([128, 512], F32, "bc") for _ in range(1)])

    heads = {}

    def load_head(h):
        q, rq = qs.next()
        k, rk = ks.next()
        S.dma("sp", lambda e: e.dma_start(out=q[:], in_=qaT[h, :, :]), writes=[rq])
        S.dma("sp", lambda e: e.dma_start(out=k[:], in_=kaT[h, :, :]), writes=[rk])
        heads[h] = (q, rq, k, rk)

    units = [(h, I, j) for h in range(8) for I in range(NT) for j in range(4 * I + 4)]
    st = {}
    tiles = {}

    def emit_qk(u):
        h, I, j = u
        if I == 0 and j == 0:
            if h == 0:
                load_head(0)
            if h + 1 < 8:
                load_head(h + 1)
        q, rq, k, rk = heads[h]
        m = j - 4 * I
        c0 = 128 * m if m > 0 else 0
        sp, rsp = s_ps.next()
        S.op("pe", lambda e: e.matmul(sp[:, c0:512], lhsT=k[0:70, j * 128:(j + 1) * 128], rhs=q[0:70, I * 512 + c0:(I + 1) * 512],
                                      start=True, stop=(m < 0)), reads=[rq, rk], writes=[rsp])
        if m >= 0:
            S.op("pe", lambda e: e.matmul(sp[:, c0:c0 + 128], lhsT=identb[:, :], rhs=mqb[:, :], start=False, stop=True),
                 reads=[R_id, R_mq], writes=[rsp])
        st[u] = (sp, rsp, c0)

    def emit_pv(u):
        h, I, j = u
        sp, rsp, c0 = st.pop(u)
        nkb = 4 * I + 4
        if j == 0:
            tiles[(h, I)] = o_ps.next()
            if I == 0:
                tiles[("oa", h)] = oas.next()
        op_, rop = tiles[(h, I)]
        pt, rpt = pts.next()
        S.op("act", lambda e: e.activation(out=pt[:, c0:512], in_=sp[:, c0:512], func=AF.Exp), reads=[rsp], writes=[rpt])
        S.op("pe", lambda e: e.matmul(op_[0:65, c0:512], lhsT=vt[:, j, h * 65:(h + 1) * 65], rhs=pt[:, c0:512],
                                      start=(j == 0), stop=(j == nkb - 1)), reads=[rpt, R_vt[j // 8]], writes=[rop])
        if j == nkb - 1:
            oa, roa = tiles[("oa", h)]
            rl, rrl = rls.next()
            bc, rbc = bcs.next()
            bp, rbp = bc_ps.next()
            S.op("dve", lambda e: e.reciprocal(out=rl[0:1, :], in_=op_[64:65, :]), reads=[rop], writes=[rrl])
            S.op("pe", lambda e: e.matmul(bp[0:64, :], lhsT=onesf[0:1, 0:64], rhs=rl[0:1, :], start=True, stop=True),
                 reads=[R_ones, rrl], writes=[rbp])
            S.op("act", lambda e: e.copy(out=bc[:, :], in_=bp[0:64, :]), reads=[rbp], writes=[rbc])
            S.op("dve", lambda e: e.tensor_tensor(out=oa[:, I * 512:(I + 1) * 512], in0=op_[0:64, :], in1=bc[:, :], op=ALU.mult),
                 reads=[rop, rbc], writes=[roa])
            del tiles[(h, I)]
            if I == NT - 1:
                S.dma("sp", lambda e: e.dma_start(out=oaT[h * 64:(h + 1) * 64, :], in_=oa[:, :]), reads=[roa])

    if pump is not None:
        pump.attach(T, 4)
    LOOK = 3
    for i in range(min(LOOK, len(units))):
        emit_qk(units[i])
    for i, u in enumerate(units):
        if i + LOOK < len(units):
            emit_qk(units[i + LOOK])
        if pump is not None and i % pump_every == 0:
            pump.pump(S, 1)
        emit_pv(u)
    T.finish()


def stageC(P, layer):
    nc = P.nc
    T = Stage(nc, "c")
    S = T.S
    qbT, kbT, vB, sgB, obT = P.qbT(), P.kbT(), P.vB(), P.sgB(), P.obT()
    identb, R_id = load_ident(T, P, S)
    dtp = T.sb([128, 8, 128], F32, "dtp")
    qdec = T.sb([128, 8], F32, "qdec")
    kdec = T.sb([128, 8], F32, "kdec")
    cd = T.sb([128, 4, 128], F32, "cd")
    gn = T.sb([128, D], F32, "gn")
    R_dtp, R_qdec, R_kdec, R_cd, R_gn = RL(5)
    c_dtp, c_qdec, c_kdec, c_cd, rng = P.inp("c_dtp"), P.inp("c_qdec"), P.inp("c_kdec"), P.inp("c_cd"), P.inp("ret_norm_g")
    S.dma("sp", lambda e: e.dma_start(out=dtp[:], in_=c_dtp[:, :, :]), writes=[R_dtp])
    S.dma("sp", lambda e: e.dma_start(out=qdec[:], in_=c_qdec[:, :]), writes=[R_qdec])
    S.dma("sp", lambda e: e.dma_start(out=kdec[:], in_=c_kdec[:, :]), writes=[R_kdec])
    S.dma("sp", lambda e: e.dma_start(out=cd[:], in_=c_cd[:, :, :]), writes=[R_cd])
    S.dma("sp", lambda e: e.dma_start(out=gn[:], in_=rng[layer:layer + 1, :].partition_broadcast(128)), writes=[R_gn])
    state = T.sb([128, 4, 128], F32, "state")
    state_bf = T.sb([128, 4, 128], BF16, "statebf")
    R_state, R_sbf = RL(2)
    S.op("pool", lambda e: e.memset(state[:], 0.0), writes=[R_state])
    S.op("pool", lambda e: e.memset(state_bf[:], 0.0), writes=[R_sbf])
    q4s = Rot([T.sb([128, 4, 512], BF16, "q4") for _ in range(2)])
    k4s = Rot([T.sb([128, 4, 512], BF16, "k4") for _ in range(2)])
    vs = Rot([T.sb([128, D], BF16, "v") for _ in range(3)])
    sgs = Rot([T.sb([128, D], BF16, "sg") for _ in range(2)])
    sggs = Rot([T.sb([128, D], F32, "sgg") for _ in range(3)])
    sTs = Rot([T.sb([128, 8, 128], BF16, "sT") for _ in range(2)])
    kss = Rot([T.sb([128, 8, 64], BF16, "ks") for _ in range(2)])
    ys = Rot([T.sb([128, 8, 128], F32, "y") for _ in range(2)])
    ysq = Rot([T.sb([128, 8, 128], F32, "ysq") for _ in range(1)])
    sss = Rot([T.sb([128, 8], F32, "ss") for _ in range(2)])
    obs = Rot([T.sb([128, D], BF16, "ob") for _ in range(2)])
    obt = Rot([T.sb([128, 8, 512], BF16, "obt") for _ in range(2)])
    sc_ps = Rot([T.ps([128, 8, 128], F32, "sc") for _ in range(1)])
    kt_ps = Rot([T.ps([128, 8, 128], BF16, "kt") for _ in range(1)])
    kv_ps = Rot([T.ps([128, 4, 256], F32, "kv") for _ in range(1)])
    y_ps = Rot([T.ps([128, 8, 128], F32, "yp") for _ in range(1)])
    ot_ps = Rot([T.ps([128, 8, 128], BF16, "ot") for _ in range(1)])
    qsrc = qbT.rearrange("(hp q) n -> q hp n", q=128)
    ksrc = kbT.rearrange("(hp q) n -> q hp n", q=128)
    ctx = {}

    def front(n):
        cc = n % 4
        if cc == 0:
            t = n // 4
            q4, rq4 = q4s.next()
            k4, rk4 = k4s.next()
            S.dma("sp", lambda e: e.dma_start(out=q4[:], in_=qsrc[:, :, t * 512:(t + 1) * 512]), writes=[rq4])
            S.dma("sp", lambda e: e.dma_start(out=k4[:], in_=ksrc[:, :, t * 512:(t + 1) * 512]), writes=[rk4])
            ctx["q4"] = (q4, rq4, k4, rk4)
        q4, rq4, k4, rk4 = ctx["q4"]
        v, rv = vs.next()
        sg, rsg = sgs.next()
        S.dma("sp", lambda e: e.dma_start(out=v[:], in_=vB[n * 128:(n + 1) * 128, :]), writes=[rv])
        S.dma("sp", lambda e: e.dma_start(out=sg[:], in_=sgB[n * 128:(n + 1) * 128, :]), writes=[rsg])
        sgg, rsgg = sggs.next()
        S.op("pool", lambda e: e.tensor_tensor(out=sgg[:], in0=sg[:], in1=gn[:], op=ALU.mult), reads=[rsg, R_gn], writes=[rsgg])
        sc, rsc = sc_ps.next()
        cs = slice(cc * 128, (cc + 1) * 128)
        if C_LEVEL == 0:
            return
        for h in range(8):
            hp, two = h // 2, h % 2
            S.op("pe", lambda e, h=h, hp=hp, two=two: e.matmul(sc[:, two * 4 + hp, :], lhsT=k4[two * 64:(two + 1) * 64, hp, cs], rhs=q4[two * 64:(two + 1) * 64, hp, cs],
                                                               start=True, stop=True), reads=[rq4, rk4], writes=[rsc])
        sT, rsT = sTs.next()
        S.op("dve", lambda e: e.tensor_tensor(out=sT[:], in0=sc[:], in1=dtp[:], op=ALU.mult), reads=[rsc, R_dtp], writes=[rsT])
        kt, rkt = kt_ps.next()
        for hp in range(4):
            S.op("pe", lambda e, hp=hp: e.transpose(out=kt[:, hp, :], in_=k4[:, hp, cs], identity=identb[:]), reads=[rk4, R_id], writes=[rkt])
        ks_, rks = kss.next()
        S.op("dve", lambda e: e.tensor_tensor(out=ks_[:], in0=kt[:, 0:4, :].rearrange("p a (two d) -> p (a two) d", two=2),
                                              in1=kdec[:, :].unsqueeze(2).to_broadcast([128, 8, 64]), op=ALU.mult),
             reads=[rkt, R_kdec], writes=[rks])
        kv, rkv = kv_ps.next()
        for hp in range(4):
            S.op("pe", lambda e, hp=hp: e.matmul(kv[:, hp, :], lhsT=ks_[:, 2 * hp:2 * hp + 2, :].rearrange("p a d -> p (a d)"),
                                                 rhs=v[:, hp * 256:(hp + 1) * 256], start=True, stop=True), reads=[rks, rv], writes=[rkv])
        ctx[n] = dict(q4=q4, rq4=rq4, v=v, rv=rv, sg=sgg, rsg=rsgg, sT=sT, rsT=rsT, kv=kv, rkv=rkv, cs=cs, cc=cc)

    def su(n):
        c = ctx[n]
        kv, rkv = c["kv"], c["rkv"]
        S.op("pool", lambda e: e.tensor_tensor(out=state[:], in0=state[:], in1=cd[:], op=ALU.mult), reads=[R_state, R_cd], writes=[R_state])
        for two in range(2):
            ps_ = slice(two * 64, (two + 1) * 64)
            S.op("dve", lambda e, ps_=ps_, two=two: e.tensor_tensor(out=state[ps_, :, :], in0=state[ps_, :, :], in1=kv[ps_, :, two * 128:(two + 1) * 128], op=ALU.add),
                 reads=[R_state, rkv], writes=[R_state])

    def sbf_copy():
        S.op("act", lambda e: e.copy(out=state_bf[:], in_=state[:]), reads=[R_state], writes=[R_sbf])

    def mid(n):
        c = ctx[n]
        q4, rq4, v, rv, sT, rsT, cs = c["q4"], c["rq4"], c["v"], c["rv"], c["sT"], c["rsT"], c["cs"]
        yp, ryp = y_ps.next()
        for h in range(8):
            hp, two = h // 2, h % 2
            S.op("pe", lambda e, h=h, hp=hp, two=two: e.matmul(yp[:, two * 4 + hp, :], lhsT=sT[:, two * 4 + hp, :], rhs=v[:, h * 128:(h + 1) * 128], start=True, stop=False),
                 reads=[rsT, rv], writes=[ryp])
            S.op("pe", lambda e, h=h, hp=hp, two=two: e.matmul(yp[:, two * 4 + hp, :], lhsT=q4[two * 64:(two + 1) * 64, hp, cs],
                                                               rhs=state_bf[two * 64:(two + 1) * 64, hp, :], start=False, stop=True),
                 reads=[rq4, R_sbf], writes=[ryp])
        y, ry = ys.next()
        S.op("dve", lambda e: e.tensor_tensor(out=y[:].rearrange("p (hp two) e -> p two hp e", two=2),
                                              in0=yp[:].rearrange("p (two hp) e -> p two hp e", two=2),
                                              in1=qdec[:, :].rearrange("p (two hp) -> p two hp", two=2).unsqueeze(3).to_broadcast([128, 2, 4, 128]), op=ALU.mult),
             reads=[ryp, R_qdec], writes=[ry])
        c["y"], c["ry"] = y, ry

    def back(n):
        c = ctx.pop(n)
        y, ry, sg, rsg, cc = c["y"], c["ry"], c["sg"], c["rsg"], c["cc"]
        sq, rsq = ysq.next()
        ss, rss = sss.next()
        for h in range(8):
            S.op("act", lambda e, h=h: e.activation(out=sq[:, h, :], in_=y[:, h, :], func=AF.Square, accum_out=ss[:, h:h + 1]), reads=[ry], writes=[rsq, rss])
        S.op("dve", lambda e: e.tensor_scalar(out=ss[:], in0=ss[:], scalar1=1.0 / 128.0, scalar2=RMS_EPS, op0=ALU.mult, op1=ALU.add), reads=[rss], writes=[rss])
        S.op("act", lambda e: e.sqrt(out=ss[:], in_=ss[:]), reads=[rss], writes=[rss])
        S.op("dve", lambda e: e.reciprocal(out=ss[:], in_=ss[:]), reads=[rss], writes=[rss])
        S.op("dve", lambda e: e.tensor_tensor(out=y[:], in0=y[:], in1=ss[:, :].unsqueeze(2).to_broadcast([128, 8, 128]), op=ALU.mult), reads=[ry, rss], writes=[ry])
        yf = y[:].rearrange("p h e -> p (h e)")
        ob, rob = obs.next()
        S.op("dve", lambda e: e.tensor_tensor(out=ob[:], in0=yf, in1=sg[:], op=ALU.mult), reads=[ry, rsg], writes=[rob])
        ot, rot = ot_ps.next()
        for k in range(8):
            S.op("pe", lambda e, k=k: e.transpose(out=ot[:, k, :], in_=ob[:, k * 128:(k + 1) * 128], identity=identb[:]), reads=[rob, R_id], writes=[rot])
        if cc == 0:
            ctx["obt"] = obt.next()
        ob4, rob4 = ctx["obt"]
        S.op("act", lambda e: e.copy(out=ob4[:, :, cc * 128:(cc + 1) * 128], in_=ot[:]), reads=[rot], writes=[rob4])
        if cc == 3:
            t = n // 4
            S.dma("sp", lambda e: e.dma_start(out=obT.rearrange("(c p) n -> p c n", p=128)[:, :, t * 512:(t + 1) * 512], in_=ob4[:]), reads=[rob4])

    LV = C_LEVEL
    NCH = C_NCH
    front(0)
    if LV >= 2:
        su(0)
    for n in range(NCH):
        if n + 1 < NCH:
            front(n + 1)
        if LV >= 3:
            mid(n)
        if n + 1 < NCH and LV >= 2:
            if LV >= 3:
                sbf_copy()
            su(n + 1)
        if n >= 1 and LV >= 4:
            back(n - 1)
    if LV >= 4:
        back(NCH - 1)
    T.finish()


class LNBufs:
    def __init__(self, T, S, P, gname, bname, layer):
        self.g_bc = T.sb([128, D], F32, "lng")
        self.b_bc = T.sb([128, D], F32, "lnb")
        self.R_g, self.R_b = RL(2)
        g, b = P.inp(gname), P.inp(bname)
        S.dma("sp", lambda e: e.dma_start(out=self.g_bc[:], in_=g[layer:layer + 1, :].partition_broadcast(128)), writes=[self.R_g])
        S.dma("sp", lambda e: e.dma_start(out=self.b_bc[:], in_=b[layer:layer + 1, :].partition_broadcast(128)), writes=[self.R_b])
        self.stats = Rot([T.sb([128, 2, 6], F32, "lnst") for _ in range(2)])
        self.mv = Rot([T.sb([128, 2], F32, "lnmv") for _ in range(2)])


def emit_ln(S, L, r, rr, dst_rows, gb_eng="pool", g_eng=None):
    st, rst = L.stats.next()
    mv, rmv = L.mv.next()
    S.op("dve", lambda e: e.bn_stats(out=st[:, 0, :], in_=r[:, 0:512]), reads=[rr], writes=[rst])
    S.op("dve", lambda e: e.bn_stats(out=st[:, 1, :], in_=r[:, 512:1024]), reads=[rr], writes=[rst])
    S.op("dve", lambda e: e.bn_aggr(out=mv[:], in_=st[:].rearrange("p a b -> p (a b)")), reads=[rst], writes=[rmv])
    S.op("dve", lambda e: e.tensor_scalar(out=mv[:, 1:2], in0=mv[:, 1:2], scalar1=LN_EPS, scalar2=None, op0=ALU.add), reads=[rmv], writes=[rmv])
    S.op("act", lambda e: e.sqrt(out=mv[:, 1:2], in_=mv[:, 1:2]), reads=[rmv], writes=[rmv])
    S.op("dve", lambda e: e.reciprocal(out=mv[:, 1:2], in_=mv[:, 1:2]), reads=[rmv], writes=[rmv])
    S.op("dve", lambda e: e.tensor_scalar(out=r[:], in0=r[:], scalar1=mv[:, 0:1], scalar2=mv[:, 1:2], op0=ALU.subtract, op1=ALU.mult),
         reads=[rr, rmv], writes=[rr])
    S.op(g_eng or gb_eng, lambda e: e.tensor_tensor(out=r[:], in0=r[:], in1=L.g_bc[:], op=ALU.mult), reads=[rr, L.R_g], writes=[rr])
    S.op(gb_eng, lambda e: e.tensor_tensor(out=r[:], in0=r[:], in1=L.b_bc[:], op=ALU.add), reads=[rr, L.R_b], writes=[rr])
    S.dma("sp", lambda e: e.dma_start(out=dst_rows, in_=r[:]), reads=[rr])


def stageD(P, layer, x_src, x_dst):
    nc = P.nc
    T = Stage(nc, "d")
    S = T.S
    oaT, obT, gT = P.oaT(), P.obT(), P.gT()
    wf = T.sb([128, 4, D], BF16, "wf")
    wr = T.sb([128, 8, D], BF16, "wr")
    wo = T.sb([128, 8, D], BF16, "wo")
    R_wf, R_wr, R_wo = RL(3)
    w_fox, w_ret, w_out = P.inp("w_branch_fox"), P.inp("w_branch_ret"), P.inp("w_out")
    for k in range(4):
        S.dma("pool", lambda e, k=k: e.dma_start(out=wf[:, k, :], in_=w_fox[layer, k * 128:(k + 1) * 128, :]), writes=[R_wf])
    for k in range(8):
        S.dma("pool", lambda e, k=k: e.dma_start(out=wr[:, k, :], in_=w_ret[layer, k * 128:(k + 1) * 128, :]), writes=[R_wr])
    for k in range(8):
        S.dma("pool", lambda e, k=k: e.dma_start(out=wo[:, k, :], in_=w_out[layer, k * 128:(k + 1) * 128, :]), writes=[R_wo])
    L = LNBufs(T, S, P, "ln_mix_g", "ln_mix_b", layer)
    oas = Rot([T.sb([128, 4, 512], BF16, "oa") for _ in range(2)])
    obs = Rot([T.sb([128, 8, 512], BF16, "ob") for _ in range(2)])
    gs = Rot([T.sb([128, 16, 512], BF16, "g") for _ in range(2)])
    mTs = Rot([T.sb([128, 8, 512], BF16, "mT") for _ in range(2)])
    t1s = Rot([T.sb([128, 512], F32, "t1") for _ in range(2)])
    t2s = Rot([T.sb([128, 512], F32, "t2") for _ in range(2)])
    xs_ = Rot([T.sb([128, D], F32, "x") for _ in range(3)])
    pa_ = Rot([T.ps([128, 512], F32, "pa") for _ in range(2)])
    pb_ = Rot([T.ps([128, 512], F32, "pb") for _ in range(2)])
    ph_ = Rot([T.ps([128, 512], F32, "ph") for _ in range(2)])
    oasrc = oaT.rearrange("(k p) n -> p k n", p=128)
    obsrc = obT.rearrange("(k p) n -> p k n", p=128)
    gsrc = gT.rearrange("(k p) n -> p k n", p=128)
    mts = {}

    def emit_merge(t):
        ts_ = slice(t * 512, (t + 1) * 512)
        oa, roa = oas.next()
        ob, rob = obs.next()
        g, rg = gs.next()
        mT, rmT = mTs.next()
        S.dma("sp", lambda e: e.dma_start(out=oa[:], in_=oasrc[:, :, ts_]), writes=[roa])
        S.dma("sp", lambda e: e.dma_start(out=ob[:], in_=obsrc[:, :, ts_]), writes=[rob])
        S.dma("sp", lambda e: e.dma_start(out=g[:, 0:8, :], in_=gsrc[:, 0:8, ts_]), writes=[rg])
        S.dma("sp", lambda e: e.dma_start(out=g[:, 8:16, :], in_=gsrc[:, 8:16, ts_]), writes=[rg])
        for cc in range(8):
            pa, rpa = pa_.next()
            pb, rpb = pb_.next()
            for k in range(4):
                S.op("pe", lambda e, pa=pa, k=k, cc=cc: e.matmul(pa[:, :], lhsT=wf[:, k, cc * 128:(cc + 1) * 128], rhs=oa[:, k, :], start=(k == 0), stop=(k == 3)),
                     reads=[R_wf, roa], writes=[rpa])
            for k in range(8):
                S.op("pe", lambda e, pb=pb, k=k, cc=cc: e.matmul(pb[:, :], lhsT=wr[:, k, cc * 128:(cc + 1) * 128], rhs=ob[:, k, :], start=(k == 0), stop=(k == 7)),
                     reads=[R_wr, rob], writes=[rpb])
            t1, rt1 = t1s.next()
            t2, rt2 = t2s.next()
            S.op("dve", lambda e, t1=t1, pa=pa, cc=cc: e.tensor_tensor(out=t1[:], in0=pa[:, :], in1=g[:, cc, :], op=ALU.mult), reads=[rpa, rg], writes=[rt1])
            S.op("dve", lambda e, t2=t2, pb=pb, cc=cc: e.tensor_tensor(out=t2[:], in0=pb[:, :], in1=g[:, 8 + cc, :], op=ALU.mult), reads=[rpb, rg], writes=[rt2])
            S.op("pool", lambda e, t1=t1, t2=t2, cc=cc: e.tensor_tensor(out=mT[:, cc, :], in0=t1[:], in1=t2[:], op=ALU.add), reads=[rt1, rt2], writes=[rmT])
        mts[t] = (mT, rmT)

    def emit_out(t):
        mT, rmT = mts.pop(t)
        for tb in range(4):
            blk = t * 4 + tb
            x, rx = xs_.next()
            S.dma("sp", lambda e, x=x, blk=blk: e.dma_start(out=x[:], in_=x_src[blk * 128:(blk + 1) * 128, :]), writes=[rx])
            for hf in range(2):
                ph, rph = ph_.next()
                for cc in range(8):
                    S.op("pe", lambda e, ph=ph, cc=cc, tb=tb, hf=hf: e.matmul(ph[:, :], lhsT=mT[:, cc, tb * 128:(tb + 1) * 128], rhs=wo[:, cc, hf * 512:(hf + 1) * 512],
                                                                            start=(cc == 0), stop=(cc == 7)), reads=[rmT, R_wo], writes=[rph])
                S.op("dve", lambda e, x=x, ph=ph, hf=hf: e.scalar_tensor_tensor(out=x[:, hf * 512:(hf + 1) * 512], in0=x[:, hf * 512:(hf + 1) * 512], scalar=DN_ALPHA, in1=ph[:, :],
                                                                                op0=ALU.mult, op1=ALU.add), reads=[rx, rph], writes=[rx])
            emit_ln(S, L, x, rx, x_dst[blk * 128:(blk + 1) * 128, :], g_eng="dve")

    emit_merge(0)
    for t in range(NT):
        if t + 1 < NT:
            emit_merge(t + 1)
        emit_out(t)
    T.finish()


def stageE0(P, layer, x_src):
    nc = P.nc
    i = layer // 2
    T = Stage(nc, "r")
    S = T.S
    identf = T.sb([128, 128], F32, "identf")
    wrf = T.sb([128, 8, 8], F32, "wrf")
    gate_all = T.sb([128, NB, 8], F32, "gates")
    R_id, R_wr, R_ga = RL(3)
    c_ident, router = P.inp("c_ident"), P.inp("moe_router")
    S.dma("sp", lambda e: e.dma_start(out=identf[:], in_=c_ident[:, :]), writes=[R_id])
    S.dma("sp", lambda e: e.dma_start(out=wrf[:], in_=router[i].rearrange("(k p) n -> p k n", p=128)), writes=[R_wr])
    xfs = Rot([T.sb([128, D], F32, "xf") for _ in range(2)])
    xTs = Rot([T.sb([128, 8, 128], F32, "xTf") for _ in range(2)])
    pT_ = Rot([T.ps([128, 8, 128], F32, "pTf") for _ in range(2)])
    lg_ = Rot([T.ps([128, 512], F32, "lg") for _ in range(2)])
    sm = [Rot([T.sb([128, 8], F32, "sm%d" % j) for _ in range(2)]) for j in range(5)]
    sc1 = [Rot([T.sb([128, 1], F32, "sc%d" % j) for _ in range(2)]) for j in range(4)]
    gd = P.scratch("gates_d", [128, NB, 8], F32)
    for blk in range(NB):
        xf, rxf = xfs.next()
        S.dma("sp", lambda e, xf=xf, blk=blk: e.dma_start(out=xf[:], in_=x_src[blk * 128:(blk + 1) * 128, :]), writes=[rxf])
        pT, rpT = pT_.next()
        for k in range(8):
            S.op("pe", lambda e, pT=pT, xf=xf, k=k: e.transpose(out=pT[:, k, :], in_=xf[:, k * 128:(k + 1) * 128], identity=identf[:]), reads=[rxf, R_id], writes=[rpT])
        xT, rxT = xTs.next()
        S.op("act", lambda e, xT=xT, pT=pT: e.copy(out=xT[:, 0:4, :], in_=pT[:, 0:4, :]), reads=[rpT], writes=[rxT])
        S.op("dve", lambda e, xT=xT, pT=pT: e.tensor_copy(out=xT[:, 4:8, :], in_=pT[:, 4:8, :]), reads=[rpT], writes=[rxT])
        lg, rlg = lg_.next()
        for k in range(8):
            S.op("pe", lambda e, lg=lg, xT=xT, k=k: e.matmul(lg[:, 0:8], lhsT=xT[:, k, :], rhs=wrf[:, k, :], start=(k == 0), stop=(k == 7)), reads=[rxT, R_wr], writes=[rlg])
        (lgs, rlgs), (eq, req), (l2, rl2), (sel, rsel), (ex, rex) = [r_.next() for r_ in sm]
        (m1, rm1), (m2, rm2), (nm1, rnm1), (den, rden) = [r_.next() for r_ in sc1]
        S.op("dve", lambda e, lgs=lgs, lg=lg: e.tensor_copy(out=lgs[:], in_=lg[:, 0:8]), reads=[rlg], writes=[rlgs])
        S.op("dve", lambda e, m1=m1, lgs=lgs: e.tensor_reduce(out=m1[:], in_=lgs[:], axis=AX.X, op=ALU.max), reads=[rlgs], writes=[rm1])
        S.op("dve", lambda e, eq=eq, lgs=lgs, m1=m1: e.tensor_scalar(out=eq[:], in0=lgs[:], scalar1=m1[:, 0:1], scalar2=None, op0=ALU.is_equal), reads=[rlgs, rm1], writes=[req])
        S.op("dve", lambda e, l2=l2, eq=eq, lgs=lgs: e.scalar_tensor_tensor(out=l2[:], in0=eq[:], scalar=-1.0e30, in1=lgs[:], op0=ALU.mult, op1=ALU.add), reads=[req, rlgs], writes=[rl2])
        S.op("dve", lambda e, m2=m2, l2=l2: e.tensor_reduce(out=m2[:], in_=l2[:], axis=AX.X, op=ALU.max), reads=[rl2], writes=[rm2])
        S.op("dve", lambda e, sel=sel, lgs=lgs, m2=m2: e.tensor_scalar(out=sel[:], in0=lgs[:], scalar1=m2[:, 0:1], scalar2=None, op0=ALU.is_ge), reads=[rlgs, rm2], writes=[rsel])
        S.op("dve", lambda e, nm1=nm1, m1=m1: e.tensor_scalar(out=nm1[:], in0=m1[:], scalar1=-1.0, scalar2=None, op0=ALU.mult), reads=[rm1], writes=[rnm1])
        S.op("act", lambda e, ex=ex, lgs=lgs, nm1=nm1: e.activation(out=ex[:], in_=lgs[:], func=AF.Exp, bias=nm1[:, 0:1], scale=1.0), reads=[rlgs, rnm1], writes=[rex])
        S.op("dve", lambda e, ex=ex, sel=sel: e.tensor_tensor(out=ex[:], in0=ex[:], in1=sel[:], op=ALU.mult), reads=[rex, rsel], writes=[rex])
        S.op("dve", lambda e, den=den, ex=ex: e.tensor_reduce(out=den[:], in_=ex[:], axis=AX.X, op=ALU.add), reads=[rex], writes=[rden])
        S.op("dve", lambda e, den=den: e.reciprocal(out=den[:], in_=den[:]), reads=[rden], writes=[rden])
        S.op("dve", lambda e, ex=ex, den=den, blk=blk: e.tensor_scalar(out=gate_all[:, blk, :], in0=ex[:], scalar1=den[:, 0:1], scalar2=None, op0=ALU.mult), reads=[rex, rden], writes=[R_ga])
    S.dma("sp", lambda e: e.dma_start(out=gd[:, :, :], in_=gate_all[:]), reads=[R_ga])
    T.finish()


def stageE(P, layer, x_src, x_dst, moe, pump=None, pump_n=0):
    nc = P.nc
    i = layer // 2
    if moe:
        NE, FF = N_EXP, D_FFE
        wg_d, wu_d, wd_d = P.inp("moe_w_gate"), P.inp("moe_w_up"), P.inp("moe_w_down")
        wg = lambda e_: wg_d[i, e_]
        wu = lambda e_: wu_d[i, e_]
        wd = lambda e_: wd_d[i, e_]
    else:
        NE, FF = 1, D_FF
        wg_d, wu_d, wd_d = P.inp("ffn_w_gate"), P.inp("ffn_w_up"), P.inp("ffn_w_down")
        wg = lambda e_: wg_d[i]
        wu = lambda e_: wu_d[i]
        wd = lambda e_: wd_d[i]
    GC = 2
    NG = FF // (128 * GC)
    NQ = 4
    HB = NB // NQ
    T = Stage(nc, "e")
    S = T.S
    ident, R_id = load_ident(T, P, S)
    L = LNBufs(T, S, P, "ln_ffn_g", "ln_ffn_b", layer)
    gate_all = None
    R_ga = Res()
    if moe:
        gate_all = T.sb([128, NB, 8], F32, "gates")
        gd = P.scratch("gates_d", [128, NB, 8], F32)
        S.dma("sp", lambda e: e.dma_start(out=gate_all[:], in_=gd[:, :, :]), writes=[R_ga])
    if pump is not None:
        pump.attach(T, 3)
    xT_b = [T.sb([128, 8, HB * 128], BF16, "xT") for _ in range(2)]
    acc_b = [T.sb([128, HB, D], F32, "acc") for _ in range(2)]
    R_xT_b = [RL(HB) for _ in range(2)]
    R_acc_b = [RL(HB) for _ in range(2)]
    epi_q = []
    wgs = Rot([T.sb([128, 8, GC * 128], BF16, "wg") for _ in range(2)])
    wus = Rot([T.sb([128, 8, GC * 128], BF16, "wu") for _ in range(2)])
    wds = Rot([T.sb([128, GC, D], BF16, "wd") for _ in range(2)])
    uTs = Rot([T.sb([128, GC, 512], BF16, "uT") for _ in range(3)])
    sgs = Rot([T.sb([128, 512], F32, "sg") for _ in range(2)])
    pg_ = Rot([T.ps([128, 512], F32, "pg") for _ in range(2)])
    pu_ = Rot([T.ps([128, 512], F32, "pu") for _ in range(2)])
    pd_ = Rot([T.ps([128, 512], F32, "pd") for _ in range(2)])
    xf = Rot([T.sb([128, D], F32, "xf") for _ in range(2)])
    xb = Rot([T.sb([128, D], BF16, "xb") for _ in range(2)])
    pT = Rot([T.ps([128, 8, 128], BF16, "pT") for _ in range(2)])
    def emit_epi(n):
        for _ in range(n):
            if not epi_q:
                return
            acc_e, racc_e, bl, r0 = epi_q.pop(0)
            f, rf = xf2.next()
            S.dma("sp", lambda e, f=f, r0=r0: e.dma_start(out=f[:], in_=x_src[r0:r0 + 128, :]), writes=[rf])
            S.op("dve", lambda e, f=f, bl=bl, acc_e=acc_e: e.scalar_tensor_tensor(out=f[:], in0=f[:], scalar=DN_ALPHA, in1=acc_e[:, bl, :], op0=ALU.mult, op1=ALU.add),
                 reads=[rf, racc_e[bl]], writes=[rf])
            emit_ln(S, L, f, rf, x_dst[r0:r0 + 128, :])

    xf2 = Rot([T.sb([128, D], F32, "xf2") for _ in range(2)])
    for half in range(NQ):
        tok0 = half * HB * 128
        xT, acc, R_xT, R_acc = xT_b[half % 2], acc_b[half % 2], R_xT_b[half % 2], R_acc_b[half % 2]
        for bl in range(HB):
            f, rf = xf.next()
            b, rb = xb.next()
            p, rp = pT.next()
            r0 = tok0 + bl * 128
            S.dma("sp", lambda e, f=f, r0=r0: e.dma_start(out=f[:], in_=x_src[r0:r0 + 128, :]), writes=[rf])
            S.op("act", lambda e, f=f, b=b: e.copy(out=b[:], in_=f[:]), reads=[rf], writes=[rb])
            for k in range(8):
                S.op("pe", lambda e, p=p, b=b, k=k: e.transpose(out=p[:, k, :], in_=b[:, k * 128:(k + 1) * 128], identity=ident[:]), reads=[rb, R_id], writes=[rp])
            S.op("dve", lambda e, p=p, bl=bl: e.tensor_copy(out=xT[:, :, bl * 128:(bl + 1) * 128], in_=p[:]), reads=[rp], writes=[R_xT[bl]])
        items = [(ex, g, t) for ex in range(NE) for g in range(NG) for t in range(HB // 4)]
        wcur = {}
        pend = {}

        def emit_gu(it):
            ex, g, t = it
            ff0 = g * GC * 128
            if t == 0:
                wgb, rwg = wgs.next()
                wub, rwu = wus.next()
                wdb, rwd = wds.next()
                S.dma("pool", lambda e: e.dma_start(out=wgb[:], in_=wg(ex)[:, ff0:ff0 + GC * 128].rearrange("(k p) n -> p k n", p=128)), writes=[rwg])
                S.dma("pool", lambda e: e.dma_start(out=wub[:], in_=wu(ex)[:, ff0:ff0 + GC * 128].rearrange("(k p) n -> p k n", p=128)), writes=[rwu])
                S.dma("pool", lambda e: e.dma_start(out=wdb[:], in_=wd(ex)[ff0:ff0 + GC * 128, :].rearrange("(c p) n -> p c n", p=128)), writes=[rwd])
                wcur[(ex, g)] = (wgb, rwg, wub, rwu, wdb, rwd)
            wgb, rwg, wub, rwu, wdb, rwd = wcur[(ex, g)]
            uT, ruT = uTs.next()
            for c in range(GC):
                pg, rpg = pg_.next()
                pu, rpu = pu_.next()
                for k in range(8):
                    S.op("pe", lambda e, pg=pg, k=k, c=c: e.matmul(pg[:, :], lhsT=wgb[:, k, c * 128:(c + 1) * 128], rhs=xT[:, k, t * 512:(t + 1) * 512],
                                                                 start=(k == 0), stop=(k == 7)), reads=[rwg] + R_xT[4 * t:4 * t + 4], writes=[rpg])
                for k in range(8):
                    S.op("pe", lambda e, pu=pu, k=k, c=c: e.matmul(pu[:, :], lhsT=wub[:, k, c * 128:(c + 1) * 128], rhs=xT[:, k, t * 512:(t + 1) * 512],
                                                                 start=(k == 0), stop=(k == 7)), reads=[rwu] + R_xT[4 * t:4 * t + 4], writes=[rpu])
                sg, rsg = sgs.next()
                S.op("act", lambda e, sg=sg, pg=pg: e.activation(out=sg[:], in_=pg[:, :], func=AF.Silu), reads=[rpg], writes=[rsg])
                S.op("dve", lambda e, c=c, sg=sg, pu=pu: e.tensor_tensor(out=uT[:, c, :], in0=pu[:, :], in1=sg[:], op=ALU.mult), reads=[rpu, rsg], writes=[ruT])
            pend[it] = (uT, ruT, wdb, rwd)

        def emit_down(it):
            ex, g, t = it
            uT, ruT, wdb, rwd = pend.pop(it)
            first = (ex == 0 and g == 0)
            for tb in range(4):
                bl = t * 4 + tb
                blk = half * HB + bl
                for hf in range(2):
                    pd, rpd = pd_.next()
                    for c in range(GC):
                        S.op("pe", lambda e, pd=pd, c=c, tb=tb, hf=hf: e.matmul(pd[:, :], lhsT=uT[:, c, tb * 128:(tb + 1) * 128], rhs=wdb[:, c, hf * 512:(hf + 1) * 512],
                                                                              start=(c == 0), stop=(c == GC - 1)), reads=[ruT, rwd], writes=[rpd])
                    a_ = acc[:, bl, hf * 512:(hf + 1) * 512]
                    if moe:
                        gsc = gate_all[:, blk, ex:ex + 1]
                        if first:
                            S.op("dve", lambda e, a_=a_, pd=pd, gsc=gsc: e.tensor_scalar(out=a_, in0=pd[:, :], scalar1=gsc, scalar2=None, op0=ALU.mult), reads=[rpd, R_ga], writes=[R_acc[bl]])
                        else:
                            S.op("dve", lambda e, a_=a_, pd=pd, gsc=gsc: e.scalar_tensor_tensor(out=a_, in0=pd[:, :], scalar=gsc, in1=a_, op0=ALU.mult, op1=ALU.add),
                                 reads=[rpd, R_ga, R_acc[bl]], writes=[R_acc[bl]])
                    else:
                        if first:
                            S.op("dve", lambda e, a_=a_, pd=pd: e.tensor_copy(out=a_, in_=pd[:, :]), reads=[rpd], writes=[R_acc[bl]])
                        else:
                            S.op("dve", lambda e, a_=a_, pd=pd: e.tensor_tensor(out=a_, in0=pd[:, :], in1=a_, op=ALU.add), reads=[rpd, R_acc[bl]], writes=[R_acc[bl]])

        emit_gu(items[0])
        for ii, it in enumerate(items):
            if ii + 1 < len(items):
                emit_gu(items[ii + 1])
            if pump is not None:
                pump.pump(S, pump_n)
            emit_down(it)
            if ii % 2 == 1:
                emit_epi(1)
        emit_epi(len(epi_q))
        for bl in range(HB):
            epi_q.append((acc, R_acc, bl, tok0 + bl * 128))
    emit_epi(len(epi_q))
    T.finish()


TS = 512
NTILE = (2 * S_LEN) // TS + N_EXP
NSLOT = NTILE * TS
SGC = 4
SNG = D_FFE // (128 * SGC)
WROW = 8 * SGC * 128
IOA = bass.IndirectOffsetOnAxis


def sp_scratch(P, layer):
    return dict(
        xsort=P.scratch("xsort", [NSLOT, D], BF16),
        ysort=P.scratch("ysort", [NSLOT, D], F32),
        wgs=P.scratch("wgs%d" % layer, [N_EXP * SNG * 128, WROW], BF16),
        wus=P.scratch("wus%d" % layer, [N_EXP * SNG * 128, WROW], BF16),
        wds=P.scratch("wds%d" % layer, [N_EXP * SNG * 128, SGC * D], BF16),
        slotA=P.scratch("slotA", [128, NB], I32),
        slotB=P.scratch("slotB", [128, NB], I32),
        gA=P.scratch("gA", [128, NB], F32),
        gB=P.scratch("gB", [128, NB], F32),
        widx=P.scratch("widx", [128, NTILE * SNG], I32),
    )


class BgPump:
    def __init__(self, P, layer):
        i = layer // 2
        D_ = sp_scratch(P, layer)
        wg_d, wu_d, wd_d = P.inp("moe_w_gate"), P.inp("moe_w_up"), P.inp("moe_w_down")
        self.jobs = []
        for ex in range(N_EXP):
            for g in range(SNG):
                ff0 = g * SGC * 128
                r0 = (ex * SNG + g) * 128
                for src, dst in ((wg_d, D_["wgs"]), (wu_d, D_["wus"])):
                    self.jobs.append(("A", src[i, ex, :, ff0:ff0 + SGC * 128].rearrange("(k p) n -> p k n", p=128), dst[r0:r0 + 128, :]))
                self.jobs.append(("D", wd_d[i, ex, ff0:ff0 + SGC * 128, :].rearrange("(c p) n -> p c n", p=128), D_["wds"][r0:r0 + 128, :]))
        self.bufs = None

    def attach(self, T, n):
        self.bufs = Rot([T.sb([128, 8 * SGC * 128], BF16, "bg") for _ in range(n)])

    def pump(self, S, n):
        for _ in range(n):
            if not self.jobs:
                return
            kind, src, dst = self.jobs.pop(0)
            b, rb = self.bufs.next()
            view = b[:].rearrange("p (k n) -> p k n", k=8) if kind == "A" else b[:].rearrange("p (c n) -> p c n", c=SGC)
            S.dma("pool", lambda e, view=view, src=src: e.dma_start(out=view, in_=src), writes=[rb])
            S.dma("sp", lambda e, b=b, dst=dst: e.dma_start(out=dst, in_=b[:]), reads=[rb])


def stageW(P, pump):
    if not pump.jobs:
        return
    T = Stage(P.nc, "w")
    pump.attach(T, 6)
    pump.pump(T.S, len(pump.jobs))
    T.finish()


def stageR(P, layer, x_src):
    nc = P.nc
    i = layer // 2
    T = Stage(nc, "r")
    S = T.S
    D_ = sp_scratch(P, layer)
    identf = T.sb([128, 128], F32, "identf")
    wrf = T.sb([128, 8, 8], F32, "wrf")
    gate_all = T.sb([128, NB, 8], F32, "gates")
    sel_all = T.sb([128, NB, 8], F32, "sel")
    xb_all = T.sb([128, NB, D], BF16, "xball")
    R_id, R_wr, R_ga, R_sel = RL(4)
    R_xb = RL(NB)
    c_ident, router = P.inp("c_ident"), P.inp("moe_router")
    S.dma("sp", lambda e: e.dma_start(out=identf[:], in_=c_ident[:, :]), writes=[R_id])
    S.dma("sp", lambda e: e.dma_start(out=wrf[:], in_=router[i].rearrange("(k p) n -> p k n", p=128)), writes=[R_wr])
    xfs = Rot([T.sb([128, D], F32, "xf") for _ in range(2)])
    xTs = Rot([T.sb([128, 8, 128], F32, "xTf") for _ in range(2)])
    pT_ = Rot([T.ps([128, 8, 128], F32, "pTf") for _ in range(2)])
    lg_ = Rot([T.ps([128, 512], F32, "lg") for _ in range(2)])
    sm = [Rot([T.sb([128, 8], F32, "sm%d" % j) for _ in range(2)]) for j in range(4)]
    sc1 = [Rot([T.sb([128, 1], F32, "sc%d" % j) for _ in range(2)]) for j in range(4)]
    for blk in range(NB):
        xf, rxf = xfs.next()
        S.dma("sp", lambda e, xf=xf, blk=blk: e.dma_start(out=xf[:], in_=x_src[blk * 128:(blk + 1) * 128, :]), writes=[rxf])
        S.op("act", lambda e, xf=xf, blk=blk: e.copy(out=xb_all[:, blk, :], in_=xf[:]), reads=[rxf], writes=[R_xb[blk]])
        pT, rpT = pT_.next()
        for k in range(8):
            S.op("pe", lambda e, pT=pT, xf=xf, k=k: e.transpose(out=pT[:, k, :], in_=xf[:, k * 128:(k + 1) * 128], identity=identf[:]), reads=[rxf, R_id], writes=[rpT])
        xT, rxT = xTs.next()
        S.op("act", lambda e, xT=xT, pT=pT: e.copy(out=xT[:, 0:4, :], in_=pT[:, 0:4, :]), reads=[rpT], writes=[rxT])
        S.op("dve", lambda e, xT=xT, pT=pT: e.tensor_copy(out=xT[:, 4:8, :], in_=pT[:, 4:8, :]), reads=[rpT], writes=[rxT])
        lg, rlg = lg_.next()
        for k in range(8):
            S.op("pe", lambda e, lg=lg, xT=xT, k=k: e.matmul(lg[:, 0:8], lhsT=xT[:, k, :], rhs=wrf[:, k, :], start=(k == 0), stop=(k == 7)), reads=[rxT, R_wr], writes=[rlg])
        (lgs, rlgs), (eq, req), (l2, rl2), (ex, rex) = [r_.next() for r_ in sm]
        (m1, rm1), (m2, rm2), (nm1, rnm1), (den, rden) = [r_.next() for r_ in sc1]
        S.op("dve", lambda e, lgs=lgs, lg=lg: e.tensor_copy(out=lgs[:], in_=lg[:, 0:8]), reads=[rlg], writes=[rlgs])
        S.op("dve", lambda e, m1=m1, lgs=lgs: e.tensor_reduce(out=m1[:], in_=lgs[:], axis=AX.X, op=ALU.max), reads=[rlgs], writes=[rm1])
        S.op("dve", lambda e, eq=eq, lgs=lgs, m1=m1: e.tensor_scalar(out=eq[:], in0=lgs[:], scalar1=m1[:, 0:1], scalar2=None, op0=ALU.is_equal), reads=[rlgs, rm1], writes=[req])
        S.op("dve", lambda e, l2=l2, eq=eq, lgs=lgs: e.scalar_tensor_tensor(out=l2[:], in0=eq[:], scalar=-1.0e30, in1=lgs[:], op0=ALU.mult, op1=ALU.add), reads=[req, rlgs], writes=[rl2])
        S.op("dve", lambda e, m2=m2, l2=l2: e.tensor_reduce(out=m2[:], in_=l2[:], axis=AX.X, op=ALU.max), reads=[rl2], writes=[rm2])
        S.op("dve", lambda e, lgs=lgs, m2=m2, blk=blk: e.tensor_scalar(out=sel_all[:, blk, :], in0=lgs[:], scalar1=m2[:, 0:1], scalar2=None, op0=ALU.is_ge), reads=[rlgs, rm2], writes=[R_sel])
        S.op("dve", lambda e, nm1=nm1, m1=m1: e.tensor_scalar(out=nm1[:], in0=m1[:], scalar1=-1.0, scalar2=None, op0=ALU.mult), reads=[rm1], writes=[rnm1])
        S.op("act", lambda e, ex=ex, lgs=lgs, nm1=nm1: e.activation(out=ex[:], in_=lgs[:], func=AF.Exp, bias=nm1[:, 0:1], scale=1.0), reads=[rlgs, rnm1], writes=[rex])
        S.op("dve", lambda e, ex=ex, blk=blk: e.tensor_tensor(out=ex[:], in0=ex[:], in1=sel_all[:, blk, :], op=ALU.mult), reads=[rex, R_sel], writes=[rex])
        S.op("dve", lambda e, den=den, ex=ex: e.tensor_reduce(out=den[:], in_=ex[:], axis=AX.X, op=ALU.add), reads=[rex], writes=[rden])
        S.op("dve", lambda e, den=den: e.reciprocal(out=den[:], in_=den[:]), reads=[rden], writes=[rden])
        S.op("dve", lambda e, ex=ex, den=den, blk=blk: e.tensor_scalar(out=gate_all[:, blk, :], in0=ex[:], scalar1=den[:, 0:1], scalar2=None, op0=ALU.mult), reads=[rex, rden], writes=[R_ga])
    NBE = NB * 8
    tri = T.sb([128, 128], BF16, "tri")
    onesb = T.sb([128, 128], BF16, "onesb")
    selb = T.sb([128, NBE], BF16, "selb")
    tot = T.sb([128, NBE], F32, "tot")
    inc = T.sb([128, NBE], F32, "inc")
    slot = T.sb([128, NBE], F32, "slot")
    tmp = T.sb([128, NBE], F32, "tmp")
    one1 = T.sb([128, 1], F32, "one1")
    ne = T.sb([128, 8], F32, "ne")
    padn = T.sb([128, 8], F32, "padn")
    send = T.sb([128, 8], F32, "send")
    sstart = T.sb([128, 8], F32, "sstart")
    sa = T.sb([128, NB], F32, "sa")
    sb_ = T.sb([128, NB], F32, "sb")
    ga = T.sb([128, NB], F32, "ga")
    gb = T.sb([128, NB], F32, "gb")
    sai = T.sb([128, NB], I32, "sai")
    sbi = T.sb([128, NB], I32, "sbi")
    tlim = T.sb([128, NTILE], F32, "tlim")
    cmp_ = T.sb([128, NTILE, 8], F32, "cmp")
    ei = T.sb([128, NTILE], F32, "ei")
    wix = T.sb([128, NTILE, SNG], F32, "wix")
    wixi = T.sb([128, NTILE, SNG], I32, "wixi")
    R_tri, R_ones, R_selb, R_tot, R_inc, R_slot, R_tmp, R_one1, R_ne, R_padn, R_send, R_ss, R_sa, R_sb, R_gab, R_sai, R_sbi, R_tlim, R_cmp, R_ei, R_wix, R_wixi = RL(22)
    rk_ps = Rot([T.ps([128, 512], F32, "rk") for _ in range(1)])
    tt_ps = Rot([T.ps([128, 512], F32, "tt") for _ in range(1)])
    c_tri, c_tlim, c_wbase = P.inp("c_tri"), P.inp("c_tlim"), P.inp("c_wbase")
    S.dma("pool", lambda e: e.dma_start(out=tri[:], in_=c_tri[:, :]), writes=[R_tri])
    S.dma("sp", lambda e: e.dma_start(out=tlim[:], in_=c_tlim[:, :]), writes=[R_tlim])
    S.dma("sp", lambda e: e.dma_start(out=wix[:], in_=c_wbase[:, :, :]), writes=[R_wix])
    S.op("pool", lambda e: e.memset(onesb[:], 1.0), writes=[R_ones])
    S.op("pool", lambda e: e.memset(one1[:], 1.0), writes=[R_one1])
    self_f = sel_all[:].rearrange("p b e -> p (b e)")
    gate_f = gate_all[:].rearrange("p b e -> p (b e)")
    S.op("dve", lambda e: e.tensor_copy(out=selb[:], in_=self_f), reads=[R_sel], writes=[R_selb])
    rk, rrk = rk_ps.next()
    tt, rtt = tt_ps.next()
    S.op("pe", lambda e: e.matmul(rk[:, 0:NBE], lhsT=tri[:], rhs=selb[:], start=True, stop=True), reads=[R_tri, R_selb], writes=[rrk])
    S.op("pe", lambda e: e.matmul(tt[:, 0:NBE], lhsT=onesb[:], rhs=selb[:], start=True, stop=True), reads=[R_ones, R_selb], writes=[rtt])
    S.op("dve", lambda e: e.tensor_copy(out=tot[:], in_=tt[:, 0:NBE]), reads=[rtt], writes=[R_tot])
    tot_v = tot[:].rearrange("p (b e) -> p e b", e=8)
    inc_v = inc[:].rearrange("p (b e) -> p e b", e=8)
    for ee in range(8):
        S.op("dve", lambda e, ee=ee: e.tensor_tensor_scan(out=inc_v[:, ee, :], data0=one1[:, 0:1].to_broadcast([128, NB]), data1=tot_v[:, ee, :], initial=0.0,
                                                          op0=ALU.mult, op1=ALU.add), reads=[R_one1, R_tot], writes=[R_inc])
    S.op("dve", lambda e: e.tensor_copy(out=ne[:], in_=inc[:, (NB - 1) * 8:NB * 8]), reads=[R_inc], writes=[R_ne])
    MAGIC = 12582912.0
    S.op("dve", lambda e: e.tensor_scalar(out=padn[:], in0=ne[:], scalar1=1.0 / TS, scalar2=(TS - 1.0) / TS - 0.5 + 1.0 / (2 * TS), op0=ALU.mult, op1=ALU.add), reads=[R_ne], writes=[R_padn])
    S.op("dve", lambda e: e.tensor_scalar(out=padn[:], in0=padn[:], scalar1=MAGIC, scalar2=None, op0=ALU.add), reads=[R_padn], writes=[R_padn])
    S.op("dve", lambda e: e.tensor_scalar(out=padn[:], in0=padn[:], scalar1=-MAGIC, scalar2=float(TS), op0=ALU.add, op1=ALU.mult), reads=[R_padn], writes=[R_padn])
    S.op("dve", lambda e: e.tensor_tensor_scan(out=send[:], data0=one1[:, 0:1].to_broadcast([128, 8]), data1=padn[:], initial=0.0, op0=ALU.mult, op1=ALU.add),
         reads=[R_one1, R_padn], writes=[R_send])
    S.op("dve", lambda e: e.tensor_tensor(out=sstart[:], in0=send[:], in1=padn[:], op=ALU.subtract), reads=[R_send, R_padn], writes=[R_ss])
    S.op("dve", lambda e: e.tensor_tensor(out=slot[:], in0=rk[:, 0:NBE], in1=inc[:], op=ALU.add), reads=[rrk, R_inc], writes=[R_slot])
    S.op("dve", lambda e: e.tensor_tensor(out=slot[:], in0=slot[:], in1=tot[:], op=ALU.subtract), reads=[R_slot, R_tot], writes=[R_slot])
    slot3 = slot[:].rearrange("p (b e) -> p b e", e=8)
    tmp3 = tmp[:].rearrange("p (b e) -> p b e", e=8)
    S.op("dve", lambda e: e.tensor_tensor(out=slot3, in0=slot3, in1=sstart[:, :].unsqueeze(1).to_broadcast([128, NB, 8]), op=ALU.add), reads=[R_slot, R_ss], writes=[R_slot])
    S.op("dve", lambda e: e.tensor_tensor(out=tmp[:], in0=slot[:], in1=self_f, op=ALU.mult), reads=[R_slot, R_sel], writes=[R_tmp])
    S.op("dve", lambda e: e.tensor_reduce(out=sb_[:], in_=tmp3, axis=AX.X, op=ALU.max), reads=[R_tmp], writes=[R_sb])
    S.op("dve", lambda e: e.scalar_tensor_tensor(out=tmp[:], in0=self_f, scalar=-1.0e6, in1=slot[:], op0=ALU.mult, op1=ALU.add), reads=[R_sel, R_slot], writes=[R_tmp])
    S.op("dve", lambda e: e.tensor_scalar(out=tmp[:], in0=tmp[:], scalar1=1.0e6, scalar2=None, op0=ALU.add), reads=[R_tmp], writes=[R_tmp])
    S.op("dve", lambda e: e.tensor_reduce(out=sa[:], in_=tmp3, axis=AX.X, op=ALU.min), reads=[R_tmp], writes=[R_sa])
    S.op("dve", lambda e: e.tensor_tensor(out=tmp3, in0=slot3, in1=sa[:, :].unsqueeze(2).to_broadcast([128, NB, 8]), op=ALU.is_equal), reads=[R_slot, R_sa], writes=[R_tmp])
    S.op("dve", lambda e: e.tensor_tensor(out=tmp[:], in0=tmp[:], in1=gate_f, op=ALU.mult), reads=[R_tmp, R_ga], writes=[R_tmp])
    S.op("dve", lambda e: e.tensor_reduce(out=ga[:], in_=tmp3, axis=AX.X, op=ALU.add), reads=[R_tmp], writes=[R_gab])
    S.op("dve", lambda e: e.tensor_reduce(out=gb[:], in_=gate_all[:], axis=AX.X, op=ALU.add), reads=[R_ga], writes=[R_gab])
    S.op("dve", lambda e: e.tensor_tensor(out=gb[:], in0=gb[:], in1=ga[:], op=ALU.subtract), reads=[R_gab], writes=[R_gab])
    S.op("dve", lambda e: e.tensor_copy(out=sai[:], in_=sa[:]), reads=[R_sa], writes=[R_sai])
    S.op("dve", lambda e: e.tensor_copy(out=sbi[:], in_=sb_[:]), reads=[R_sb], writes=[R_sbi])
    S.op("dve", lambda e: e.tensor_tensor(out=cmp_[:], in0=send[:, :].unsqueeze(1).to_broadcast([128, NTILE, 8]),
                                          in1=tlim[:, :].unsqueeze(2).to_broadcast([128, NTILE, 8]), op=ALU.is_le), reads=[R_send, R_tlim], writes=[R_cmp])
    S.op("dve", lambda e: e.tensor_reduce(out=ei[:], in_=cmp_[:], axis=AX.X, op=ALU.add), reads=[R_cmp], writes=[R_ei])
    S.op("dve", lambda e: e.tensor_scalar(out=ei[:], in0=ei[:], scalar1=7.0, scalar2=float(SNG * 128), op0=ALU.min, op1=ALU.mult), reads=[R_ei], writes=[R_ei])
    S.op("dve", lambda e: e.tensor_tensor(out=wix[:], in0=wix[:], in1=ei[:, :].unsqueeze(2).to_broadcast([128, NTILE, SNG]), op=ALU.add), reads=[R_wix, R_ei], writes=[R_wix])
    S.op("dve", lambda e: e.tensor_copy(out=wixi[:], in_=wix[:]), reads=[R_wix], writes=[R_wixi])
    S.dma("sp", lambda e: e.dma_start(out=D_["slotA"][:, :], in_=sai[:]), reads=[R_sai])
    S.dma("sp", lambda e: e.dma_start(out=D_["slotB"][:, :], in_=sbi[:]), reads=[R_sbi])
    S.dma("sp", lambda e: e.dma_start(out=D_["gA"][:, :], in_=ga[:]), reads=[R_gab])
    S.dma("sp", lambda e: e.dma_start(out=D_["gB"][:, :], in_=gb[:]), reads=[R_gab])
    S.dma("sp", lambda e: e.dma_start(out=D_["widx"][:, :], in_=wixi[:].rearrange("p a b -> p (a b)")), reads=[R_wixi])
    xsort = D_["xsort"]
    zt = T.sb([128, 8, D], BF16, "zeros")
    R_zt, R_xz = RL(2)
    S.op("pool", lambda e: e.memset(zt[:], 0.0), writes=[R_zt])
    xz = xsort.rearrange("(n p) d -> p n d", p=128)
    for n0 in range(0, NSLOT // 128, 8):
        S.dma("sp", lambda e, n0=n0: e.dma_start(out=xz[:, n0:n0 + 8, :], in_=zt[:]), reads=[R_zt], writes=[R_xz])
    for blk in range(NB):
        for which, idx_t, r_idx in (("a", sai, R_sai), ("b", sbi, R_sbi)):
            S.dma("pool", lambda e, blk=blk, idx_t=idx_t: e.indirect_dma_start(out=xsort[:, :], out_offset=IOA(ap=idx_t[:, blk:blk + 1], axis=0),
                                                                             in_=xb_all[:, blk, :], in_offset=None), reads=[R_xb[blk], r_idx, R_xz])
    T.finish()


def stageES(P, layer, pump=None, pump_every=2):
    nc = P.nc
    T = Stage(nc, "es")
    S = T.S
    D_ = sp_scratch(P, layer)
    ident, R_id = load_ident(T, P, S)
    widx = T.sb([128, NTILE * SNG], I32, "widx")
    R_widx = Res()
    S.dma("sp", lambda e: e.dma_start(out=widx[:], in_=D_["widx"][:, :]), writes=[R_widx])
    xTs = Rot([T.sb([128, 8, TS], BF16, "xT") for _ in range(2)])
    accs = Rot([T.sb([128, 4, D], F32, "acc") for _ in range(2)])
    wgs = Rot([T.sb([128, 8, SGC * 128], BF16, "wg") for _ in range(2)])
    wus = Rot([T.sb([128, 8, SGC * 128], BF16, "wu") for _ in range(2)])
    wds = Rot([T.sb([128, SGC, D], BF16, "wd") for _ in range(2)])
    uTs = Rot([T.sb([128, SGC, 512], BF16, "uT") for _ in range(3)])
    sgs = Rot([T.sb([128, 512], F32, "sg") for _ in range(2)])
    xbs = Rot([T.sb([128, D], BF16, "xb") for _ in range(3)])
    pg_ = Rot([T.ps([128, 512], F32, "pg") for _ in range(2)])
    pu_ = Rot([T.ps([128, 512], F32, "pu") for _ in range(2)])
    pd_ = Rot([T.ps([128, 512], F32, "pd") for _ in range(2)])
    pT_ = Rot([T.ps([128, 8, 128], BF16, "pT") for _ in range(2)])
    xsort, ysort = D_["xsort"], D_["ysort"]
    if pump is not None:
        pump.attach(T, 4)
    items = [(ti, g) for ti in range(NTILE) for g in range(SNG)]
    cur = {}
    pend = {}

    def emit_gu(it):
        ti, g = it
        if g == 0:
            xT, rxT = xTs.next()
            acc, racc = accs.next()
            for b4 in range(4):
                xb, rxb = xbs.next()
                p, rp = pT_.next()
                r0 = ti * TS + b4 * 128
                S.dma("sp", lambda e, xb=xb, r0=r0: e.dma_start(out=xb[:], in_=xsort[r0:r0 + 128, :]), writes=[rxb])
                for k in range(8):
                    S.op("pe", lambda e, p=p, xb=xb, k=k: e.transpose(out=p[:, k, :], in_=xb[:, k * 128:(k + 1) * 128], identity=ident[:]), reads=[rxb, R_id], writes=[rp])
                S.op("dve", lambda e, p=p, b4=b4, xT=xT: e.tensor_copy(out=xT[:, :, b4 * 128:(b4 + 1) * 128], in_=p[:]), reads=[rp], writes=[rxT])
            cur[ti] = (xT, rxT, acc, racc)
        xT, rxT, acc, racc = cur[ti]
        col = ti * SNG + g
        wgb, rwg = wgs.next()
        wub, rwu = wus.next()
        wdb, rwd = wds.next()
        S.dma("pool", lambda e: e.indirect_dma_start(out=wgb[:].rearrange("p k n -> p (k n)"), out_offset=None, in_=D_["wgs"][:, :], in_offset=IOA(ap=widx[:, col:col + 1], axis=0)),
              reads=[R_widx], writes=[rwg])
        S.dma("pool", lambda e: e.indirect_dma_start(out=wub[:].rearrange("p k n -> p (k n)"), out_offset=None, in_=D_["wus"][:, :], in_offset=IOA(ap=widx[:, col:col + 1], axis=0)),
              reads=[R_widx], writes=[rwu])
        S.dma("pool", lambda e: e.indirect_dma_start(out=wdb[:].rearrange("p c n -> p (c n)"), out_offset=None, in_=D_["wds"][:, :], in_offset=IOA(ap=widx[:, col:col + 1], axis=0)),
              reads=[R_widx], writes=[rwd])
        uT, ruT = uTs.next()
        for c in range(SGC):
            pg, rpg = pg_.next()
            pu, rpu = pu_.next()
            for k in range(8):
                S.op("pe", lambda e, pg=pg, k=k, c=c: e.matmul(pg[:, :], lhsT=wgb[:, k, c * 128:(c + 1) * 128], rhs=xT[:, k, :], start=(k == 0), stop=(k == 7)),
                     reads=[rwg, rxT], writes=[rpg])
            for k in range(8):
                S.op("pe", lambda e, pu=pu, k=k, c=c: e.matmul(pu[:, :], lhsT=wub[:, k, c * 128:(c + 1) * 128], rhs=xT[:, k, :], start=(k == 0), stop=(k == 7)),
                     reads=[rwu, rxT], writes=[rpu])
            sg, rsg = sgs.next()
            S.op("act", lambda e, sg=sg, pg=pg: e.activation(out=sg[:], in_=pg[:, :], func=AF.Silu), reads=[rpg], writes=[rsg])
            S.op("dve", lambda e, c=c, sg=sg, pu=pu: e.tensor_tensor(out=uT[:, c, :], in0=pu[:, :], in1=sg[:], op=ALU.mult), reads=[rpu, rsg], writes=[ruT])
        pend[it] = (uT, ruT, wdb, rwd, acc, racc)

    def emit_down(it):
        ti, g = it
        uT, ruT, wdb, rwd, acc, racc = pend.pop(it)
        for tb in range(4):
            for hf in range(2):
                pd, rpd = pd_.next()
                for c in range(SGC):
                    S.op("pe", lambda e, pd=pd, c=c, tb=tb, hf=hf: e.matmul(pd[:, :], lhsT=uT[:, c, tb * 128:(tb + 1) * 128], rhs=wdb[:, c, hf * 512:(hf + 1) * 512],
                                                                          start=(c == 0), stop=(c == SGC - 1)), reads=[ruT, rwd], writes=[rpd])
                a_ = acc[:, tb, hf * 512:(hf + 1) * 512]
                if g == 0:
                    S.op("dve", lambda e, a_=a_, pd=pd: e.tensor_copy(out=a_, in_=pd[:, :]), reads=[rpd], writes=[racc])
                else:
                    S.op("dve", lambda e, a_=a_, pd=pd: e.tensor_tensor(out=a_, in0=pd[:, :], in1=a_, op=ALU.add), reads=[rpd, racc], writes=[racc])
        if g == SNG - 1:
            for tb in range(4):
                r0 = ti * TS + tb * 128
                S.dma("sp", lambda e, tb=tb, r0=r0: e.dma_start(out=ysort[r0:r0 + 128, :], in_=acc[:, tb, :]), reads=[racc])

    emit_gu(items[0])
    for ii, it in enumerate(items):
        if ii + 1 < len(items):
            emit_gu(items[ii + 1])
        if pump is not None and ii % pump_every == 0:
            pump.pump(S, 1)
        emit_down(it)
    T.finish()


def stageF(P, layer, x_src, x_dst):
    nc = P.nc
    T = Stage(nc, "f")
    S = T.S
    D_ = sp_scratch(P, layer)
    L = LNBufs(T, S, P, "ln_ffn_g", "ln_ffn_b", layer)
    sai = T.sb([128, NB], I32, "sai")
    sbi = T.sb([128, NB], I32, "sbi")
    ga = T.sb([128, NB], F32, "ga")
    gb = T.sb([128, NB], F32, "gb")
    R_sai, R_sbi, R_ga, R_gb = RL(4)
    S.dma("sp", lambda e: e.dma_start(out=sai[:], in_=D_["slotA"][:, :]), writes=[R_sai])
    S.dma("sp", lambda e: e.dma_start(out=sbi[:], in_=D_["slotB"][:, :]), writes=[R_sbi])
    S.dma("sp", lambda e: e.dma_start(out=ga[:], in_=D_["gA"][:, :]), writes=[R_ga])
    S.dma("sp", lambda e: e.dma_start(out=gb[:], in_=D_["gB"][:, :]), writes=[R_gb])
    yas = Rot([T.sb([128, D], F32, "ya") for _ in range(3)])
    ybs = Rot([T.sb([128, D], F32, "yb") for _ in range(3)])
    xfs = Rot([T.sb([128, D], F32, "xf") for _ in range(3)])
    ysort = D_["ysort"]
    for blk in range(NB):
        ya, rya = yas.next()
        yb, ryb = ybs.next()
        xf, rxf = xfs.next()
        S.dma("pool", lambda e, ya=ya, blk=blk: e.indirect_dma_start(out=ya[:], out_offset=None, in_=ysort[:, :], in_offset=IOA(ap=sai[:, blk:blk + 1], axis=0)),
              reads=[R_sai], writes=[rya])
        S.dma("pool", lambda e, yb=yb, blk=blk: e.indirect_dma_start(out=yb[:], out_offset=None, in_=ysort[:, :], in_offset=IOA(ap=sbi[:, blk:blk + 1], axis=0)),
              reads=[R_sbi], writes=[ryb])
        S.dma("sp", lambda e, xf=xf, blk=blk: e.dma_start(out=xf[:], in_=x_src[blk * 128:(blk + 1) * 128, :]), writes=[rxf])
        S.op("act", lambda e, ya=ya, blk=blk: e.activation(out=ya[:], in_=ya[:], func=AF.Copy, scale=ga[:, blk:blk + 1]), reads=[rya, R_ga], writes=[rya])
        S.op("dve", lambda e, ya=ya, yb=yb, blk=blk: e.scalar_tensor_tensor(out=ya[:], in0=yb[:], scalar=gb[:, blk:blk + 1], in1=ya[:], op0=ALU.mult, op1=ALU.add),
             reads=[rya, ryb, R_gb], writes=[rya])
        S.op("dve", lambda e, xf=xf, ya=ya: e.scalar_tensor_tensor(out=xf[:], in0=xf[:], scalar=DN_ALPHA, in1=ya[:], op0=ALU.mult, op1=ALU.add), reads=[rxf, rya], writes=[rxf])
        emit_ln(S, L, xf, rxf, x_dst[blk * 128:(blk + 1) * 128, :], gb_eng="dve")
    T.finish()


def build(n_layers=DEPTH, debug_out=(), stages="0ABCDWRE", first_layer=0, x_in=None):
    P = Prog(debug_out)
    y = P.out("y", [S_LEN, D], F32)
    if "0" in stages:
        stage0(P)
    pumps = {}
    if SPARSE and "W" in stages:
        for l in range(first_layer, first_layer + n_layers):
            if l % 2 == 1:
                pumps[l] = BgPump(P, l)
    for layer in range(first_layer, first_layer + n_layers):
        x_src = P.inp("x") if layer == 0 else P.xs(1)
        x_mid = P.xs(0)
        x_dst = y if layer == DEPTH - 1 else P.xs(1)
        if "A" in stages:
            stageA(P, layer, x_src)
        if "B" in stages:
            bp = pumps.get(layer) if layer % 2 == 1 else pumps.get(layer + 1)
            stageB(P, layer, pump=bp, pump_every=13)
        if "C" in stages:
            stageC(P, layer)
        if "D" in stages:
            stageD(P, layer, x_src, x_mid)
        moe = (layer % 2 == 1)
        if x_in is not None:
            x_mid = P.inp(x_in)
        if moe and SPARSE:
            if "W" in stages:
                stageW(P, pumps[layer])
            if "R" in stages:
                stageR(P, layer, x_mid)
            if "E" in stages:
                stageES(P, layer)
                stageF(P, layer, x_mid, x_dst)
        else:
            if moe and "R" in stages:
                stageE0(P, layer, x_mid)
            if "E" in stages:
                stageE(P, layer, x_mid, x_dst, moe)
    return P


def make_in_maps(P, inputs):
    consts = host_consts()
    maps = []
    for c in range(8):
        m = {}
        for name in P.used_inputs:
            if name == "x":
                m[name] = np.ascontiguousarray(inputs["x"][c])
            elif name == "positions":
                m[name] = np.ascontiguousarray(inputs["positions"][c].reshape(1, S_LEN)).astype(np.int32)
            elif name in consts:
                m[name] = consts[name]
            else:
                m[name] = np.ascontiguousarray(inputs[name])
        maps.append(m)
    return maps


def kernel(**inputs):
    P = build()
    maps = make_in_maps(P, inputs)
    res = run_bass_kernel_spmd(P.nc, maps, core_ids=list(range(8)))
    return np.stack([np.asarray(res.results[c]["y"]) for c in range(8)], axis=0).astype(np.float32)
```
